# Optimizing a Trainium2 kernel written in Bass

```python
import jax
import jax.numpy as jnp
from jax import lax
import numpy as np

D_MODEL = 1024
BATCH = 8
SEQ = 8192
DEPTH = 1

ATT_HEADS = 8
ATT_KV_HEADS = 2
ATT_HEAD_DIM = 64
WINDOW = 128
ATT_BLOCK = 128
RET_HEADS = 4
RET_QK_DIM = 64
RET_V_DIM = 128
RET_CHUNK = 128
MEM_LEN = 256
MEM_HEADS = 4
MEM_HEAD_DIM = 128
PEER_HEADS = 8
PEER_N_KEYS = 128
PEER_N_EXPERTS = PEER_N_KEYS * PEER_N_KEYS
PEER_TOPK = 16
PEER_QUERY_DIM = 256
PEER_HALF = PEER_QUERY_DIM // 2
PEER_TOKEN_CHUNK = 128
N_BRANCHES = 3
ALPHA = (2.0 * DEPTH) ** 0.25
BETA = (8.0 * DEPTH) ** -0.25
LN_EPS = 1e-5

ATT_Q_WIDTH = ATT_HEADS * ATT_HEAD_DIM
ATT_KV_WIDTH = ATT_KV_HEADS * ATT_HEAD_DIM
RET_QK_WIDTH = RET_HEADS * RET_QK_DIM
RET_V_WIDTH = RET_HEADS * RET_V_DIM
MEM_WIDTH = MEM_HEADS * MEM_HEAD_DIM
GATE_WIDTH = N_BRANCHES * D_MODEL
IN_SIZES = (ATT_Q_WIDTH, ATT_KV_WIDTH, ATT_KV_WIDTH, RET_QK_WIDTH, RET_QK_WIDTH, RET_V_WIDTH, RET_V_WIDTH, MEM_WIDTH, GATE_WIDTH)
IN_WIDTH = sum(IN_SIZES)

kernel_name = "hybrid_swa_retention_mem_peer_deepnorm"

F32 = jnp.float32


def layer_norm(x, g, b):
    xf = x.astype(F32)
    mu = jnp.mean(xf, axis=-1, keepdims=True)
    var = jnp.mean(jnp.square(xf - mu), axis=-1, keepdims=True)
    y = (xf - mu) * lax.rsqrt(var + LN_EPS) * g.astype(F32) + b.astype(F32)
    return y.astype(x.dtype)


def sliding_window_gqa(q, k, v, sinks):
    b, s = q.shape[0], q.shape[1]
    nb = s // ATT_BLOCK
    rep = ATT_HEADS // ATT_KV_HEADS
    qb = q.reshape(b, nb, ATT_BLOCK, ATT_KV_HEADS, rep, ATT_HEAD_DIM)

    def band(t):
        tb = t.reshape(b, nb, ATT_BLOCK, ATT_KV_HEADS, ATT_HEAD_DIM)
        prev = jnp.pad(tb[:, :-1], ((0, 0), (1, 0), (0, 0), (0, 0), (0, 0)))
        return jnp.concatenate([prev, tb], axis=2)

    kb, vb = band(k), band(v)
    logits = jnp.einsum('bnqgrd,bnkgd->bngrqk', qb, kb).astype(F32) * (ATT_HEAD_DIM ** -0.5)
    qi = jnp.arange(ATT_BLOCK)[:, None]
    kj = jnp.arange(2 * ATT_BLOCK)[None, :]
    dist = qi + ATT_BLOCK - kj
    in_window = (dist >= 0) & (dist < WINDOW)
    has_prev = jnp.arange(nb)[:, None, None] > 0
    mask = in_window[None] & (has_prev | (kj >= ATT_BLOCK)[None])
    logits = jnp.where(mask[None, :, None, None], logits, -jnp.inf)
    sink = sinks.astype(F32).reshape(ATT_KV_HEADS, rep)[None, None, :, :, None, None]
    m = jnp.maximum(jnp.max(logits, axis=-1, keepdims=True), sink)
    p = jnp.exp(logits - m)
    p = p / (jnp.sum(p, axis=-1, keepdims=True) + jnp.exp(sink - m))
    out = jnp.einsum('bngrqk,bnkgd->bnqgrd', p.astype(v.dtype), vb)
    return out.reshape(b, s, ATT_Q_WIDTH)


def retention_rotate(t, pos):
    half = RET_QK_DIM // 2
    theta = 1.0 / (10000.0 ** jnp.linspace(0.0, 1.0, half, dtype=F32))
    ang = pos[:, None] * theta[None, :]
    cos = jnp.cos(ang)[None, :, None, :]
    sin = jnp.sin(ang)[None, :, None, :]
    t1, t2 = t[..., :half], t[..., half:]
    return jnp.concatenate([t1 * cos - t2 * sin, t1 * sin + t2 * cos], axis=-1)


def chunkwise_retention(q, k, v):
    b, s = q.shape[0], q.shape[1]
    nc = s // RET_CHUNK
    log_gamma = jnp.log(1.0 - 2.0 ** (-5.0 - jnp.arange(RET_HEADS, dtype=F32)))
    k = k * (RET_QK_DIM ** -0.5)
    qc = q.reshape(b, nc, RET_CHUNK, RET_HEADS, RET_QK_DIM)
    kc = k.reshape(b, nc, RET_CHUNK, RET_HEADS, RET_QK_DIM)
    vc = v.reshape(b, nc, RET_CHUNK, RET_HEADS, RET_V_DIM)
    i = jnp.arange(RET_CHUNK, dtype=F32)
    diff = i[:, None] - i[None, :]
    decay_in = jnp.where(diff >= 0, jnp.exp(log_gamma[:, None, None] * jnp.maximum(diff, 0.0)), 0.0)
    inner = jnp.einsum('bnihd,bnjhd->bnhij', qc, kc) * decay_in
    o_in = jnp.einsum('bnhij,bnjhe->bnihe', inner, vc)
    w_k = jnp.exp(log_gamma[:, None] * (RET_CHUNK - 1.0 - i)[None, :])
    kv = jnp.einsum('bnjhd,bnjhe,hj->nbhde', kc, vc, w_k)
    chunk_decay = jnp.exp(log_gamma * RET_CHUNK)[None, :, None, None]

    def step(state, kv_n):
        return chunk_decay * state + kv_n, state

    init = jnp.zeros((b, RET_HEADS, RET_QK_DIM, RET_V_DIM), F32)
    _, states = lax.scan(step, init, kv)
    w_q = jnp.exp(log_gamma[:, None] * (i + 1.0)[None, :])
    o_x = jnp.einsum('bnihd,nbhde,hi->bnihe', qc, states, w_q)
    return (o_in + o_x).reshape(b, s, RET_HEADS, RET_V_DIM)


def head_group_norm(t, g):
    mu = jnp.mean(t, axis=-1, keepdims=True)
    var = jnp.mean(jnp.square(t - mu), axis=-1, keepdims=True)
    return (t - mu) * lax.rsqrt(var + LN_EPS) * g.astype(F32).reshape(RET_HEADS, RET_V_DIM)


def memory_cross_attention(q, mem, mem_ln_g, mem_ln_b, w_mem_kv):
    b, s = q.shape[0], q.shape[1]
    m_len = mem.shape[1]
    mkv = layer_norm(mem, mem_ln_g, mem_ln_b) @ w_mem_kv
    mk, mv = jnp.split(mkv, 2, axis=-1)
    mk = mk.reshape(b, m_len, MEM_HEADS, MEM_HEAD_DIM)
    mv = mv.reshape(b, m_len, MEM_HEADS, MEM_HEAD_DIM)
    qh = q.reshape(b, s, MEM_HEADS, MEM_HEAD_DIM)
    logits = jnp.einsum('bshd,bmhd->bhsm', qh, mk).astype(F32) * (MEM_HEAD_DIM ** -0.5)
    p = jax.nn.softmax(logits, axis=-1).astype(mv.dtype)
    return jnp.einsum('bhsm,bmhd->bshd', p, mv).reshape(b, s, MEM_WIDTH)


def peer(h, w_peer_q, peer_sub_keys, peer_u, peer_v):
    b, s, d = h.shape
    n_chunks = (b * s) // PEER_TOKEN_CHUNK
    hc = h.reshape(n_chunks, PEER_TOKEN_CHUNK, d)

    def chunk(ht):
        t = ht.shape[0]
        q = (ht @ w_peer_q).reshape(t, PEER_HEADS, 2, PEER_HALF)
        sc = jnp.einsum('thpd,hpnd->thpn', q, peer_sub_keys).astype(F32)
        sv, si = lax.top_k(sc, PEER_TOPK)
        cand = (sv[:, :, 0, :, None] + sv[:, :, 1, None, :]).reshape(t, PEER_HEADS, PEER_TOPK * PEER_TOPK)
        cidx = (si[:, :, 0, :, None] * PEER_N_KEYS + si[:, :, 1, None, :]).reshape(t, PEER_HEADS, PEER_TOPK * PEER_TOPK)
        top_s, pos = lax.top_k(cand, PEER_TOPK)
        eidx = jnp.take_along_axis(cidx, pos, axis=-1)
        gate = jax.nn.softmax(top_s, axis=-1).astype(ht.dtype)
        u = peer_u[eidx]
        act = jax.nn.gelu(jnp.einsum('thkd,td->thk', u, ht), approximate=False)
        return jnp.einsum('thk,thkd->td', gate * act, peer_v[eidx])

    out = lax.map(chunk, hc)
    return out.reshape(b, s, d)


def hybrid_layer(x, mem, w_in, b_in, attn_sinks, ret_gn_g, mem_ln_g, mem_ln_b, w_mem_kv,
                 w_branch_attn, w_branch_ret, w_branch_mem, w_out, ln1_g, ln1_b,
                 w_peer_q, peer_sub_keys, peer_u, peer_v, ln2_g, ln2_b):
    b, s, d = x.shape
    pos = jnp.arange(s, dtype=F32)
    proj = x @ w_in + b_in
    offsets = np.cumsum(IN_SIZES)[:-1].tolist()
    aq, ak, av, rq, rk, rv, rg, mq, gates = jnp.split(proj, offsets, axis=-1)

    br_a = sliding_window_gqa(aq.reshape(b, s, ATT_HEADS, ATT_HEAD_DIM),
                              ak.reshape(b, s, ATT_KV_HEADS, ATT_HEAD_DIM),
                              av.reshape(b, s, ATT_KV_HEADS, ATT_HEAD_DIM), attn_sinks)

    rq = retention_rotate(rq.reshape(b, s, RET_HEADS, RET_QK_DIM).astype(F32), pos)
    rk = retention_rotate(rk.reshape(b, s, RET_HEADS, RET_QK_DIM).astype(F32), pos)
    ret = chunkwise_retention(rq, rk, rv.reshape(b, s, RET_HEADS, RET_V_DIM).astype(F32))
    ret = head_group_norm(ret, ret_gn_g).reshape(b, s, RET_V_WIDTH).astype(x.dtype)
    br_b = ret * jax.nn.silu(rg)

    br_c = memory_cross_attention(mq, mem, mem_ln_g, mem_ln_b, w_mem_kv)

    g = jax.nn.sigmoid(gates).reshape(b, s, N_BRANCHES, d)
    merged = (g[:, :, 0] * (br_a @ w_branch_attn) + g[:, :, 1] * (br_b @ w_branch_ret)
              + g[:, :, 2] * (br_c @ w_branch_mem))
    h = layer_norm(ALPHA * x + merged @ w_out, ln1_g, ln1_b)

    y = layer_norm(ALPHA * h + peer(h, w_peer_q, peer_sub_keys, peer_u, peer_v), ln2_g, ln2_b)
    return y


def setup_inputs(seed: int = 0) -> dict:
    key = jax.random.key(seed)
    ks = jax.random.split(key, 24)

    def nrm(k, shape, scale):
        return jax.random.normal(k, shape, F32) * scale

    L = DEPTH
    return {
        "x": nrm(ks[0], (BATCH, SEQ, D_MODEL), 1.0),
        "mem": nrm(ks[1], (BATCH, MEM_LEN, D_MODEL), 1.0),
        "w_in": nrm(ks[2], (L, D_MODEL, IN_WIDTH), D_MODEL ** -0.5),
        "b_in": nrm(ks[3], (L, IN_WIDTH), 0.02),
        "attn_sinks": nrm(ks[4], (L, ATT_HEADS), 0.5),
        "ret_gn_g": 1.0 + nrm(ks[5], (L, RET_V_WIDTH), 0.02),
        "mem_ln_g": 1.0 + nrm(ks[6], (L, D_MODEL), 0.02),
        "mem_ln_b": nrm(ks[7], (L, D_MODEL), 0.02),
        "w_mem_kv": nrm(ks[8], (L, D_MODEL, 2 * MEM_WIDTH), D_MODEL ** -0.5),
        "w_branch_attn": nrm(ks[9], (L, ATT_Q_WIDTH, D_MODEL), BETA * ATT_Q_WIDTH ** -0.5),
        "w_branch_ret": nrm(ks[10], (L, RET_V_WIDTH, D_MODEL), BETA * RET_V_WIDTH ** -0.5),
        "w_branch_mem": nrm(ks[11], (L, MEM_WIDTH, D_MODEL), BETA * MEM_WIDTH ** -0.5),
        "w_out": nrm(ks[12], (L, D_MODEL, D_MODEL), BETA * D_MODEL ** -0.5),
        "ln1_g": 1.0 + nrm(ks[13], (L, D_MODEL), 0.02),
        "ln1_b": nrm(ks[14], (L, D_MODEL), 0.02),
        "w_peer_q": nrm(ks[15], (L, D_MODEL, PEER_HEADS * PEER_QUERY_DIM), D_MODEL ** -0.5),
        "peer_sub_keys": nrm(ks[16], (L, PEER_HEADS, 2, PEER_N_KEYS, PEER_HALF), PEER_HALF ** -0.5),
        "peer_u": nrm(ks[17], (L, PEER_N_EXPERTS, D_MODEL), D_MODEL ** -0.5),
        "peer_v": nrm(ks[18], (L, PEER_N_EXPERTS, D_MODEL), BETA * PEER_HEADS ** -0.5),
        "ln2_g": 1.0 + nrm(ks[19], (L, D_MODEL), 0.02),
        "ln2_b": nrm(ks[20], (L, D_MODEL), 0.02),
    }


def reference(x, mem, w_in, b_in, attn_sinks, ret_gn_g, mem_ln_g, mem_ln_b, w_mem_kv,
              w_branch_attn, w_branch_ret, w_branch_mem, w_out, ln1_g, ln1_b,
              w_peer_q, peer_sub_keys, peer_u, peer_v, ln2_g, ln2_b):
    for l in range(DEPTH):
        x = hybrid_layer(x, mem, w_in[l], b_in[l], attn_sinks[l], ret_gn_g[l], mem_ln_g[l], mem_ln_b[l],
                         w_mem_kv[l], w_branch_attn[l], w_branch_ret[l], w_branch_mem[l], w_out[l],
                         ln1_g[l], ln1_b[l], w_peer_q[l], peer_sub_keys[l], peer_u[l], peer_v[l],
                         ln2_g[l], ln2_b[l])
    return x
```

```python
import numpy as np
import ml_dtypes
from contextlib import ExitStack

import concourse.bass as bass
import concourse.mybir as mybir
from concourse.bass_utils import run_bass_kernel_spmd

F32 = mybir.dt.float32
BF16 = mybir.dt.bfloat16
I32 = mybir.dt.int32
U32 = mybir.dt.uint32
AF = mybir.ActivationFunctionType
ALU = mybir.AluOpType
AX = mybir.AxisListType

D = 1024
SEQ = 8192
NCORES = 8
ALPHA = 2.0 ** 0.25
LN_EPS = 1e-5
N_EXP = 16384

FM_COLS = 18 * 128
TM_A1 = 128 + 512 + 512
NCOL_A1 = FM_COLS + TM_A1
NCOL_G = 3072

SAME_ENGINE_SYNC = True
EPS_T = [None]
import os as _os
A1_STAGE = [int(_os.environ.get('A1_STAGE', '0'))]


class Sync:
    def __init__(self, nc, es):
        self.nc = nc
        self.es = es
        self.sems = {}
        self.cnt = {}

    def sem(self, name):
        if name not in self.sems:
            self.sems[name] = self.es.enter_context(self.nc.semaphore("s_" + name))
            self.cnt[name] = 0
        return self.sems[name]


ENGS = ("pe", "act", "dve", "pool", "sp")


class Prog:
    def __init__(self, nc, sync):
        self.nc = nc
        self.sync = sync
        self.ops = []
        self.lastw = {}
        self.readers = {}

    def op(self, eng, fn, r=(), w=(), dma=None):
        i = len(self.ops)
        deps = set()
        for k in r:
            if k in self.lastw:
                deps.add(self.lastw[k])
        for k in w:
            if k in self.lastw:
                deps.add(self.lastw[k])
            deps.update(self.readers.get(k, ()))
        for k in r:
            self.readers.setdefault(k, []).append(i)
        for k in w:
            self.lastw[k] = i
            self.readers[k] = []
        deps.discard(i)
        self.ops.append(dict(eng=eng, fn=fn, deps=deps, dma=dma))
        return i

    def emit(self):
        nc, sync, ops = self.nc, self.sync, self.ops
        n = len(ops)
        per_eng = {e: [] for e in ENGS}
        for i, o in enumerate(ops):
            per_eng[o["eng"]].append(i)
        need = [False] * n
        for i, o in enumerate(ops):
            for d in o["deps"]:
                od = ops[d]
                if od["dma"] is not None:
                    continue
                if od["eng"] == o["eng"] and o["dma"] is None:
                    if o["eng"] == "pe" or not SAME_ENGINE_SYNC:
                        continue
                need[d] = True
        for e in ENGS:
            for i in reversed(per_eng[e]):
                if ops[i]["dma"] is None:
                    need[i] = True
                    break
        mark = [None] * n
        for i, o in enumerate(ops):
            if o["dma"] is not None:
                nm = "d_" + o["dma"]
                sync.sem(nm)
                sync.cnt[nm] += 16
                mark[i] = (nm, sync.cnt[nm])
            elif need[i]:
                nm = "e_" + o["eng"]
                sync.sem(nm)
                sync.cnt[nm] += 1
                mark[i] = (nm, sync.cnt[nm])
        final = {nm: sync.cnt[nm] for nm in sync.cnt}
        waits = [None] * n
        for i, o in enumerate(ops):
            wl = {}
            for d in o["deps"]:
                od = ops[d]
                if od["dma"] is None and od["eng"] == o["eng"] and o["dma"] is None:
                    if o["eng"] == "pe" or not SAME_ENGINE_SYNC:
                        continue
                nm, v = mark[d]
                wl[nm] = max(wl.get(nm, 0), v)
            waits[i] = wl
        with nc.Block() as block:
            decos = {"pe": block.tensor, "act": block.scalar, "dve": block.vector,
                     "pool": block.gpsimd, "sp": block.sync}
            for eng in ENGS:
                idxs = per_eng[eng]

                def body(e, idxs=idxs):
                    waited = {}
                    for i in idxs:
                        o = ops[i]
                        for nm, v in waits[i].items():
                            if waited.get(nm, 0) >= v:
                                continue
                            e.wait_ge(sync.sems[nm], v)
                            waited[nm] = v
                        ins = o["fn"](e)
                        if mark[i] is not None:
                            nm, v = mark[i]
                            ins.then_inc(sync.sems[nm], 16 if o["dma"] is not None else 1)
                    for nm, v in final.items():
                        if v > 0 and waited.get(nm, 0) < v:
                            e.wait_ge(sync.sems[nm], v)

                decos[eng](body)


class PsumPool:
    def __init__(self, tensors, prefix):
        self.t = tensors
        self.prefix = prefix
        self.i = 0

    def get(self):
        k = self.i % len(self.t)
        self.i += 1
        return self.t[k], "%s%d" % (self.prefix, k)


def bcast(ap, axis, shape):
    return ap.unsqueeze(axis).to_broadcast(list(shape))


def phase_b0(nc, sync, G, weff, wpqT_d, skT_d):
    with ExitStack() as es:
        wq = es.enter_context(nc.sbuf_tensor("b0_wq", [128, 16, 1024], F32))
        sk = es.enter_context(nc.sbuf_tensor("b0_sk", [128, 16, 128], F32))
        P = Prog(nc, sync)
        for q in range(4):
            P.op("sp", lambda e, q=q: e.dma_start(
                out=wq[:, q * 4:(q + 1) * 4, :],
                in_=wpqT_d.rearrange("(g p) m -> p g m", p=128)[:, q * 4:(q + 1) * 4, :]),
                w=["wq"], dma="b0w")
        P.op("sp", lambda e: e.dma_start(out=sk[:], in_=skT_d.rearrange("(g p) n -> p g n", p=128)),
             w=["sk"], dma="b0w")
        for mc in range(8):
            for q in range(4):
                bank, bk = G["psf"].get()
                for j in range(4):
                    hp = q * 4 + j
                    P.op("pe", lambda e, bank=bank, hp=hp, mc=mc, j=j: e.matmul(
                        bank[:, j * 128:(j + 1) * 128], lhsT=wq[:, hp, mc * 128:(mc + 1) * 128],
                        rhs=sk[:, hp, :], start=True, stop=True), r=["wq", "sk"], w=[bk])
                eng = "act" if (mc * 4 + q) % 2 == 0 else "dve"
                if eng == "act":
                    P.op("act", lambda e, bank=bank, mc=mc, q=q: e.copy(
                        out=weff[:, mc, q * 512:(q + 1) * 512], in_=bank[:]), r=[bk], w=["weff"])
                else:
                    P.op("dve", lambda e, bank=bank, mc=mc, q=q: e.tensor_copy(
                        out=weff[:, mc, q * 512:(q + 1) * 512], in_=bank[:]), r=[bk], w=["weff"])
        P.emit()


def phase_bt(nc, sync, G, peer_u, peer_v, uv_scr):
    with ExitStack() as es:
        uv = [es.enter_context(nc.sbuf_tensor("bt_uv%d" % i, [128, 8, 2 * D], BF16)) for i in range(2)]
        P = Prog(nc, sync)
        for ch in range(16):
            s = ch % 2
            k = "uv%d" % s
            rows = slice(ch * 1024, (ch + 1) * 1024)
            P.op("pool", lambda e, s=s, rows=rows: e.dma_start(
                out=uv[s][:, :, 0:D], in_=peer_u[rows, :].rearrange("(p r) d -> p r d", r=8)), w=[k], dma="btl%d" % s)
            P.op("pool", lambda e, s=s, rows=rows: e.dma_start(
                out=uv[s][:, :, D:2 * D], in_=peer_v[rows, :].rearrange("(p r) d -> p r d", r=8)), w=[k], dma="btl%d" % s)
            P.op("sp", lambda e, s=s, rows=rows: e.dma_start(
                out=uv_scr[rows, :].rearrange("(p r) d -> p r d", r=8), in_=uv[s][:]), r=[k], dma="bts%d" % s)
        P.emit()


def phase_b(nc, sync, G, n_tiles, weff, h_scr, uv_scr, ln2g_d, ln2b_d, y_d, NS=16, ND=4, GS=8):
    with ExitStack() as es:
        def sb(name, shape, dt):
            return es.enter_context(nc.sbuf_tensor("b_" + name, shape, dt))
        ident, identf, iota16 = G["ident"], G["identf"], G["iota16"]
        g2 = sb("g2", [128, D], F32)
        b2 = sb("b2", [128, D], F32)
        h_t = [sb("h%d" % i, [128, D], F32) for i in range(2)]
        hb = sb("hb", [128, D], BF16)
        hT = sb("hT", [128, 8, 128], BF16)
        sc = sb("sc", [128, 16, 128], F32)
        sv = sb("sv", [128, 16, 16], F32)
        si = sb("si", [128, 16, 16], U32)
        sif = sb("sif", [128, 16, 16], F32)
        cand = sb("cand", [128, 8, 256], F32)
        ts = sb("ts", [128, 8, 16], F32)
        pos = sb("pos", [128, 8, 16], U32)
        pij = sb("pij", [128, 2, 128], U32)
        pijf = sb("pijf", [128, 2, 8, 16], F32)
        oh = [sb("oh%d" % i, [128, 8, 16, 16], F32) for i in range(2)]
        ee = sb("ee", [128, 2, 128], F32)
        ef = sb("ef", [128, 128], F32)
        eidx = [sb("eidx%d" % i, [128, 128], I32) for i in range(2)]
        dsm = sb("dsm", [128, 8, 16], F32)
        ex = sb("ex", [128, 8, 16], F32)
        ssum = sb("ssum", [128, 8], F32)
        gate = [sb("gate%d" % i, [128, 128], F32) for i in range(2)]
        dots = sb("dots", [128, 128], F32)
        actt = sb("actt", [128, 128], F32)
        wt = sb("wt", [128, 128], F32)
        junk = sb("junk", [128, D], BF16)
        uvb = [sb("uvb%d" % i, [128, 2 * D], BF16) for i in range(NS)]
        dg = [sb("dg%d" % i, [128, 128], BF16) for i in range(ND)]
        y_t = [sb("y%d" % i, [128, D], F32) for i in range(2)]
        st6 = sb("st6", [128, 12], F32)
        mv2 = sb("mv2", [128, 2], F32)
        rstd = sb("rstd", [128, 1], F32)
        nmr = sb("nmr", [128, 1], F32)
        pv = G["pv"]
        NG = 128 // GS

        P = Prog(nc, sync)
        P.op("sp", lambda e: e.dma_start(out=g2[:], in_=ln2g_d[:, :]), w=["g2"], dma="bw")
        P.op("sp", lambda e: e.dma_start(out=b2[:], in_=ln2b_d[:, :]), w=["b2"], dma="bw")
        cnt = dict(u=0, d=0)

        def load_h(t):
            s = t % 2
            P.op("sp", lambda e: e.dma_start(out=h_t[s][:], in_=h_scr[t * 128:(t + 1) * 128, :]),
                 w=["h%d" % s], dma="hld%d" % s)

        def front(t):
            s = t % 2
            hk_ = "h%d" % s
            P.op("act", lambda e: e.copy(out=hb[:], in_=h_t[s][:]), r=[hk_], w=["hb"])
            pb, pbk = G["psb"].get()
            for c in range(8):
                P.op("pe", lambda e, c=c: e.transpose(out=pb[:, c * 128:(c + 1) * 128],
                                                       in_=hb[:, c * 128:(c + 1) * 128], identity=ident[:]),
                     r=["hb", "ident"], w=[pbk])
            P.op("act", lambda e: e.copy(out=hT[:].rearrange("p c t -> p (c t)"), in_=pb[:]), r=[pbk], w=["hT"])
            for nb in range(4):
                bank, bk = G["psf"].get()
                for c in range(8):
                    P.op("pe", lambda e, c=c, nb=nb, bank=bank: e.matmul(
                        bank[:], lhsT=hT[:, c, :], rhs=weff[:, c, nb * 512:(nb + 1) * 512],
                        start=(c == 0), stop=(c == 7)), r=["hT", "weff"], w=[bk])
                for q in range(4):
                    g = nb * 4 + q
                    P.op("act", lambda e, q=q, g=g, bank=bank: e.copy(out=sc[:, g, :], in_=bank[:, q * 128:(q + 1) * 128]),
                         r=[bk], w=["sc%d" % g])
            for g in range(16):
                P.op("dve", lambda e, g=g: e.max(out=sv[:, g, 0:8], in_=sc[:, g, :]), r=["sc%d" % g], w=["sva%d" % g])
            for g in range(16):
                P.op("dve", lambda e, g=g: e.max_index(out=si[:, g, 0:8], in_max=sv[:, g, 0:8], in_values=sc[:, g, :]),
                     r=["sc%d" % g, "sva%d" % g], w=["sia%d" % g])
            for g in range(16):
                P.op("dve", lambda e, g=g: e.match_replace(out=sc[:, g, :], in_to_replace=sv[:, g, 0:8],
                                                           in_values=sc[:, g, :], imm_value=-1e30),
                     r=["sva%d" % g], w=["sc%d" % g])
            for g in range(16):
                P.op("dve", lambda e, g=g: e.max(out=sv[:, g, 8:16], in_=sc[:, g, :]), r=["sc%d" % g], w=["svb%d" % g])
            for g in range(16):
                P.op("dve", lambda e, g=g: e.max_index(out=si[:, g, 8:16], in_max=sv[:, g, 8:16], in_values=sc[:, g, :]),
                     r=["sc%d" % g, "svb%d" % g], w=["sib%d" % g])
            allsv = ["sva%d" % g for g in range(16)] + ["svb%d" % g for g in range(16)]
            allsi = ["sia%d" % g for g in range(16)] + ["sib%d" % g for g in range(16)]
            P.op("dve", lambda e: e.tensor_copy(out=sif[:], in_=si[:]), r=allsi, w=["sif"])
            sv4 = sv[:].rearrange("p (h two) k -> p h two k", two=2)
            P.op("dve", lambda e: e.tensor_tensor(
                out=cand[:].rearrange("p h (i j) -> p h i j", j=16),
                in0=bcast(sv4[:, :, 0, :], 3, [128, 8, 16, 16]),
                in1=bcast(sv4[:, :, 1, :], 2, [128, 8, 16, 16]), op=ALU.add), r=allsv,
                w=["cand%d" % h for h in range(8)])
            for h in range(8):
                P.op("dve", lambda e, h=h: e.max(out=ts[:, h, 0:8], in_=cand[:, h, :]), r=["cand%d" % h], w=["tsa%d" % h])
            for h in range(8):
                P.op("dve", lambda e, h=h: e.max_index(out=pos[:, h, 0:8], in_max=ts[:, h, 0:8], in_values=cand[:, h, :]),
                     r=["cand%d" % h, "tsa%d" % h], w=["posa%d" % h])
            for h in range(8):
                P.op("dve", lambda e, h=h: e.match_replace(out=cand[:, h, :], in_to_replace=ts[:, h, 0:8],
                                                           in_values=cand[:, h, :], imm_value=-1e30),
                     r=["tsa%d" % h], w=["cand%d" % h])
            for h in range(8):
                P.op("dve", lambda e, h=h: e.max(out=ts[:, h, 8:16], in_=cand[:, h, :]), r=["cand%d" % h], w=["tsb%d" % h])
            for h in range(8):
                P.op("dve", lambda e, h=h: e.max_index(out=pos[:, h, 8:16], in_max=ts[:, h, 8:16], in_values=cand[:, h, :]),
                     r=["cand%d" % h, "tsb%d" % h], w=["posb%d" % h])
            allts = ["tsa%d" % h for h in range(8)] + ["tsb%d" % h for h in range(8)]
            allpos = ["posa%d" % h for h in range(8)] + ["posb%d" % h for h in range(8)]
            posf = pos[:].rearrange("p h k -> p (h k)")
            P.op("dve", lambda e: e.tensor_single_scalar(out=pij[:, 0, :], in_=posf, scalar=4,
                                                         op=ALU.logical_shift_right), r=allpos, w=["pij0"])
            P.op("dve", lambda e: e.tensor_single_scalar(out=pij[:, 1, :], in_=posf, scalar=15,
                                                         op=ALU.bitwise_and), r=allpos, w=["pij1"])
            P.op("dve", lambda e: e.tensor_copy(out=pijf[:].rearrange("p a h k -> p a (h k)"), in_=pij[:]),
                 r=["pij0", "pij1"], w=["pijf"])
            sif4 = sif[:].rearrange("p (h two) k -> p h two k", two=2)
            for a in range(2):
                P.op("dve", lambda e, a=a: e.tensor_tensor(
                    out=oh[a][:], in0=bcast(pijf[:, a, :, :], 3, [128, 8, 16, 16]),
                    in1=iota16[:].unsqueeze(1).unsqueeze(1).to_broadcast([128, 8, 16, 16]),
                    op=ALU.is_equal), r=["pijf"], w=["oh%d" % a])
            for a in range(2):
                P.op("dve", lambda e, a=a: e.tensor_tensor(
                    out=oh[a][:], in0=oh[a][:], in1=bcast(sif4[:, :, a, :], 2, [128, 8, 16, 16]),
                    op=ALU.mult), r=["sif"], w=["oh%d" % a])
            for a in range(2):
                P.op("dve", lambda e, a=a: e.tensor_reduce(
                    out=ee[:, a, :], in_=oh[a][:].rearrange("p h k i -> p (h k) i"), axis=AX.X, op=ALU.add),
                    r=["oh%d" % a], w=["ee%d" % a])
            P.op("dve", lambda e: e.scalar_tensor_tensor(out=ef[:], in0=ee[:, 0, :], scalar=128.0, in1=ee[:, 1, :],
                                                         op0=ALU.mult, op1=ALU.add), r=["ee0", "ee1"], w=["ef"])
            ek = "eidx%d" % s
            P.op("dve", lambda e: e.tensor_copy(out=eidx[s][:], in_=ef[:]), r=["ef"], w=[ek])
            P.op("dve", lambda e: e.tensor_tensor(out=dsm[:], in0=ts[:], in1=ts[:, :, 0:1].to_broadcast([128, 8, 16]),
                                                  op=ALU.subtract), r=allts, w=["dsm"])
            P.op("act", lambda e: e.activation(out=ex[:], in_=dsm[:], func=AF.Exp), r=["dsm"], w=["ex"])
            P.op("dve", lambda e: e.tensor_reduce(out=ssum[:], in_=ex[:], axis=AX.X, op=ALU.add), r=["ex"], w=["ssum"])
            P.op("dve", lambda e: e.reciprocal(out=ssum[:], in_=ssum[:]), r=["ssum"], w=["ssum"])
            P.op("dve", lambda e: e.tensor_tensor(out=gate[s][:].rearrange("p (h k) -> p h k", k=16), in0=ex[:],
                                                  in1=ssum[:].unsqueeze(2).to_broadcast([128, 8, 16]), op=ALU.mult),
                 r=["ex", "ssum"], w=["gate%d" % s])

        def group(t, gq):
            s = t % 2
            hk_, ek = "h%d" % s, "eidx%d" % s
            cols = slice(gq * GS, (gq + 1) * GS)
            slots = []
            for hk in range(gq * GS, (gq + 1) * GS):
                u = cnt["u"] % NS
                cnt["u"] += 1
                slots.append(u)
                P.op("pool", lambda e, u=u, hk=hk: e.indirect_dma_start(
                    out=uvb[u][:], out_offset=None, in_=uv_scr[:, :],
                    in_offset=bass.IndirectOffsetOnAxis(ap=eidx[s][:, hk:hk + 1], axis=0)),
                    r=[ek], w=["uvb%d" % u], dma="gu%d" % u)
                P.op("dve", lambda e, u=u, hk=hk: e.scalar_tensor_tensor(
                    out=junk[:], in0=uvb[u][:, 0:D], scalar=1.0, in1=h_t[s][:],
                    op0=ALU.mult, op1=ALU.mult, accum_out=dots[:, hk:hk + 1]),
                    r=["uvb%d" % u, hk_], w=["dots%d" % gq])
            P.op("act", lambda e: e.activation(out=actt[:, cols], in_=dots[:, cols], func=AF.Gelu),
                 r=["dots%d" % gq], w=["actt%d" % gq])
            P.op("dve", lambda e: e.tensor_tensor(out=wt[:, cols], in0=gate[s][:, cols], in1=actt[:, cols], op=ALU.mult),
                 r=["gate%d" % s, "actt%d" % gq], w=["wt%d" % gq])
            for j, hk in enumerate(range(gq * GS, (gq + 1) * GS)):
                u = slots[j]
                d = cnt["d"] % ND
                cnt["d"] += 1
                P.op("act", lambda e, d=d, hk=hk: e.activation(out=dg[d][:], in_=identf[:], func=AF.Copy,
                                                               scale=wt[:, hk:hk + 1]),
                     r=["wt%d" % gq, "identf"], w=["dg%d" % d])
                for half in range(2):
                    P.op("pe", lambda e, d=d, u=u, half=half, hk=hk: e.matmul(
                        pv[half][:], lhsT=dg[d][:], rhs=uvb[u][:, D + half * 512:D + (half + 1) * 512],
                        start=(hk == 0), stop=(hk == 127)), r=["dg%d" % d, "uvb%d" % u], w=["pv%d" % half])

        def tail(t):
            s = t % 2
            hk_, yk = "h%d" % s, "y%d" % s
            for half in range(2):
                P.op("dve", lambda e, half=half: e.scalar_tensor_tensor(
                    out=y_t[s][:, half * 512:(half + 1) * 512], in0=h_t[s][:, half * 512:(half + 1) * 512],
                    scalar=ALPHA, in1=pv[half][:], op0=ALU.mult, op1=ALU.add),
                    r=[hk_, "pv%d" % half], w=[yk])
            layer_norm_tail(P, y_t[s], yk, g2, "g2", b2, "b2", st6, mv2, rstd, nmr, "b", gb_eng="dve")
            P.op("sp", lambda e: e.dma_start(out=y_d[t * 128:(t + 1) * 128, :], in_=y_t[s][:]),
                 r=[yk], dma="yst%d" % s)

        load_h(0)
        front(0)
        for t in range(n_tiles):
            if t + 1 < n_tiles:
                load_h(t + 1)
            for gq in range(NG):
                group(t, gq)
                if gq == NG // 2 - 1 and t + 1 < n_tiles:
                    front(t + 1)
            tail(t)
        P.emit()


def layer_norm_tail(P, z, zk, g, gk, b, bk, st6, mv2, rstd, nmr, pfx, gb_eng="pool"):
    k6, k2, kr, kn = pfx + "st6", pfx + "mv2", pfx + "rstd", pfx + "nmr"
    for half in range(2):
        P.op("dve", lambda e, half=half: e.bn_stats(out=st6[:, half * 6:(half + 1) * 6],
                                                     in_=z[:, half * 512:(half + 1) * 512]), r=[zk], w=[k6])
    P.op("dve", lambda e: e.bn_aggr(out=mv2[:], in_=st6[:]), r=[k6], w=[k2])
    P.op("act", lambda e: e.activation(out=rstd[:], in_=mv2[:, 1:2], func=AF.Sqrt, bias=EPS_T[0][:], scale=1.0),
         r=[k2, "eps"], w=[kr])
    P.op("dve", lambda e: e.reciprocal(out=rstd[:], in_=rstd[:]), r=[kr], w=[kr])
    P.op("dve", lambda e: e.scalar_tensor_tensor(out=nmr[:], in0=mv2[:, 0:1], scalar=-1.0, in1=rstd[:],
                                                 op0=ALU.mult, op1=ALU.mult), r=[k2, kr], w=[kn])
    P.op("act", lambda e: e.activation(out=z[:], in_=z[:], func=AF.Identity, bias=nmr[:], scale=rstd[:]),
         r=[zk, kr, kn], w=[zk])
    P.op(gb_eng, lambda e: e.tensor_tensor(out=z[:], in0=z[:], in1=g[:], op=ALU.mult), r=[zk, gk], w=[zk])
    P.op(gb_eng, lambda e: e.tensor_tensor(out=z[:], in0=z[:], in1=b[:], op=ALU.add), r=[zk, bk], w=[zk])


def alloc_globals(nc, es, sync):
    G = {}
    psf = [es.enter_context(nc.psum_tensor("psf%d" % i, [128, 512], F32)) for i in range(4)]
    pv = [es.enter_context(nc.psum_tensor("pv%d" % i, [128, 512], F32)) for i in range(2)]
    psb = [es.enter_context(nc.psum_tensor("psb%d" % i, [128, 1024], BF16)) for i in range(2)]
    G["psf"] = PsumPool(psf, "psf")
    G["psb"] = PsumPool(psb, "psb")
    G["pv"] = pv
    G["ident"] = es.enter_context(nc.sbuf_tensor("sb_ident", [128, 128], BF16))
    G["identf"] = es.enter_context(nc.sbuf_tensor("sb_identf", [128, 128], F32))
    G["iota16"] = es.enter_context(nc.sbuf_tensor("sb_iota16", [128, 16], F32))
    G["eps"] = es.enter_context(nc.sbuf_tensor("sb_eps", [128, 1], F32))
    EPS_T[0] = G["eps"]
    return G


def phase_const(nc, sync, G, identf_d, iota16_d):
    P = Prog(nc, sync)
    P.op("sp", lambda e: e.dma_start(out=G["identf"][:], in_=identf_d[:, :]), w=["identf"], dma="c0")
    P.op("sp", lambda e: e.dma_start(out=G["iota16"][:], in_=iota16_d[:, :]), w=["iota16"], dma="c0")
    P.op("dve", lambda e: e.tensor_copy(out=G["ident"][:], in_=G["identf"][:]), r=["identf"], w=["ident"])
    P.op("dve", lambda e: e.memset(G["eps"][:], LN_EPS), w=["eps"])
    P.emit()


def build_b_only(n_tiles, tab_dt=F32):
    nc = bass.Bass("TRN2", target_bir_lowering=False)
    T = n_tiles * 128
    dt = lambda name, shape, dtype, kind="ExternalInput": nc.dram_tensor(name, shape, dtype, kind=kind).ap()
    h_d = dt("h_in", [T, D], F32)
    wpqT_d = dt("wpqT", [2048, D], F32)
    skT_d = dt("skT", [2048, 128], F32)
    pu = dt("peer_u", [N_EXP, D], tab_dt)
    pvv = dt("peer_v", [N_EXP, D], tab_dt)
    g2 = dt("ln2g", [128, D], F32)
    b2 = dt("ln2b", [128, D], F32)
    identf_d = dt("identf", [128, 128], F32)
    iota_d = dt("iota16", [128, 16], F32)
    y_d = dt("y", [T, D], F32, kind="ExternalOutput")
    with ExitStack() as es:
        sync = Sync(nc, es)
        G = alloc_globals(nc, es, sync)
        phase_const(nc, sync, G, identf_d, iota_d)
        weff = es.enter_context(nc.sbuf_tensor("sb_weff", [128, 8, 2048], BF16))
        phase_b0(nc, sync, G, weff, wpqT_d, skT_d)
        uv_scr = nc.dram_tensor("uv_scr", [N_EXP, 2 * D], BF16, kind="Internal").ap()
        phase_bt(nc, sync, G, pu, pvv, uv_scr)
        phase_b(nc, sync, G, n_tiles, weff, h_d, uv_scr, g2, b2, y_d)
    return nc


def phase_a0(nc, sync, G, mkT, mv_aug, mem_d, memg_d, memb_d, wkv_d):
    with ExitStack() as es:
        def sb(name, shape, dt):
            return es.enter_context(nc.sbuf_tensor("a0_" + name, shape, dt))
        ident = G["ident"]
        wkv = sb("wkv", [128, 8, 1024], BF16)
        mg = sb("mg", [128, D], F32)
        mb = sb("mb", [128, D], F32)
        mt = [sb("mt%d" % i, [128, D], F32) for i in range(2)]
        mn = sb("mn", [128, D], BF16)
        mnT = sb("mnT", [128, 8, 256], BF16)
        st6 = sb("st6", [128, 12], F32)
        mv2 = sb("mv2", [128, 2], F32)
        rstd = sb("rstd", [128, 1], F32)
        nmr = sb("nmr", [128, 1], F32)
        P = Prog(nc, sync)
        for c in range(8):
            P.op("pool", lambda e, c=c: e.dma_start(out=wkv[:, c, :], in_=wkv_d[c * 128:(c + 1) * 128, :]),
                 w=["wkv"], dma="a0w")
        P.op("sp", lambda e: e.dma_start(out=mg[:], in_=memg_d[:, :]), w=["mg"], dma="a0p")
        P.op("sp", lambda e: e.dma_start(out=mb[:], in_=memb_d[:, :]), w=["mb"], dma="a0p")
        P.op("dve", lambda e: e.memset(mv_aug[:], 1.0), w=["mv_aug"])
        for mc in range(2):
            mk_ = "mt%d" % mc
            P.op("sp", lambda e, mc=mc: e.dma_start(out=mt[mc][:], in_=mem_d[mc * 128:(mc + 1) * 128, :]),
                 w=[mk_], dma="a0m%d" % mc)
            layer_norm_tail(P, mt[mc], mk_, mg, "mg", mb, "mb", st6, mv2, rstd, nmr, "a0", gb_eng="dve")
            P.op("act", lambda e, mc=mc: e.copy(out=mn[:], in_=mt[mc][:]), r=[mk_], w=["mn"])
            pb, pbk = G["psb"].get()
            for c in range(8):
                P.op("pe", lambda e, c=c, pb=pb: e.transpose(out=pb[:, c * 128:(c + 1) * 128],
                                                             in_=mn[:, c * 128:(c + 1) * 128], identity=ident[:]),
                     r=["mn", "ident"], w=[pbk])
            P.op("act", lambda e, mc=mc, pb=pb: e.copy(out=mnT[:, :, mc * 128:(mc + 1) * 128],
                                                       in_=pb[:].rearrange("p (c t) -> p c t", t=128)),
                 r=[pbk], w=["mnT"])
        for h in range(4):
            bank, bk = G["psA"].get()
            for c in range(8):
                P.op("pe", lambda e, c=c, h=h, bank=bank: e.matmul(
                    bank[:, 0:256], lhsT=wkv[:, c, h * 128:(h + 1) * 128], rhs=mnT[:, c, :],
                    start=(c == 0), stop=(c == 7)), r=["wkv", "mnT"], w=[bk])
            P.op("act", lambda e, h=h, bank=bank: e.copy(out=mkT[:, h, :], in_=bank[:, 0:256]), r=[bk], w=["mkT"])
        for mc in range(2):
            bank, bk = G["psA"].get()
            for c in range(8):
                P.op("pe", lambda e, c=c, mc=mc, bank=bank: e.matmul(
                    bank[:], lhsT=mnT[:, c, mc * 128:(mc + 1) * 128], rhs=wkv[:, c, 512:1024],
                    start=(c == 0), stop=(c == 7)), r=["wkv", "mnT"], w=[bk])
            P.op("dve", lambda e, mc=mc, bank=bank: e.tensor_copy(
                out=mv_aug[:, mc, :, 0:128], in_=bank[:].rearrange("p (h d) -> p h d", d=128)),
                r=[bk], w=["mv_aug"])
        P.emit()


def phase_a1(nc, sync, G, n_tiles, mkT, mv_aug, xT_d, w1_d, b1_d, cc_d, ss_d, dt_d, wq_d, wk_d, cd_d,
             mask_d, gng_d, sink_d, brT_scr):
    with ExitStack() as es:
        def sb(name, shape, dt):
            return es.enter_context(nc.sbuf_tensor("a1_" + name, shape, dt))
        ident = G["ident"]
        FM = FM_COLS
        w1 = sb("w1", [128, 8, NCOL_A1], BF16)
        b1 = sb("b1", [1, NCOL_A1], BF16)
        ones = sb("ones", [1, 128], BF16)
        dtab = sb("dtab", [128, 4, 128], F32)
        wqt = sb("wqt", [128, 2, 128], F32)
        wkt = sb("wkt", [128, 4], F32)
        cdt = sb("cdt", [128, 2], F32)
        msk = sb("msk", [128, 2, 128], F32)
        gng = sb("gng", [128, 512], F32)
        esink = sb("esink", [128, 8], F32)
        state = sb("state", [128, 2, 128], F32)
        state_bf = sb("state_bf", [128, 2, 128], BF16)
        xT = [sb("xT%d" % i, [128, 8, 128], BF16) for i in range(2)]
        cct = [sb("cc%d" % i, [128, 128], F32) for i in range(2)]
        sst = [sb("ss%d" % i, [128, 128], F32) for i in range(2)]
        qT = sb("qT", [128, 4, 128], BF16)
        kT = [sb("kT%d" % i, [128, 2, 2, 128], BF16) for i in range(2)]
        kpad = sb("kpad", [128, 2, 2, 128], BF16)
        qspad = sb("qspad", [128, 2, 2, 128], BF16)
        vaug = [sb("vaug%d" % i, [128, 2, 65], BF16) for i in range(2)]
        mqT = sb("mqT", [128, 4, 128], BF16)
        tmp1 = sb("tmp1", [128, 4, 128], F32)
        tmp2 = sb("tmp2", [128, 4, 128], F32)
        rot = sb("rot", [128, 4, 128], BF16)
        ktok = sb("ktok", [128, 256], BF16)
        vret = sb("vret", [128, 4, 128], BF16)
        vw = sb("vw", [128, 4, 128], BF16)
        sg = sb("sg", [128, 512], F32)
        pT = sb("pT", [128, 2, 8, 128], BF16)
        pTc = sb("pTc", [128, 2, 4, 128], BF16)
        innerTm = sb("innerTm", [128, 4, 128], BF16)
        den = sb("den", [128, 8], F32)
        denc = sb("denc", [128, 4], F32)
        br = sb("br", [128, 3, 512], BF16)
        xn = sb("xn", [128, 512], F32)
        gst = sb("gst", [128, 24], F32)
        gmv = sb("gmv", [128, 4, 2], F32)
        grs = sb("grs", [128, 4], F32)
        brT = [sb("brT%d" % i, [128, 12, 128], BF16) for i in range(2)]

        P = Prog(nc, sync)
        for c in range(8):
            P.op("pool", lambda e, c=c: e.dma_start(out=w1[:, c, :], in_=w1_d[c * 128:(c + 1) * 128, :],
                                                    max_dma_last_dim=4096), w=["w1"], dma="a1w")
        P.op("pool", lambda e: e.dma_start(out=b1[:], in_=b1_d[:, :], max_dma_last_dim=4096), w=["b1"], dma="a1w")
        for (tt, dd, kk) in ((dtab, dt_d, "dtab"), (wqt, wq_d, "wqt")):
            P.op("sp", lambda e, tt=tt, dd=dd: e.dma_start(out=tt[:].rearrange("p a b -> p (a b)"), in_=dd[:, :]),
                 w=[kk], dma="a1p")
        P.op("sp", lambda e: e.dma_start(out=msk[:].rearrange("p a b -> p (a b)"), in_=mask_d[:, :]),
             w=["msk"], dma="a1p")
        for (tt, dd, kk) in ((wkt, wk_d, "wkt"), (cdt, cd_d, "cdt"), (gng, gng_d, "gng"), (esink, sink_d, "esink")):
            P.op("sp", lambda e, tt=tt, dd=dd: e.dma_start(out=tt[:], in_=dd[:, :]), w=[kk], dma="a1p")
        P.op("act", lambda e: e.activation(out=esink[:], in_=esink[:], func=AF.Exp), r=["esink"], w=["esink"])
        P.op("dve", lambda e: e.memset(ones[:], 1.0), w=["ones"])
        P.op("dve", lambda e: e.memset(state[:], 0.0), w=["state"])
        P.op("dve", lambda e: e.memset(state_bf[:], 0.0), w=["state_bf"])
        for i in range(2):
            P.op("dve", lambda e, i=i: e.memset(vaug[i][:], 1.0), w=["vaug%d" % i])
            P.op("dve", lambda e, i=i: e.memset(kT[i][:], 0.0), w=["kT%d" % i])
        P.op("dve", lambda e: e.memset(kpad[:], 0.0), w=["kpad"])
        P.op("dve", lambda e: e.memset(qspad[:], 0.0), w=["qspad"])

        def loads(t):
            s = t % 2
            P.op("pool", lambda e: e.dma_start(
                out=xT[s][:], in_=xT_d.rearrange("(c p) t -> p c t", p=128)[:, :, t * 128:(t + 1) * 128]),
                w=["xT%d" % s], dma="a1x%d" % s)
            P.op("sp", lambda e: e.dma_start(out=cct[s][:], in_=cc_d[:, t * 128:(t + 1) * 128]),
                 w=["cc%d" % s], dma="a1c%d" % s)
            P.op("sp", lambda e: e.dma_start(out=sst[s][:], in_=ss_d[:, t * 128:(t + 1) * 128]),
                 w=["ss%d" % s], dma="a1c%d" % s)

        def fm_group(s, fbs, bank, bk):
            xk = "xT%d" % s
            for j, fb in enumerate(fbs):
                for c in range(8):
                    P.op("pe", lambda e, c=c, fb=fb, j=j: e.matmul(
                        bank[:, j * 128:(j + 1) * 128], lhsT=w1[:, c, fb * 128:(fb + 1) * 128], rhs=xT[s][:, c, :],
                        start=(c == 0), stop=False), r=["w1", xk], w=[bk])
                P.op("pe", lambda e, fb=fb, j=j: e.matmul(
                    bank[:, j * 128:(j + 1) * 128], lhsT=b1[0:1, fb * 128:(fb + 1) * 128], rhs=ones[0:1, :],
                    start=False, stop=True), r=["b1", "ones"], w=[bk])

        def tm_group(s, col0, ncols, bank, bk):
            xk = "xT%d" % s
            for c in range(8):
                P.op("pe", lambda e, c=c: e.matmul(
                    bank[:, 0:ncols], lhsT=xT[s][:, c, :], rhs=w1[:, c, col0:col0 + ncols],
                    start=(c == 0), stop=False), r=["w1", xk], w=[bk])
            P.op("pe", lambda e: e.matmul(bank[:, 0:ncols], lhsT=ones[0:1, :], rhs=b1[0:1, col0:col0 + ncols],
                                          start=False, stop=True), r=["b1", "ones"], w=[bk])

        def do_tile(t):
            s = t % 2
            sp_ = 1 - s
            if t + 1 < n_tiles:
                loads(t + 1)
            if A1_STAGE[0] == -1:
                return
            bank, bk = G["psA"].get()
            fm_group(s, [0, 1, 2, 3], bank, bk)
            P.op("act", lambda e: e.copy(out=qT[:].rearrange("p a b -> p (a b)"), in_=bank[:]), r=[bk], w=["qT"])
            if A1_STAGE[0] == -2:
                return
            bankk, bkk = G["psA"].get()
            fm_group(s, [4, 5], bankk, bkk)
            kTk = "kT%d" % s
            for hf in range(2):
                P.op("act", lambda e, hf=hf: e.copy(
                    out=kT[s][hf * 64:(hf + 1) * 64, :, hf, :],
                    in_=bankk[hf * 64:(hf + 1) * 64, 0:256].rearrange("p (g t) -> p g t", t=128)), r=[bkk], w=[kTk])
            if A1_STAGE[0] == -3:
                return
            bra, bkra = G["psA"].get()
            fm_group(s, [6, 7, 8, 9], bra, bkra)
            brb, bkrb = G["psA"].get()
            fm_group(s, [10, 11, 12, 13], brb, bkrb)
            P.op("dve", lambda e: e.tensor_tensor(out=tmp1[:], in0=bra[:].rearrange("p (a b) -> p a b", b=128),
                                                  in1=bcast(cct[s][:], 1, [128, 4, 128]), op=ALU.mult),
                 r=[bkra, "cc%d" % s], w=["tmp1"])
            P.op("dve", lambda e: e.tensor_tensor(out=tmp2[:], in0=brb[:].rearrange("p (a b) -> p a b", b=128),
                                                  in1=bcast(sst[s][:], 1, [128, 4, 128]), op=ALU.mult),
                 r=[bkrb, "ss%d" % s], w=["tmp2"])
            P.op("pool", lambda e: e.tensor_tensor(out=rot[:], in0=tmp1[:], in1=tmp2[:], op=ALU.add),
                 r=["tmp1", "tmp2"], w=["rot"])
            for hf in range(2):
                rows = slice(hf * 64, (hf + 1) * 64)
                P.op("pool", lambda e, hf=hf, rows=rows: e.tensor_tensor(
                    out=qspad[rows, :, hf, :], in0=rot[rows, 0:2, :], in1=wqt[rows, :, :], op=ALU.mult),
                    r=["rot", "wqt"], w=["qspad"])
                P.op("pool", lambda e, hf=hf, rows=rows: e.tensor_copy(out=kpad[rows, :, hf, :], in_=rot[rows, 2:4, :]),
                     r=["rot"], w=["kpad"])
            if A1_STAGE[0] == -4:
                return
            bankm, bkm = G["psA"].get()
            fm_group(s, [14, 15, 16, 17], bankm, bkm)
            P.op("act", lambda e: e.copy(out=mqT[:].rearrange("p a b -> p (a b)"), in_=bankm[:]), r=[bkm], w=["mqT"])
            if A1_STAGE[0] == -5:
                return
            bav, bkav = G["psA"].get()
            tm_group(s, FM, 128, bav, bkav)
            vk = "vaug%d" % s
            P.op("act", lambda e: e.copy(out=vaug[s][:, :, 0:64], in_=bav[:, 0:128].rearrange("p (g d) -> p g d", d=64)),
                 r=[bkav], w=[vk])
            if A1_STAGE[0] == -6:
                return
            brv, bkrv = G["psA"].get()
            tm_group(s, FM + 128, 512, brv, bkrv)
            P.op("act", lambda e: e.copy(out=vret[:].rearrange("p a b -> p (a b)"), in_=brv[:]), r=[bkrv], w=["vret"])
            if A1_STAGE[0] == -8:
                return
            VV = _os.environ.get("VV", "2")
            if VV == "0":
                P.op("dve", lambda e: e.tensor_tensor(out=vw[:], in0=brv[:].rearrange("p (a b) -> p a b", b=128),
                                                      in1=bcast(wkt[:], 2, [128, 4, 128]), op=ALU.mult),
                     r=[bkrv, "wkt"], w=["vw"])
            elif VV == "1":
                P.op("dve", lambda e: e.tensor_tensor(out=tmp1[:], in0=brv[:].rearrange("p (a b) -> p a b", b=128),
                                                      in1=bcast(wkt[:], 2, [128, 4, 128]), op=ALU.mult),
                     r=[bkrv, "wkt"], w=["tmp1"])
            elif VV == "2":
                P.op("dve", lambda e: e.tensor_tensor(out=vw[:], in0=vret[:],
                                                      in1=bcast(wkt[:], 2, [128, 4, 128]), op=ALU.mult),
                     r=["vret", "wkt"], w=["vw"])
            elif VV == "3":
                for h in range(4):
                    P.op("dve", lambda e, h=h: e.tensor_scalar(
                        out=vw[:, h, :], in0=brv[:, h * 128:(h + 1) * 128], scalar1=wkt[:, h:h + 1], scalar2=None,
                        op0=ALU.mult), r=[bkrv, "wkt"], w=["vw"])
            if A1_STAGE[0] == -7:
                return
            brg, bkrg = G["psA"].get()
            tm_group(s, FM + 640, 512, brg, bkrg)
            P.op("act", lambda e: e.activation(out=sg[:], in_=brg[:], func=AF.Silu), r=[bkrg], w=["sg"])
            P.op("pool", lambda e: e.tensor_tensor(out=sg[:], in0=sg[:], in1=gng[:], op=ALU.mult),
                 r=["sg", "gng"], w=["sg"])
            if A1_STAGE[0] == 1:
                return
            whichs = [(0, s)] + ([(1, sp_)] if t > 0 else [])
            for hb2 in range(2):
                for (wi, slot) in whichs:
                    bl, bkl = G["psA"].get()
                    for j in range(4):
                        h = 4 * hb2 + j
                        i, half = h // 2, h % 2
                        P.op("pe", lambda e, j=j, i=i, half=half, slot=slot, bl=bl, hb2=hb2: e.matmul(
                            bl[:, j * 128:(j + 1) * 128], lhsT=kT[slot][:, hb2, half, :],
                            rhs=qT[:, i, :], start=True, stop=True),
                            r=["kT%d" % slot, "qT"], w=[bkl])
                    pk = "pT%d%d" % (wi, hb2)
                    P.op("act", lambda e, wi=wi, bl=bl, hb2=hb2: e.activation(
                        out=pT[:, wi, hb2 * 4:(hb2 + 1) * 4, :].rearrange("p a b -> p (a b)"), in_=bl[:],
                        func=AF.Exp, scale=0.125), r=[bkl], w=[pk])
                    if A1_STAGE[0] == 11:
                        continue
                    P.op("pool", lambda e, wi=wi, hb2=hb2: e.tensor_tensor(
                        out=pT[:, wi, hb2 * 4:(hb2 + 1) * 4, :], in0=pT[:, wi, hb2 * 4:(hb2 + 1) * 4, :],
                        in1=bcast(msk[:, wi, :], 1, [128, 4, 128]), op=ALU.mult), r=[pk, "msk"], w=[pk])
            if A1_STAGE[0] in (11, 12):
                return
            for hb2 in range(2):
                bo, bko = G["psA"].get()
                for j in range(4):
                    h = 4 * hb2 + j
                    if t > 0:
                        P.op("pe", lambda e, j=j, h=h, bo=bo, hb2=hb2: e.matmul(
                            bo[:, j * 65:(j + 1) * 65], lhsT=pT[:, 1, h, :], rhs=vaug[sp_][:, hb2, :],
                            start=True, stop=False), r=["pT1%d" % hb2, "vaug%d" % sp_], w=[bko])
                    P.op("pe", lambda e, j=j, h=h, bo=bo, hb2=hb2: e.matmul(
                        bo[:, j * 65:(j + 1) * 65], lhsT=pT[:, 0, h, :], rhs=vaug[s][:, hb2, :],
                        start=(t == 0), stop=True), r=["pT0%d" % hb2, vk], w=[bko])
                bo3 = bo[:, 0:260].rearrange("p (j d) -> p j d", d=65)
                if A1_STAGE[0] == 13:
                    continue
                dk = "den%d" % hb2
                P.op("dve", lambda e, bo3=bo3, hb2=hb2: e.tensor_tensor(
                    out=den[:, hb2 * 4:(hb2 + 1) * 4], in0=bo3[:, :, 64], in1=esink[:, hb2 * 4:(hb2 + 1) * 4],
                    op=ALU.add), r=[bko, "esink"], w=[dk])
                P.op("dve", lambda e, hb2=hb2: e.reciprocal(out=den[:, hb2 * 4:(hb2 + 1) * 4], in_=den[:, hb2 * 4:(hb2 + 1) * 4]),
                     r=[dk], w=[dk])
                P.op("dve", lambda e, bo3=bo3, hb2=hb2: e.tensor_tensor(
                    out=br[:, 0, hb2 * 256:(hb2 + 1) * 256].rearrange("p (j d) -> p j d", d=64),
                    in0=bo3[:, :, 0:64], in1=bcast(den[:, hb2 * 4:(hb2 + 1) * 4], 2, [128, 4, 64]), op=ALU.mult),
                    r=[bko, dk], w=["br0"])
            if A1_STAGE[0] == 2:
                return
            for mc in range(2):
                bl, bkl = G["psA"].get()
                for h in range(4):
                    P.op("pe", lambda e, h=h, mc=mc, bl=bl: e.matmul(
                        bl[:, h * 128:(h + 1) * 128], lhsT=mkT[:, h, mc * 128:(mc + 1) * 128], rhs=mqT[:, h, :],
                        start=True, stop=True), r=["mkT", "mqT"], w=[bkl])
                P.op("act", lambda e, mc=mc, bl=bl: e.activation(
                    out=pTc[:, mc, :, :].rearrange("p a b -> p (a b)"), in_=bl[:], func=AF.Exp,
                    scale=float(128 ** -0.5)), r=[bkl], w=["pTc%d" % mc])
            for hp2 in range(2):
                bo, bko = G["psA"].get()
                for j in range(2):
                    h = 2 * hp2 + j
                    for mc in range(2):
                        P.op("pe", lambda e, j=j, h=h, mc=mc, bo=bo: e.matmul(
                            bo[:, j * 129:(j + 1) * 129], lhsT=pTc[:, mc, h, :], rhs=mv_aug[:, mc, h, :],
                            start=(mc == 0), stop=(mc == 1)), r=["pTc%d" % mc, "mv_aug"], w=[bko])
                bo3 = bo[:, 0:258].rearrange("p (j d) -> p j d", d=129)
                dk = "denc%d" % hp2
                P.op("dve", lambda e, bo3=bo3, hp2=hp2: e.reciprocal(out=denc[:, hp2 * 2:(hp2 + 1) * 2], in_=bo3[:, :, 128]),
                     r=[bko], w=[dk])
                P.op("dve", lambda e, bo3=bo3, hp2=hp2: e.tensor_tensor(
                    out=br[:, 2, hp2 * 256:(hp2 + 1) * 256].rearrange("p (j d) -> p j d", d=128),
                    in0=bo3[:, :, 0:128], in1=bcast(denc[:, hp2 * 2:(hp2 + 1) * 2], 2, [128, 2, 128]), op=ALU.mult),
                    r=[bko, dk], w=["br2"])
            if A1_STAGE[0] == 3:
                return
            pb, pbk = G["psb"].get()
            for blk in range(2):
                P.op("pe", lambda e, blk=blk: e.transpose(out=pb[:, blk * 128:(blk + 1) * 128], in_=rot[:, 2 + blk, :],
                                                          identity=ident[:]), r=["rot", "ident"], w=[pbk])
            P.op("act", lambda e: e.copy(out=ktok[:], in_=pb[:, 0:256]), r=[pbk], w=["ktok"])
            bi, bki = G["psA"].get()
            for h in range(4):
                blk, half = h // 2, h % 2
                P.op("pe", lambda e, h=h, blk=blk, half=half: e.matmul(
                    bi[:, h * 128:(h + 1) * 128], lhsT=kpad[:, blk, half, :],
                    rhs=rot[:, blk, :], start=True, stop=True), r=["rot", "kpad"], w=[bki])
            P.op("dve", lambda e: e.tensor_tensor(out=innerTm[:], in0=bi[:].rearrange("p (a b) -> p a b", b=128),
                                                  in1=dtab[:], op=ALU.mult), r=[bki, "dtab"], w=["innerTm"])
            bo, bko = G["psA"].get()
            for h in range(4):
                blk, half = h // 2, h % 2
                P.op("pe", lambda e, h=h: e.matmul(
                    bo[:, h * 128:(h + 1) * 128], lhsT=innerTm[:, h, :], rhs=vret[:, h, :],
                    start=True, stop=(t == 0)), r=["innerTm", "vret"], w=[bko])
                if t > 0:
                    P.op("pe", lambda e, h=h, blk=blk, half=half: e.matmul(
                        bo[:, h * 128:(h + 1) * 128], lhsT=qspad[:, blk, half, :],
                        rhs=state_bf[:, blk, :], start=False, stop=True),
                        r=["qspad", "state_bf"], w=[bko])
            bkv, bkkv = G["psA"].get()
            for h in range(4):
                blk, half = h // 2, h % 2
                P.op("pe", lambda e, h=h, blk=blk, half=half: e.matmul(
                    bkv[:, h * 128:(h + 1) * 128], lhsT=ktok[:, blk * 128:(blk + 1) * 128],
                    rhs=vw[:, h, :], start=True, stop=True), r=["ktok", "vw"], w=[bkkv])
            for h in range(4):
                blk, half = h // 2, h % 2
                rows = slice(half * 64, (half + 1) * 64)
                P.op("dve", lambda e, h=h, blk=blk, rows=rows: e.scalar_tensor_tensor(
                    out=state[rows, blk, :], in0=state[rows, blk, :], scalar=cdt[rows, blk:blk + 1],
                    in1=bkv[rows, h * 128:(h + 1) * 128], op0=ALU.mult, op1=ALU.add),
                    r=["state", "cdt", bkkv], w=["state"])
            P.op("act", lambda e: e.copy(out=state_bf[:], in_=state[:]), r=["state"], w=["state_bf"])
            for h in range(4):
                P.op("dve", lambda e, h=h: e.bn_stats(out=gst[:, h * 6:(h + 1) * 6], in_=bo[:, h * 128:(h + 1) * 128]),
                     r=[bko], w=["gst%d" % h])
                P.op("dve", lambda e, h=h: e.bn_aggr(out=gmv[:, h, :], in_=gst[:, h * 6:(h + 1) * 6]),
                     r=["gst%d" % h], w=["gmv"])
            P.op("act", lambda e: e.activation(out=grs[:], in_=gmv[:, :, 1], func=AF.Sqrt, bias=EPS_T[0][:], scale=1.0),
                 r=["gmv", "eps"], w=["grs"])
            P.op("dve", lambda e: e.reciprocal(out=grs[:], in_=grs[:]), r=["grs"], w=["grs"])
            for h in range(4):
                P.op("dve", lambda e, h=h: e.tensor_scalar(
                    out=xn[:, h * 128:(h + 1) * 128], in0=bo[:, h * 128:(h + 1) * 128], scalar1=gmv[:, h, 0:1],
                    scalar2=grs[:, h:h + 1], op0=ALU.subtract, op1=ALU.mult), r=[bko, "gmv", "grs"], w=["xn"])
            P.op("pool", lambda e: e.tensor_tensor(out=br[:, 1, :], in0=xn[:], in1=sg[:], op=ALU.mult),
                 r=["xn", "sg"], w=["br1"])
            if A1_STAGE[0] == 4:
                return
            pba, pbka = G["psb"].get()
            for b in range(2):
                for kc in range(4):
                    P.op("pe", lambda e, b=b, kc=kc: e.transpose(
                        out=pba[:, (b * 4 + kc) * 128:(b * 4 + kc + 1) * 128], in_=br[:, b, kc * 128:(kc + 1) * 128],
                        identity=ident[:]), r=["br%d" % b, "ident"], w=[pbka])
            bt = "brT%d" % s
            P.op("act", lambda e: e.copy(out=brT[s][:, 0:8, :].rearrange("p a b -> p (a b)"), in_=pba[:]),
                 r=[pbka], w=[bt])
            pbc, pbkc = G["psb"].get()
            for kc in range(4):
                P.op("pe", lambda e, kc=kc: e.transpose(out=pbc[:, kc * 128:(kc + 1) * 128],
                                                        in_=br[:, 2, kc * 128:(kc + 1) * 128], identity=ident[:]),
                     r=["br2", "ident"], w=[pbkc])
            P.op("dve", lambda e: e.tensor_copy(out=brT[s][:, 8:12, :].rearrange("p a b -> p (a b)"), in_=pbc[:, 0:512]),
                 r=[pbkc], w=[bt])
            P.op("sp", lambda e: e.dma_start(out=brT_scr[t, :, :], in_=brT[s][:].rearrange("p a b -> p (a b)")),
                 r=[bt], dma="a1s%d" % s)

        loads(0)
        for t in range(n_tiles):
            do_tile(t)
        P.emit()


def phase_a2(nc, sync, G, n_tiles, x_d, xT_d, wg_d, bg_d, wbr_d, wout_d, ln1g_d, ln1b_d, brT_scr, h_scr):
    with ExitStack() as es:
        def sb(name, shape, dt):
            return es.enter_context(nc.sbuf_tensor("a2_" + name, shape, dt))
        ident = G["ident"]
        wg = sb("wg", [128, 8, NCOL_G], BF16)
        bg = sb("bg", [1, NCOL_G], BF16)
        ones = sb("ones", [1, 128], BF16)
        wbr = sb("wbr", [128, 12, D], BF16)
        wout = sb("wout", [128, 8, D], BF16)
        g1 = sb("g1", [128, D], F32)
        b1 = sb("b1", [128, D], F32)
        xT = [sb("xT%d" % i, [128, 8, 128], BF16) for i in range(2)]
        xt = [sb("x%d" % i, [128, D], F32) for i in range(2)]
        brT = [sb("brT%d" % i, [128, 12, 128], BF16) for i in range(2)]
        gsb = [sb("gsb%d" % i, [128, 512], F32) for i in range(2)]
        acc = sb("acc", [128, D], F32)
        tmp = [sb("tmp%d" % i, [128, 512], F32) for i in range(2)]
        mbf = sb("mbf", [128, D], BF16)
        mT = sb("mT", [128, 8, 128], BF16)
        z = [sb("z%d" % i, [128, D], F32) for i in range(2)]
        st6 = sb("st6", [128, 12], F32)
        mv2 = sb("mv2", [128, 2], F32)
        rstd = sb("rstd", [128, 1], F32)
        nmr = sb("nmr", [128, 1], F32)

        P = Prog(nc, sync)
        for c in range(8):
            P.op("pool", lambda e, c=c: e.dma_start(out=wg[:, c, :], in_=wg_d[c * 128:(c + 1) * 128, :],
                                                    max_dma_last_dim=4096), w=["wg"], dma="a2w")
        P.op("pool", lambda e: e.dma_start(out=bg[:], in_=bg_d[:, :], max_dma_last_dim=4096), w=["bg"], dma="a2w")
        for c in range(12):
            P.op("pool", lambda e, c=c: e.dma_start(out=wbr[:, c, :], in_=wbr_d[c * 128:(c + 1) * 128, :]),
                 w=["wbr"], dma="a2w")
        for c in range(8):
            P.op("pool", lambda e, c=c: e.dma_start(out=wout[:, c, :], in_=wout_d[c * 128:(c + 1) * 128, :]),
                 w=["wout"], dma="a2w")
        P.op("sp", lambda e: e.dma_start(out=g1[:], in_=ln1g_d[:, :]), w=["g1"], dma="a2p")
        P.op("sp", lambda e: e.dma_start(out=b1[:], in_=ln1b_d[:, :]), w=["b1"], dma="a2p")
        P.op("dve", lambda e: e.memset(ones[:], 1.0), w=["ones"])
        cnt = dict(g=0)

        def loads(t):
            s = t % 2
            P.op("pool", lambda e: e.dma_start(
                out=xT[s][:], in_=xT_d.rearrange("(c p) t -> p c t", p=128)[:, :, t * 128:(t + 1) * 128]),
                w=["xT%d" % s], dma="a2x%d" % s)
            P.op("sp", lambda e: e.dma_start(out=xt[s][:], in_=x_d[t * 128:(t + 1) * 128, :]),
                 w=["x%d" % s], dma="a2l%d" % s)
            P.op("sp", lambda e: e.dma_start(out=brT[s][:].rearrange("p a b -> p (a b)"), in_=brT_scr[t, :, :]),
                 w=["brT%d" % s], dma="a2l%d" % s)

        def do_tile(t):
            s = t % 2
            if t + 1 < n_tiles:
                loads(t + 1)
            xk, xtk, btk = "xT%d" % s, "x%d" % s, "brT%d" % s
            for b in range(3):
                for half in range(2):
                    col0 = b * 1024 + half * 512
                    gi = cnt["g"] % 2
                    cnt["g"] += 1
                    bgk, bkg = G["psA"].get()
                    for c in range(8):
                        P.op("pe", lambda e, c=c, col0=col0, bgk=bgk: e.matmul(
                            bgk[:], lhsT=xT[s][:, c, :], rhs=wg[:, c, col0:col0 + 512], start=(c == 0), stop=False),
                            r=["wg", xk], w=[bkg])
                    P.op("pe", lambda e, col0=col0, bgk=bgk: e.matmul(
                        bgk[:], lhsT=ones[0:1, :], rhs=bg[0:1, col0:col0 + 512], start=False, stop=True),
                        r=["bg", "ones"], w=[bkg])
                    P.op("act", lambda e, gi=gi, bgk=bgk: e.activation(out=gsb[gi][:], in_=bgk[:], func=AF.Sigmoid),
                         r=[bkg], w=["gsb%d" % gi])
                    by, bky = G["psA"].get()
                    for kc in range(4):
                        P.op("pe", lambda e, kc=kc, b=b, half=half, by=by: e.matmul(
                            by[:], lhsT=brT[s][:, b * 4 + kc, :], rhs=wbr[:, b * 4 + kc, half * 512:(half + 1) * 512],
                            start=(kc == 0), stop=(kc == 3)), r=["wbr", btk], w=[bky])
                    ak = "acc%d" % half
                    if b == 0:
                        P.op("dve", lambda e, gi=gi, half=half, by=by: e.tensor_tensor(
                            out=acc[:, half * 512:(half + 1) * 512], in0=by[:], in1=gsb[gi][:], op=ALU.mult),
                            r=[bky, "gsb%d" % gi], w=[ak])
                    else:
                        P.op("dve", lambda e, gi=gi, by=by: e.tensor_tensor(
                            out=tmp[gi][:], in0=by[:], in1=gsb[gi][:], op=ALU.mult),
                            r=[bky, "gsb%d" % gi], w=["tmp%d" % gi])
                        if b == 1:
                            P.op("pool", lambda e, gi=gi, half=half: e.tensor_tensor(
                                out=acc[:, half * 512:(half + 1) * 512], in0=acc[:, half * 512:(half + 1) * 512],
                                in1=tmp[gi][:], op=ALU.add), r=[ak, "tmp%d" % gi], w=[ak])
                        else:
                            P.op("pool", lambda e, gi=gi, half=half: e.tensor_tensor(
                                out=mbf[:, half * 512:(half + 1) * 512], in0=acc[:, half * 512:(half + 1) * 512],
                                in1=tmp[gi][:], op=ALU.add), r=[ak, "tmp%d" % gi], w=["mbf"])
            pb, pbk = G["psb"].get()
            for c in range(8):
                P.op("pe", lambda e, c=c: e.transpose(out=pb[:, c * 128:(c + 1) * 128], in_=mbf[:, c * 128:(c + 1) * 128],
                                                      identity=ident[:]), r=["mbf", "ident"], w=[pbk])
            P.op("act", lambda e: e.copy(out=mT[:].rearrange("p a b -> p (a b)"), in_=pb[:]), r=[pbk], w=["mT"])
            zk = "z%d" % s
            for half in range(2):
                bz, bkz = G["psA"].get()
                for c in range(8):
                    P.op("pe", lambda e, c=c, half=half, bz=bz: e.matmul(
                        bz[:], lhsT=mT[:, c, :], rhs=wout[:, c, half * 512:(half + 1) * 512],
                        start=(c == 0), stop=(c == 7)), r=["mT", "wout"], w=[bkz])
                P.op("dve", lambda e, half=half, bz=bz: e.scalar_tensor_tensor(
                    out=z[s][:, half * 512:(half + 1) * 512], in0=xt[s][:, half * 512:(half + 1) * 512],
                    scalar=ALPHA, in1=bz[:], op0=ALU.mult, op1=ALU.add), r=[xtk, bkz], w=[zk])
            layer_norm_tail(P, z[s], zk, g1, "g1", b1, "b1", st6, mv2, rstd, nmr, "a2", gb_eng="pool")
            P.op("sp", lambda e: e.dma_start(out=h_scr[t * 128:(t + 1) * 128, :], in_=z[s][:]), r=[zk],
                 dma="a2s%d" % s)

        loads(0)
        for t in range(n_tiles):
            do_tile(t)
        P.emit()


IN_SPECS = [
    ("x", lambda T: [T, D], F32), ("xT", lambda T: [D, T], F32), ("mem", lambda T: [256, D], F32),
    ("memg", lambda T: [128, D], F32), ("memb", lambda T: [128, D], F32), ("wkv", lambda T: [D, D], F32),
    ("w1", lambda T: [D, NCOL_A1], F32), ("b1", lambda T: [1, NCOL_A1], F32),
    ("cc", lambda T: [128, T], F32), ("ss", lambda T: [128, T], F32),
    ("dtab", lambda T: [128, 512], F32), ("wqt", lambda T: [128, 256], F32), ("wkt", lambda T: [128, 4], F32),
    ("cdt", lambda T: [128, 2], F32), ("mask", lambda T: [128, 256], F32), ("gng", lambda T: [128, 512], F32),
    ("sink", lambda T: [128, 8], F32),
    ("wg", lambda T: [D, NCOL_G], F32), ("bg", lambda T: [1, NCOL_G], F32), ("wbr", lambda T: [1536, D], F32),
    ("wout", lambda T: [D, D], F32), ("ln1g", lambda T: [128, D], F32), ("ln1b", lambda T: [128, D], F32),
    ("wpqT", lambda T: [2048, D], F32), ("skT", lambda T: [2048, 128], F32),
    ("peer_u", lambda T: [N_EXP, D], F32), ("peer_v", lambda T: [N_EXP, D], F32),
    ("ln2g", lambda T: [128, D], F32), ("ln2b", lambda T: [128, D], F32),
    ("identf", lambda T: [128, 128], F32), ("iota16", lambda T: [128, 16], F32),
]


def build_full(n_tiles, debug_h=False, stop_after=None):
    nc = bass.Bass("TRN2", target_bir_lowering=False)
    T = n_tiles * 128
    d = {}
    for name, shp, dtp in IN_SPECS:
        d[name] = nc.dram_tensor(name, shp(T), dtp, kind="ExternalInput").ap()
    y_d = nc.dram_tensor("y", [T, D], F32, kind="ExternalOutput").ap()
    h_scr = nc.dram_tensor("h_scr", [T, D], F32, kind="ExternalOutput" if debug_h else "Internal").ap()
    brT_scr = nc.dram_tensor("brT_scr", [n_tiles, 128, 1536], BF16,
                             kind="ExternalOutput" if debug_h else "Internal").ap()
    with ExitStack() as es:
        sync = Sync(nc, es)
        G = alloc_globals(nc, es, sync)
        G["psA"] = PsumPool(G["psf"].t + G["pv"], "psA")
        phase_const(nc, sync, G, d["identf"], d["iota16"])
        with ExitStack() as es1:
            mkT = es1.enter_context(nc.sbuf_tensor("sb_mkT", [128, 4, 256], BF16))
            mv_aug = es1.enter_context(nc.sbuf_tensor("sb_mvaug", [128, 2, 4, 129], BF16))
            phase_a0(nc, sync, G, mkT, mv_aug, d["mem"], d["memg"], d["memb"], d["wkv"])
            if stop_after == "a0":
                return nc
            phase_a1(nc, sync, G, n_tiles, mkT, mv_aug, d["xT"], d["w1"], d["b1"], d["cc"], d["ss"], d["dtab"],
                     d["wqt"], d["wkt"], d["cdt"], d["mask"], d["gng"], d["sink"], brT_scr)
        if stop_after == "a1":
            return nc
        phase_a2(nc, sync, G, n_tiles, d["x"], d["xT"], d["wg"], d["bg"], d["wbr"], d["wout"], d["ln1g"],
                 d["ln1b"], brT_scr, h_scr)
        if stop_after == "a2":
            return nc
        uv_scr = nc.dram_tensor("uv_scr", [N_EXP, 2 * D], BF16, kind="Internal").ap()
        phase_bt(nc, sync, G, d["peer_u"], d["peer_v"], uv_scr)
        with ExitStack() as es2:
            weff = es2.enter_context(nc.sbuf_tensor("sb_weff", [128, 8, 2048], BF16))
            phase_b0(nc, sync, G, weff, d["wpqT"], d["skT"])
            phase_b(nc, sync, G, n_tiles, weff, h_scr, uv_scr, d["ln2g"], d["ln2b"], y_d)
    return nc


def _w_in_cols():
    fm = list(range(0, 512))
    fm += list(range(512, 576)) * 2 + list(range(576, 640)) * 2
    rq0, rk0 = 768, 1024
    fm += list(range(rq0, rq0 + 256)) + list(range(rk0, rk0 + 256))
    sw = []
    for base in (rq0, rk0):
        for h in range(4):
            hb = base + 64 * h
            sw += list(range(hb + 32, hb + 64)) + list(range(hb, hb + 32))
    fm += sw
    fm += list(range(2304, 2816))
    tm = list(range(640, 768)) + list(range(1280, 1792)) + list(range(1792, 2304))
    return np.array(fm + tm), np.arange(2816, 5888)


def _const_tables(T):
    half = 32
    theta = (1.0 / np.power(np.float32(10000.0), np.linspace(0.0, 1.0, half, dtype=np.float32))).astype(np.float32)
    pos = np.arange(T, dtype=np.float32)
    ang = (pos[:, None] * theta[None, :]).astype(np.float32)
    cos, sin = np.cos(ang).astype(np.float32), np.sin(ang).astype(np.float32)
    p = np.arange(128)
    cc = np.ascontiguousarray(cos[:, p % 32].T)
    sgn = np.where((p % 64) < 32, -1.0, 1.0).astype(np.float32)
    ss = np.ascontiguousarray((sin[:, p % 32] * sgn[None, :]).T)
    lg = np.log(1.0 - 2.0 ** (-5.0 - np.arange(4, dtype=np.float64)))
    i = np.arange(128, dtype=np.float64)
    diff = i[None, :] - i[:, None]
    dt = np.zeros((128, 4, 128), np.float64)
    for h in range(4):
        dt[:, h, :] = np.where(diff >= 0, np.exp(lg[h] * np.maximum(diff, 0.0)), 0.0) * 0.125
    wq = np.zeros((128, 2, 128), np.float64)
    cd = np.zeros((128, 2), np.float64)
    for blk in range(2):
        for hf in range(2):
            h = blk * 2 + hf
            wq[hf * 64:(hf + 1) * 64, blk, :] = np.exp(lg[h] * (i + 1.0))[None, :]
            cd[hf * 64:(hf + 1) * 64, blk] = np.exp(lg[h] * 128.0)
    wk = np.zeros((128, 4), np.float64)
    for h in range(4):
        wk[:, h] = np.exp(lg[h] * (127.0 - i)) * 0.125
    k = np.arange(128)[:, None]
    q = np.arange(128)[None, :]
    mask = np.concatenate([(k <= q), (k > q)], axis=1).astype(np.float32)
    f = lambda a: np.ascontiguousarray(a.astype(np.float32))
    return dict(cc=cc, ss=ss, dtab=f(dt.reshape(128, 512)), wqt=f(wq.reshape(128, 256)), wkt=f(wk), cdt=f(cd),
                mask=mask, identf=np.eye(128, dtype=np.float32),
                iota16=f(np.broadcast_to(np.arange(16.0), (128, 16))))


def _rep(v, n=128):
    return np.ascontiguousarray(np.broadcast_to(np.asarray(v, np.float32)[None, :], (n, v.shape[-1])))


def host_prep(inputs, T):
    g = lambda n: np.asarray(inputs[n], np.float32)[0]
    c1, cg = _w_in_cols()
    w_in, b_in = g("w_in"), g("b_in")
    sh = dict(
        memg=_rep(g("mem_ln_g")), memb=_rep(g("mem_ln_b")), wkv=g("w_mem_kv"),
        w1=np.ascontiguousarray(w_in[:, c1]), b1=np.ascontiguousarray(b_in[c1][None, :]),
        gng=_rep(g("ret_gn_g")), sink=_rep(g("attn_sinks")),
        wg=np.ascontiguousarray(w_in[:, cg]), bg=np.ascontiguousarray(b_in[cg][None, :]),
        wbr=np.ascontiguousarray(np.concatenate([g("w_branch_attn"), g("w_branch_ret"), g("w_branch_mem")], axis=0)),
        wout=g("w_out"), ln1g=_rep(g("ln1_g")), ln1b=_rep(g("ln1_b")),
        wpqT=np.ascontiguousarray(g("w_peer_q").T),
        skT=np.ascontiguousarray(g("peer_sub_keys").transpose(0, 1, 3, 2).reshape(2048, 128)),
        peer_u=g("peer_u"), peer_v=g("peer_v"), ln2g=_rep(g("ln2_g")), ln2b=_rep(g("ln2_b")),
    )
    sh.update(_const_tables(T))
    return sh


def kernel(**inputs):
    x = np.asarray(inputs["x"], np.float32)
    mem = np.asarray(inputs["mem"], np.float32)
    B, S, _ = x.shape
    n_tiles = S // 128
    sh = host_prep(inputs, S)
    in_maps = []
    for b in range(B):
        m = dict(sh)
        m["x"] = np.ascontiguousarray(x[b])
        m["xT"] = np.ascontiguousarray(x[b].T)
        m["mem"] = np.ascontiguousarray(mem[b])
        in_maps.append(m)
    nc = build_full(n_tiles, stop_after=_os.environ.get('MK_STOP'))
    res = run_bass_kernel_spmd(nc, in_maps, core_ids=list(range(B)))
    return np.stack([r["y"] for r in res.results], axis=0).astype(np.float32)
```

```python
import numpy as np
import ml_dtypes
from contextlib import ExitStack

import concourse.bass as bass
import concourse.mybir as mybir
from concourse.bass_utils import run_bass_kernel_spmd

F32 = mybir.dt.float32
BF16 = mybir.dt.bfloat16
I32 = mybir.dt.int32
U32 = mybir.dt.uint32
AF = mybir.ActivationFunctionType
ALU = mybir.AluOpType
AX = mybir.AxisListType

D = 1024
SEQ = 8192
NCORES = 8
ALPHA = 2.0 ** 0.25
LN_EPS = 1e-5
N_EXP = 16384

FM_COLS = 18 * 128
TM_A1 = 128 + 512 + 512
NCOL_A1 = FM_COLS + TM_A1
NCOL_G = 3072

SAME_ENGINE_SYNC = True
EPS_T = [None]
import os as _os
A1_STAGE = [int(_os.environ.get('A1_STAGE', '0'))]


class Sync:
    def __init__(self, nc, es):
        self.nc = nc
        self.es = es
        self.sems = {}
        self.cnt = {}

    def sem(self, name):
        if name not in self.sems:
            self.sems[name] = self.es.enter_context(self.nc.semaphore("s_" + name))
            self.cnt[name] = 0
        return self.sems[name]


ENGS = ("pe", "act", "dve", "pool", "sp")


class Prog:
    def __init__(self, nc, sync):
        self.nc = nc
        self.sync = sync
        self.ops = []
        self.lastw = {}
        self.readers = {}

    def op(self, eng, fn, r=(), w=(), dma=None):
        i = len(self.ops)
        deps = set()
        for k in r:
            if k in self.lastw:
                deps.add(self.lastw[k])
        for k in w:
            if k in self.lastw:
                deps.add(self.lastw[k])
            deps.update(self.readers.get(k, ()))
        for k in r:
            self.readers.setdefault(k, []).append(i)
        for k in w:
            self.lastw[k] = i
            self.readers[k] = []
        deps.discard(i)
        self.ops.append(dict(eng=eng, fn=fn, deps=deps, dma=dma))
        return i

    def emit(self):
        nc, sync, ops = self.nc, self.sync, self.ops
        n = len(ops)
        per_eng = {e: [] for e in ENGS}
        for i, o in enumerate(ops):
            per_eng[o["eng"]].append(i)
        need = [False] * n
        for i, o in enumerate(ops):
            for d in o["deps"]:
                od = ops[d]
                if od["dma"] is not None:
                    continue
                if od["eng"] == o["eng"] and o["dma"] is None:
                    if o["eng"] == "pe" or not SAME_ENGINE_SYNC:
                        continue
                need[d] = True
        for e in ENGS:
            for i in reversed(per_eng[e]):
                if ops[i]["dma"] is None:
                    need[i] = True
                    break
        mark = [None] * n
        for i, o in enumerate(ops):
            if o["dma"] is not None:
                nm = "d_" + o["dma"]
                sync.sem(nm)
                sync.cnt[nm] += 16
                mark[i] = (nm, sync.cnt[nm])
            elif need[i]:
                nm = "e_" + o["eng"]
                sync.sem(nm)
                sync.cnt[nm] += 1
                mark[i] = (nm, sync.cnt[nm])
        final = {nm: sync.cnt[nm] for nm in sync.cnt}
        waits = [None] * n
        for i, o in enumerate(ops):
            wl = {}
            for d in o["deps"]:
                od = ops[d]
                if od["dma"] is None and od["eng"] == o["eng"] and o["dma"] is None:
                    if o["eng"] == "pe" or not SAME_ENGINE_SYNC:
                        continue
                nm, v = mark[d]
                wl[nm] = max(wl.get(nm, 0), v)
            waits[i] = wl
        with nc.Block() as block:
            decos = {"pe": block.tensor, "act": block.scalar, "dve": block.vector,
                     "pool": block.gpsimd, "sp": block.sync}
            for eng in ENGS:
                idxs = per_eng[eng]

                def body(e, idxs=idxs):
                    waited = {}
                    for i in idxs:
                        o = ops[i]
                        for nm, v in waits[i].items():
                            if waited.get(nm, 0) >= v:
                                continue
                            e.wait_ge(sync.sems[nm], v)
                            waited[nm] = v
                        ins = o["fn"](e)
                        if mark[i] is not None:
                            nm, v = mark[i]
                            ins.then_inc(sync.sems[nm], 16 if o["dma"] is not None else 1)
                    for nm, v in final.items():
                        if v > 0 and waited.get(nm, 0) < v:
                            e.wait_ge(sync.sems[nm], v)

                decos[eng](body)


class PsumPool:
    def __init__(self, tensors, prefix):
        self.t = tensors
        self.prefix = prefix
        self.i = 0

    def get(self):
        k = self.i % len(self.t)
        self.i += 1
        return self.t[k], "%s%d" % (self.prefix, k)


def bcast(ap, axis, shape):
    return ap.unsqueeze(axis).to_broadcast(list(shape))


def phase_b0(nc, sync, G, weff, wpqT_d, skT_d):
    with ExitStack() as es:
        wq = es.enter_context(nc.sbuf_tensor("b0_wq", [128, 16, 1024], F32))
        sk = es.enter_context(nc.sbuf_tensor("b0_sk", [128, 16, 128], F32))
        P = Prog(nc, sync)
        for q in range(4):
            P.op("sp", lambda e, q=q: e.dma_start(
                out=wq[:, q * 4:(q + 1) * 4, :],
                in_=wpqT_d.rearrange("(g p) m -> p g m", p=128)[:, q * 4:(q + 1) * 4, :]),
                w=["wq"], dma="b0w")
        P.op("sp", lambda e: e.dma_start(out=sk[:], in_=skT_d.rearrange("(g p) n -> p g n", p=128)),
             w=["sk"], dma="b0w")
        for mc in range(8):
            for q in range(4):
                bank, bk = G["psf"].get()
                for j in range(4):
                    hp = q * 4 + j
                    P.op("pe", lambda e, bank=bank, hp=hp, mc=mc, j=j: e.matmul(
                        bank[:, j * 128:(j + 1) * 128], lhsT=wq[:, hp, mc * 128:(mc + 1) * 128],
                        rhs=sk[:, hp, :], start=True, stop=True), r=["wq", "sk"], w=[bk])
                eng = "act" if (mc * 4 + q) % 2 == 0 else "dve"
                if eng == "act":
                    P.op("act", lambda e, bank=bank, mc=mc, q=q: e.copy(
                        out=weff[:, mc, q * 512:(q + 1) * 512], in_=bank[:]), r=[bk], w=["weff"])
                else:
                    P.op("dve", lambda e, bank=bank, mc=mc, q=q: e.tensor_copy(
                        out=weff[:, mc, q * 512:(q + 1) * 512], in_=bank[:]), r=[bk], w=["weff"])
        P.emit()


def phase_bt(nc, sync, G, peer_u, peer_v, uv_scr):
    with ExitStack() as es:
        uv = [es.enter_context(nc.sbuf_tensor("bt_uv%d" % i, [128, 8, 2 * D], BF16)) for i in range(2)]
        P = Prog(nc, sync)
        for ch in range(16):
            s = ch % 2
            k = "uv%d" % s
            rows = slice(ch * 1024, (ch + 1) * 1024)
            P.op("pool", lambda e, s=s, rows=rows: e.dma_start(
                out=uv[s][:, :, 0:D], in_=peer_u[rows, :].rearrange("(p r) d -> p r d", r=8)), w=[k], dma="btl%d" % s)
            P.op("pool", lambda e, s=s, rows=rows: e.dma_start(
                out=uv[s][:, :, D:2 * D], in_=peer_v[rows, :].rearrange("(p r) d -> p r d", r=8)), w=[k], dma="btl%d" % s)
            P.op("sp", lambda e, s=s, rows=rows: e.dma_start(
                out=uv_scr[rows, :].rearrange("(p r) d -> p r d", r=8), in_=uv[s][:]), r=[k], dma="bts%d" % s)
        P.emit()


def phase_b(nc, sync, G, n_tiles, weff, h_scr, uv_scr, ln2g_d, ln2b_d, y_d, NS=16, ND=4, GS=4, NP=4,
            DSPLIT=int(_os.environ.get('DSPLIT', '0'))):
    with ExitStack() as es:
        def sb(name, shape, dt):
            return es.enter_context(nc.sbuf_tensor("b_" + name, shape, dt))
        ident, identf, iota16 = G["ident"], G["identf"], G["iota16"]
        g2 = sb("g2", [128, D], F32)
        b2 = sb("b2", [128, D], F32)
        h_t = [sb("h%d" % i, [128, D], F32) for i in range(2)]
        hb = [sb("hb%d" % i, [128, D], BF16) for i in range(2)]
        prod = [sb("prod%d" % i, [128, D], BF16) for i in range(NP)]
        junk2 = sb("junk2", [128, D], BF16)
        hT = sb("hT", [128, 8, 128], BF16)
        sc = sb("sc", [128, 16, 128], F32)
        sv = sb("sv", [128, 16, 16], F32)
        si = sb("si", [128, 16, 16], U32)
        sif = sb("sif", [128, 16, 16], F32)
        cand = sb("cand", [128, 8, 256], F32)
        ts = sb("ts", [128, 8, 16], F32)
        pos = sb("pos", [128, 8, 16], U32)
        pij = sb("pij", [128, 2, 128], U32)
        pijf = sb("pijf", [128, 2, 8, 16], F32)
        oh = [sb("oh%d" % i, [128, 8, 16, 16], F32) for i in range(2)]
        ee = sb("ee", [128, 2, 128], F32)
        ef = sb("ef", [128, 128], F32)
        eidx = [sb("eidx%d" % i, [128, 128], I32) for i in range(2)]
        dsm = sb("dsm", [128, 8, 16], F32)
        ex = sb("ex", [128, 8, 16], F32)
        ssum = sb("ssum", [128, 8], F32)
        gate = [sb("gate%d" % i, [128, 128], F32) for i in range(2)]
        dots = sb("dots", [128, 128], F32)
        actt = sb("actt", [128, 128], F32)
        wt = sb("wt", [128, 128], F32)
        junk = sb("junk", [128, D], BF16)
        uvb = [sb("uvb%d" % i, [128, 2 * D], BF16) for i in range(NS)]
        dg = [sb("dg%d" % i, [128, 128], BF16) for i in range(ND)]
        y_t = [sb("y%d" % i, [128, D], F32) for i in range(2)]
        st6 = sb("st6", [128, 12], F32)
        mv2 = sb("mv2", [128, 2], F32)
        rstd = sb("rstd", [128, 1], F32)
        nmr = sb("nmr", [128, 1], F32)
        pv = G["pv"]
        NG = 128 // GS

        P = Prog(nc, sync)
        P.op("sp", lambda e: e.dma_start(out=g2[:], in_=ln2g_d[:, :]), w=["g2"], dma="bw")
        P.op("sp", lambda e: e.dma_start(out=b2[:], in_=ln2b_d[:, :]), w=["b2"], dma="bw")
        cnt = dict(u=0, d=0, p=0, q=0)
        scpool = PsumPool(G["psf"].t[0:2], "psf")
        dq = [(G["psf"].t[2 + i // 4], (i % 4) * 128) for i in range(8)]

        def load_h(t):
            s = t % 2
            P.op("sp", lambda e: e.dma_start(out=h_t[s][:], in_=h_scr[t * 128:(t + 1) * 128, :]),
                 w=["h%d" % s], dma="hld%d" % s)

        def front(t, OP=None):
            OP = OP or P.op
            s = t % 2
            hk_ = "h%d" % s
            OP("act", lambda e: e.copy(out=hb[s][:], in_=h_t[s][:]), r=[hk_], w=["hb%d" % s])
            pb, pbk = G["psb"].get()
            for c in range(8):
                OP("pe", lambda e, c=c: e.transpose(out=pb[:, c * 128:(c + 1) * 128],
                                                       in_=hb[s][:, c * 128:(c + 1) * 128], identity=ident[:]),
                     r=["hb%d" % s, "ident"], w=[pbk])
            OP("act", lambda e: e.copy(out=hT[:].rearrange("p c t -> p (c t)"), in_=pb[:]), r=[pbk], w=["hT"])
            for nb in range(4):
                bank, bk = scpool.get()
                for c in range(8):
                    OP("pe", lambda e, c=c, nb=nb, bank=bank: e.matmul(
                        bank[:], lhsT=hT[:, c, :], rhs=weff[:, c, nb * 512:(nb + 1) * 512],
                        start=(c == 0), stop=(c == 7)), r=["hT", "weff"], w=[bk])
                for q in range(4):
                    g = nb * 4 + q
                    OP("act", lambda e, q=q, g=g, bank=bank: e.copy(out=sc[:, g, :], in_=bank[:, q * 128:(q + 1) * 128]),
                         r=[bk], w=["sc%d" % g])
            for g in range(16):
                OP("dve", lambda e, g=g: e.max(out=sv[:, g, 0:8], in_=sc[:, g, :]), r=["sc%d" % g], w=["sva%d" % g])
            for g in range(16):
                OP("dve", lambda e, g=g: e.max_index(out=si[:, g, 0:8], in_max=sv[:, g, 0:8], in_values=sc[:, g, :]),
                     r=["sc%d" % g, "sva%d" % g], w=["sia%d" % g])
            for g in range(16):
                OP("dve", lambda e, g=g: e.match_replace(out=sc[:, g, :], in_to_replace=sv[:, g, 0:8],
                                                           in_values=sc[:, g, :], imm_value=-1e30),
                     r=["sva%d" % g], w=["sc%d" % g])
            for g in range(16):
                OP("dve", lambda e, g=g: e.max(out=sv[:, g, 8:16], in_=sc[:, g, :]), r=["sc%d" % g], w=["svb%d" % g])
            for g in range(16):
                OP("dve", lambda e, g=g: e.max_index(out=si[:, g, 8:16], in_max=sv[:, g, 8:16], in_values=sc[:, g, :]),
                     r=["sc%d" % g, "svb%d" % g], w=["sib%d" % g])
            allsv = ["sva%d" % g for g in range(16)] + ["svb%d" % g for g in range(16)]
            allsi = ["sia%d" % g for g in range(16)] + ["sib%d" % g for g in range(16)]
            OP("dve", lambda e: e.tensor_copy(out=sif[:], in_=si[:]), r=allsi, w=["sif"])
            sv4 = sv[:].rearrange("p (h two) k -> p h two k", two=2)
            OP("dve", lambda e: e.tensor_tensor(
                out=cand[:].rearrange("p h (i j) -> p h i j", j=16),
                in0=bcast(sv4[:, :, 0, :], 3, [128, 8, 16, 16]),
                in1=bcast(sv4[:, :, 1, :], 2, [128, 8, 16, 16]), op=ALU.add), r=allsv,
                w=["cand%d" % h for h in range(8)])
            for h in range(8):
                OP("dve", lambda e, h=h: e.max(out=ts[:, h, 0:8], in_=cand[:, h, :]), r=["cand%d" % h], w=["tsa%d" % h])
            for h in range(8):
                OP("dve", lambda e, h=h: e.max_index(out=pos[:, h, 0:8], in_max=ts[:, h, 0:8], in_values=cand[:, h, :]),
                     r=["cand%d" % h, "tsa%d" % h], w=["posa%d" % h])
            for h in range(8):
                OP("dve", lambda e, h=h: e.match_replace(out=cand[:, h, :], in_to_replace=ts[:, h, 0:8],
                                                           in_values=cand[:, h, :], imm_value=-1e30),
                     r=["tsa%d" % h], w=["cand%d" % h])
            for h in range(8):
                OP("dve", lambda e, h=h: e.max(out=ts[:, h, 8:16], in_=cand[:, h, :]), r=["cand%d" % h], w=["tsb%d" % h])
            for h in range(8):
                OP("dve", lambda e, h=h: e.max_index(out=pos[:, h, 8:16], in_max=ts[:, h, 8:16], in_values=cand[:, h, :]),
                     r=["cand%d" % h, "tsb%d" % h], w=["posb%d" % h])
            allts = ["tsa%d" % h for h in range(8)] + ["tsb%d" % h for h in range(8)]
            allpos = ["posa%d" % h for h in range(8)] + ["posb%d" % h for h in range(8)]
            posf = pos[:].rearrange("p h k -> p (h k)")
            OP("dve", lambda e: e.tensor_single_scalar(out=pij[:, 0, :], in_=posf, scalar=4,
                                                         op=ALU.logical_shift_right), r=allpos, w=["pij0"])
            OP("dve", lambda e: e.tensor_single_scalar(out=pij[:, 1, :], in_=posf, scalar=15,
                                                         op=ALU.bitwise_and), r=allpos, w=["pij1"])
            OP("dve", lambda e: e.tensor_copy(out=pijf[:].rearrange("p a h k -> p a (h k)"), in_=pij[:]),
                 r=["pij0", "pij1"], w=["pijf"])
            sif4 = sif[:].rearrange("p (h two) k -> p h two k", two=2)
            for a in range(2):
                OP("dve", lambda e, a=a: e.tensor_tensor(
                    out=oh[a][:], in0=bcast(pijf[:, a, :, :], 3, [128, 8, 16, 16]),
                    in1=iota16[:].unsqueeze(1).unsqueeze(1).to_broadcast([128, 8, 16, 16]),
                    op=ALU.is_equal), r=["pijf"], w=["oh%d" % a])
            for a in range(2):
                OP("dve", lambda e, a=a: e.tensor_tensor(
                    out=oh[a][:], in0=oh[a][:], in1=bcast(sif4[:, :, a, :], 2, [128, 8, 16, 16]),
                    op=ALU.mult), r=["sif"], w=["oh%d" % a])
            for a in range(2):
                OP("dve", lambda e, a=a: e.tensor_reduce(
                    out=ee[:, a, :], in_=oh[a][:].rearrange("p h k i -> p (h k) i"), axis=AX.X, op=ALU.add),
                    r=["oh%d" % a], w=["ee%d" % a])
            OP("dve", lambda e: e.scalar_tensor_tensor(out=ef[:], in0=ee[:, 0, :], scalar=128.0, in1=ee[:, 1, :],
                                                         op0=ALU.mult, op1=ALU.add), r=["ee0", "ee1"], w=["ef"])
            ek = "eidx%d" % s
            OP("dve", lambda e: e.tensor_copy(out=eidx[s][:], in_=ef[:]), r=["ef"], w=[ek])
            OP("dve", lambda e: e.tensor_tensor(out=dsm[:], in0=ts[:], in1=ts[:, :, 0:1].to_broadcast([128, 8, 16]),
                                                  op=ALU.subtract), r=allts, w=["dsm"])
            OP("act", lambda e: e.activation(out=ex[:], in_=dsm[:], func=AF.Exp), r=["dsm"], w=["ex"])
            OP("dve", lambda e: e.tensor_reduce(out=ssum[:], in_=ex[:], axis=AX.X, op=ALU.add), r=["ex"], w=["ssum"])
            OP("dve", lambda e: e.reciprocal(out=ssum[:], in_=ssum[:]), r=["ssum"], w=["ssum"])
            OP("dve", lambda e: e.tensor_tensor(out=gate[s][:].rearrange("p (h k) -> p h k", k=16), in0=ex[:],
                                                  in1=ssum[:].unsqueeze(2).to_broadcast([128, 8, 16]), op=ALU.mult),
                 r=["ex", "ssum"], w=["gate%d" % s])

        def s1(t, gq):
            s = t % 2
            ek = "eidx%d" % s
            cols = slice(gq * GS, (gq + 1) * GS)
            slots = []
            accs = []
            for hk in range(gq * GS, (gq + 1) * GS):
                u = cnt["u"] % NS
                cnt["u"] += 1
                slots.append(u)
                pi = cnt["p"] % NP
                cnt["p"] += 1
                P.op("pool", lambda e, u=u, hk=hk: e.indirect_dma_start(
                    out=uvb[u][:], out_offset=None, in_=uv_scr[:, :],
                    in_offset=bass.IndirectOffsetOnAxis(ap=eidx[s][:, hk:hk + 1], axis=0)),
                    r=[ek], w=["uvb%d" % u], dma="gu%d" % u)
                hk_ = "h%d" % s
                if (hk % GS) < DSPLIT:
                    P.op("dve", lambda e, u=u, pi=pi: e.tensor_tensor(
                        out=prod[pi][:], in0=uvb[u][:, 0:D], in1=hb[s][:], op=ALU.mult),
                        r=["uvb%d" % u, "hb%d" % s], w=["prod%d" % pi])
                    accs.append((pi, hk))
                else:
                    P.op("dve", lambda e, u=u, hk=hk: e.scalar_tensor_tensor(
                        out=junk[:], in0=uvb[u][:, 0:D], scalar=1.0, in1=h_t[s][:],
                        op0=ALU.mult, op1=ALU.mult, accum_out=dots[:, hk:hk + 1]),
                        r=["uvb%d" % u, hk_], w=["dots%d" % gq])
            for (pi, hk) in accs:
                P.op("act", lambda e, pi=pi, hk=hk: e.activation(
                    out=junk2[:], in_=prod[pi][:], func=AF.Copy, accum_out=dots[:, hk:hk + 1]),
                    r=["prod%d" % pi], w=["dots%d" % gq])
            P.op("act", lambda e: e.activation(out=actt[:, cols], in_=dots[:, cols], func=AF.Gelu),
                 r=["dots%d" % gq], w=["actt%d" % gq])
            return slots

        def s2(t, gq, slots):
            s = t % 2
            cols = slice(gq * GS, (gq + 1) * GS)
            P.op("dve", lambda e: e.tensor_tensor(out=wt[:, cols], in0=gate[s][:, cols], in1=actt[:, cols], op=ALU.mult),
                 r=["gate%d" % s, "actt%d" % gq], w=["wt%d" % gq])
            for j, hk in enumerate(range(gq * GS, (gq + 1) * GS)):
                u = slots[j]
                d = cnt["d"] % ND
                cnt["d"] += 1
                P.op("act", lambda e, d=d, hk=hk: e.activation(out=dg[d][:], in_=identf[:], func=AF.Copy,
                                                               scale=wt[:, hk:hk + 1]),
                     r=["wt%d" % gq, "identf"], w=["dg%d" % d])
                for half in range(2):
                    P.op("pe", lambda e, d=d, u=u, half=half, hk=hk: e.matmul(
                        pv[half][:], lhsT=dg[d][:], rhs=uvb[u][:, D + half * 512:D + (half + 1) * 512],
                        start=(hk == 0), stop=(hk == 127)), r=["dg%d" % d, "uvb%d" % u], w=["pv%d" % half])

        def tail(t):
            s = t % 2
            hk_, yk = "h%d" % s, "y%d" % s
            for half in range(2):
                P.op("dve", lambda e, half=half: e.scalar_tensor_tensor(
                    out=y_t[s][:, half * 512:(half + 1) * 512], in0=h_t[s][:, half * 512:(half + 1) * 512],
                    scalar=ALPHA, in1=pv[half][:], op0=ALU.mult, op1=ALU.add),
                    r=[hk_, "pv%d" % half], w=[yk])
            layer_norm_tail(P, y_t[s], yk, g2, "g2", b2, "b2", st6, mv2, rstd, nmr, "b", gb_eng="dve")
            P.op("sp", lambda e: e.dma_start(out=y_d[t * 128:(t + 1) * 128, :], in_=y_t[s][:]),
                 r=[yk], dma="yst%d" % s)

        load_h(0)
        front(0)
        for t in range(n_tiles):
            if t + 1 < n_tiles:
                load_h(t + 1)
            pend = None
            todo = []
            if t + 1 < n_tiles:
                front(t + 1, OP=lambda *a, **k: todo.append((a, k)))
            per = -(-len(todo) // max(1, NG - 4))
            for gq in range(NG):
                slots = s1(t, gq)
                if pend is not None:
                    s2(t, *pend)
                pend = (gq, slots)
                for (a, k) in todo[:per]:
                    P.op(*a, **k)
                del todo[:per]
            s2(t, *pend)
            for (a, k) in todo:
                P.op(*a, **k)
            tail(t)
        P.emit()


def layer_norm_tail(P, z, zk, g, gk, b, bk, st6, mv2, rstd, nmr, pfx, gb_eng="pool"):
    k6, k2, kr, kn = pfx + "st6", pfx + "mv2", pfx + "rstd", pfx + "nmr"
    for half in range(2):
        P.op("dve", lambda e, half=half: e.bn_stats(out=st6[:, half * 6:(half + 1) * 6],
                                                     in_=z[:, half * 512:(half + 1) * 512]), r=[zk], w=[k6])
    P.op("dve", lambda e: e.bn_aggr(out=mv2[:], in_=st6[:]), r=[k6], w=[k2])
    P.op("act", lambda e: e.activation(out=rstd[:], in_=mv2[:, 1:2], func=AF.Sqrt, bias=EPS_T[0][:], scale=1.0),
         r=[k2, "eps"], w=[kr])
    P.op("dve", lambda e: e.reciprocal(out=rstd[:], in_=rstd[:]), r=[kr], w=[kr])
    P.op("dve", lambda e: e.scalar_tensor_tensor(out=nmr[:], in0=mv2[:, 0:1], scalar=-1.0, in1=rstd[:],
                                                 op0=ALU.mult, op1=ALU.mult), r=[k2, kr], w=[kn])
    P.op("act", lambda e: e.activation(out=z[:], in_=z[:], func=AF.Identity, bias=nmr[:], scale=rstd[:]),
         r=[zk, kr, kn], w=[zk])
    P.op(gb_eng, lambda e: e.tensor_tensor(out=z[:], in0=z[:], in1=g[:], op=ALU.mult), r=[zk, gk], w=[zk])
    P.op(gb_eng, lambda e: e.tensor_tensor(out=z[:], in0=z[:], in1=b[:], op=ALU.add), r=[zk, bk], w=[zk])


def alloc_globals(nc, es, sync):
    G = {}
    psf = [es.enter_context(nc.psum_tensor("psf%d" % i, [128, 512], F32)) for i in range(4)]
    pv = [es.enter_context(nc.psum_tensor("pv%d" % i, [128, 512], F32)) for i in range(2)]
    psb = [es.enter_context(nc.psum_tensor("psb%d" % i, [128, 1024], BF16)) for i in range(2)]
    G["psf"] = PsumPool(psf, "psf")
    G["psb"] = PsumPool(psb, "psb")
    G["pv"] = pv
    G["ident"] = es.enter_context(nc.sbuf_tensor("sb_ident", [128, 128], BF16))
    G["identf"] = es.enter_context(nc.sbuf_tensor("sb_identf", [128, 128], F32))
    G["iota16"] = es.enter_context(nc.sbuf_tensor("sb_iota16", [128, 16], F32))
    G["eps"] = es.enter_context(nc.sbuf_tensor("sb_eps", [128, 1], F32))
    EPS_T[0] = G["eps"]
    return G


def phase_const(nc, sync, G, identf_d, iota16_d):
    P = Prog(nc, sync)
    P.op("sp", lambda e: e.dma_start(out=G["identf"][:], in_=identf_d[:, :]), w=["identf"], dma="c0")
    P.op("sp", lambda e: e.dma_start(out=G["iota16"][:], in_=iota16_d[:, :]), w=["iota16"], dma="c0")
    P.op("dve", lambda e: e.tensor_copy(out=G["ident"][:], in_=G["identf"][:]), r=["identf"], w=["ident"])
    P.op("dve", lambda e: e.memset(G["eps"][:], LN_EPS), w=["eps"])
    P.emit()


def build_b_only(n_tiles, tab_dt=F32):
    nc = bass.Bass("TRN2", target_bir_lowering=False)
    T = n_tiles * 128
    dt = lambda name, shape, dtype, kind="ExternalInput": nc.dram_tensor(name, shape, dtype, kind=kind).ap()
    h_d = dt("h_in", [T, D], F32)
    wpqT_d = dt("wpqT", [2048, D], F32)
    skT_d = dt("skT", [2048, 128], F32)
    pu = dt("peer_u", [N_EXP, D], tab_dt)
    pvv = dt("peer_v", [N_EXP, D], tab_dt)
    g2 = dt("ln2g", [128, D], F32)
    b2 = dt("ln2b", [128, D], F32)
    identf_d = dt("identf", [128, 128], F32)
    iota_d = dt("iota16", [128, 16], F32)
    y_d = dt("y", [T, D], F32, kind="ExternalOutput")
    with ExitStack() as es:
        sync = Sync(nc, es)
        G = alloc_globals(nc, es, sync)
        phase_const(nc, sync, G, identf_d, iota_d)
        weff = es.enter_context(nc.sbuf_tensor("sb_weff", [128, 8, 2048], BF16))
        phase_b0(nc, sync, G, weff, wpqT_d, skT_d)
        uv_scr = nc.dram_tensor("uv_scr", [N_EXP, 2 * D], BF16, kind="Internal").ap()
        phase_bt(nc, sync, G, pu, pvv, uv_scr)
        phase_b(nc, sync, G, n_tiles, weff, h_d, uv_scr, g2, b2, y_d)
    return nc


def phase_a0(nc, sync, G, mkT, mv_aug, mem_d, memg_d, memb_d, wkv_d):
    with ExitStack() as es:
        def sb(name, shape, dt):
            return es.enter_context(nc.sbuf_tensor("a0_" + name, shape, dt))
        ident = G["ident"]
        wkv = sb("wkv", [128, 8, 1024], BF16)
        mg = sb("mg", [128, D], F32)
        mb = sb("mb", [128, D], F32)
        mt = [sb("mt%d" % i, [128, D], F32) for i in range(2)]
        mn = sb("mn", [128, D], BF16)
        mnT = sb("mnT", [128, 8, 256], BF16)
        st6 = sb("st6", [128, 12], F32)
        mv2 = sb("mv2", [128, 2], F32)
        rstd = sb("rstd", [128, 1], F32)
        nmr = sb("nmr", [128, 1], F32)
        P = Prog(nc, sync)
        for c in range(8):
            P.op("pool", lambda e, c=c: e.dma_start(out=wkv[:, c, :], in_=wkv_d[c * 128:(c + 1) * 128, :]),
                 w=["wkv"], dma="a0w")
        P.op("sp", lambda e: e.dma_start(out=mg[:], in_=memg_d[:, :]), w=["mg"], dma="a0p")
        P.op("sp", lambda e: e.dma_start(out=mb[:], in_=memb_d[:, :]), w=["mb"], dma="a0p")
        P.op("dve", lambda e: e.memset(mv_aug[:], 1.0), w=["mv_aug"])
        for mc in range(2):
            mk_ = "mt%d" % mc
            P.op("sp", lambda e, mc=mc: e.dma_start(out=mt[mc][:], in_=mem_d[mc * 128:(mc + 1) * 128, :]),
                 w=[mk_], dma="a0m%d" % mc)
            layer_norm_tail(P, mt[mc], mk_, mg, "mg", mb, "mb", st6, mv2, rstd, nmr, "a0", gb_eng="dve")
            P.op("act", lambda e, mc=mc: e.copy(out=mn[:], in_=mt[mc][:]), r=[mk_], w=["mn"])
            pb, pbk = G["psb"].get()
            for c in range(8):
                P.op("pe", lambda e, c=c, pb=pb: e.transpose(out=pb[:, c * 128:(c + 1) * 128],
                                                             in_=mn[:, c * 128:(c + 1) * 128], identity=ident[:]),
                     r=["mn", "ident"], w=[pbk])
            P.op("act", lambda e, mc=mc, pb=pb: e.copy(out=mnT[:, :, mc * 128:(mc + 1) * 128],
                                                       in_=pb[:].rearrange("p (c t) -> p c t", t=128)),
                 r=[pbk], w=["mnT"])
        for h in range(4):
            bank, bk = G["psA"].get()
            for c in range(8):
                P.op("pe", lambda e, c=c, h=h, bank=bank: e.matmul(
                    bank[:, 0:256], lhsT=wkv[:, c, h * 128:(h + 1) * 128], rhs=mnT[:, c, :],
                    start=(c == 0), stop=(c == 7)), r=["wkv", "mnT"], w=[bk])
            P.op("act", lambda e, h=h, bank=bank: e.copy(out=mkT[:, h, :], in_=bank[:, 0:256]), r=[bk], w=["mkT"])
        for mc in range(2):
            bank, bk = G["psA"].get()
            for c in range(8):
                P.op("pe", lambda e, c=c, mc=mc, bank=bank: e.matmul(
                    bank[:], lhsT=mnT[:, c, mc * 128:(mc + 1) * 128], rhs=wkv[:, c, 512:1024],
                    start=(c == 0), stop=(c == 7)), r=["wkv", "mnT"], w=[bk])
            P.op("dve", lambda e, mc=mc, bank=bank: e.tensor_copy(
                out=mv_aug[:, mc, :, 0:128], in_=bank[:].rearrange("p (h d) -> p h d", d=128)),
                r=[bk], w=["mv_aug"])
        P.emit()


def phase_a1(nc, sync, G, n_tiles, mkT, mv_aug, xT_d, w1_d, b1_d, cc_d, ss_d, dt_d, wq_d, wk_d, cd_d,
             mask_d, gng_d, sink_d, brT_scr):
    with ExitStack() as es:
        def sb(name, shape, dt):
            return es.enter_context(nc.sbuf_tensor("a1_" + name, shape, dt))
        ident = G["ident"]
        FM = FM_COLS
        w1 = sb("w1", [128, 8, NCOL_A1], BF16)
        b1 = sb("b1", [1, NCOL_A1], BF16)
        ones = sb("ones", [1, 128], BF16)
        dtab = sb("dtab", [128, 4, 128], F32)
        wqt = sb("wqt", [128, 2, 128], F32)
        wkt = sb("wkt", [128, 4], F32)
        cdt = sb("cdt", [128, 2], F32)
        msk = sb("msk", [128, 2, 128], F32)
        gng = sb("gng", [128, 512], F32)
        esink = sb("esink", [128, 8], F32)
        state = sb("state", [128, 2, 128], F32)
        state_bf = sb("state_bf", [128, 2, 128], BF16)
        xT = [sb("xT%d" % i, [128, 8, 128], BF16) for i in range(2)]
        cct = [sb("cc%d" % i, [128, 128], F32) for i in range(2)]
        sst = [sb("ss%d" % i, [128, 128], F32) for i in range(2)]
        qT = sb("qT", [128, 4, 128], BF16)
        kT = [sb("kT%d" % i, [128, 2, 2, 128], BF16) for i in range(2)]
        kpad = sb("kpad", [128, 2, 2, 128], BF16)
        qspad = sb("qspad", [128, 2, 2, 128], BF16)
        vaug = [sb("vaug%d" % i, [128, 2, 65], BF16) for i in range(2)]
        mqT = sb("mqT", [128, 4, 128], BF16)
        tmp1 = sb("tmp1", [128, 4, 128], F32)
        tmp2 = sb("tmp2", [128, 4, 128], F32)
        rot = sb("rot", [128, 4, 128], BF16)
        ktok = sb("ktok", [128, 256], BF16)
        vret = sb("vret", [128, 4, 128], BF16)
        vw = sb("vw", [128, 4, 128], BF16)
        sg = sb("sg", [128, 512], F32)
        pT = sb("pT", [128, 2, 8, 128], BF16)
        pTc = sb("pTc", [128, 2, 4, 128], BF16)
        innerTm = sb("innerTm", [128, 4, 128], BF16)
        den = sb("den", [128, 8], F32)
        denc = sb("denc", [128, 4], F32)
        br = sb("br", [128, 3, 512], BF16)
        xn = sb("xn", [128, 512], F32)
        gst = sb("gst", [128, 24], F32)
        gmv = sb("gmv", [128, 4, 2], F32)
        grs = sb("grs", [128, 4], F32)
        brT = [sb("brT%d" % i, [128, 12, 128], BF16) for i in range(2)]

        P = Prog(nc, sync)
        for c in range(8):
            P.op("pool", lambda e, c=c: e.dma_start(out=w1[:, c, :], in_=w1_d[c * 128:(c + 1) * 128, :],
                                                    max_dma_last_dim=4096), w=["w1"], dma="a1w")
        P.op("pool", lambda e: e.dma_start(out=b1[:], in_=b1_d[:, :], max_dma_last_dim=4096), w=["b1"], dma="a1w")
        for (tt, dd, kk) in ((dtab, dt_d, "dtab"), (wqt, wq_d, "wqt")):
            P.op("sp", lambda e, tt=tt, dd=dd: e.dma_start(out=tt[:].rearrange("p a b -> p (a b)"), in_=dd[:, :]),
                 w=[kk], dma="a1p")
        P.op("sp", lambda e: e.dma_start(out=msk[:].rearrange("p a b -> p (a b)"), in_=mask_d[:, :]),
             w=["msk"], dma="a1p")
        for (tt, dd, kk) in ((wkt, wk_d, "wkt"), (cdt, cd_d, "cdt"), (gng, gng_d, "gng"), (esink, sink_d, "esink")):
            P.op("sp", lambda e, tt=tt, dd=dd: e.dma_start(out=tt[:], in_=dd[:, :]), w=[kk], dma="a1p")
        P.op("act", lambda e: e.activation(out=esink[:], in_=esink[:], func=AF.Exp), r=["esink"], w=["esink"])
        P.op("dve", lambda e: e.memset(ones[:], 1.0), w=["ones"])
        P.op("dve", lambda e: e.memset(state[:], 0.0), w=["state"])
        P.op("dve", lambda e: e.memset(state_bf[:], 0.0), w=["state_bf"])
        for i in range(2):
            P.op("dve", lambda e, i=i: e.memset(vaug[i][:], 1.0), w=["vaug%d" % i])
            P.op("dve", lambda e, i=i: e.memset(kT[i][:], 0.0), w=["kT%d" % i])
        P.op("dve", lambda e: e.memset(kpad[:], 0.0), w=["kpad"])
        P.op("dve", lambda e: e.memset(qspad[:], 0.0), w=["qspad"])

        def loads(t):
            s = t % 2
            P.op("pool", lambda e: e.dma_start(
                out=xT[s][:], in_=xT_d.rearrange("(c p) t -> p c t", p=128)[:, :, t * 128:(t + 1) * 128]),
                w=["xT%d" % s], dma="a1x%d" % s)
            P.op("sp", lambda e: e.dma_start(out=cct[s][:], in_=cc_d[:, t * 128:(t + 1) * 128]),
                 w=["cc%d" % s], dma="a1c%d" % s)
            P.op("sp", lambda e: e.dma_start(out=sst[s][:], in_=ss_d[:, t * 128:(t + 1) * 128]),
                 w=["ss%d" % s], dma="a1c%d" % s)

        def fm_group(s, fbs, bank, bk):
            xk = "xT%d" % s
            for j, fb in enumerate(fbs):
                for c in range(8):
                    P.op("pe", lambda e, c=c, fb=fb, j=j: e.matmul(
                        bank[:, j * 128:(j + 1) * 128], lhsT=w1[:, c, fb * 128:(fb + 1) * 128], rhs=xT[s][:, c, :],
                        start=(c == 0), stop=False), r=["w1", xk], w=[bk])
                P.op("pe", lambda e, fb=fb, j=j: e.matmul(
                    bank[:, j * 128:(j + 1) * 128], lhsT=b1[0:1, fb * 128:(fb + 1) * 128], rhs=ones[0:1, :],
                    start=False, stop=True), r=["b1", "ones"], w=[bk])

        def tm_group(s, col0, ncols, bank, bk):
            xk = "xT%d" % s
            for c in range(8):
                P.op("pe", lambda e, c=c: e.matmul(
                    bank[:, 0:ncols], lhsT=xT[s][:, c, :], rhs=w1[:, c, col0:col0 + ncols],
                    start=(c == 0), stop=False), r=["w1", xk], w=[bk])
            P.op("pe", lambda e: e.matmul(bank[:, 0:ncols], lhsT=ones[0:1, :], rhs=b1[0:1, col0:col0 + ncols],
                                          start=False, stop=True), r=["b1", "ones"], w=[bk])

        def do_tile(t):
            s = t % 2
            sp_ = 1 - s
            if t + 1 < n_tiles:
                loads(t + 1)
            if A1_STAGE[0] == -1:
                return
            bank, bk = G["psA"].get()
            fm_group(s, [0, 1, 2, 3], bank, bk)
            P.op("act", lambda e: e.copy(out=qT[:].rearrange("p a b -> p (a b)"), in_=bank[:]), r=[bk], w=["qT"])
            if A1_STAGE[0] == -2:
                return
            bankk, bkk = G["psA"].get()
            fm_group(s, [4, 5], bankk, bkk)
            kTk = "kT%d" % s
            for hf in range(2):
                P.op("act", lambda e, hf=hf: e.copy(
                    out=kT[s][hf * 64:(hf + 1) * 64, :, hf, :],
                    in_=bankk[hf * 64:(hf + 1) * 64, 0:256].rearrange("p (g t) -> p g t", t=128)), r=[bkk], w=[kTk])
            if A1_STAGE[0] == -3:
                return
            bra, bkra = G["psA"].get()
            fm_group(s, [6, 7, 8, 9], bra, bkra)
            brb, bkrb = G["psA"].get()
            fm_group(s, [10, 11, 12, 13], brb, bkrb)
            P.op("dve", lambda e: e.tensor_tensor(out=tmp1[:], in0=bra[:].rearrange("p (a b) -> p a b", b=128),
                                                  in1=bcast(cct[s][:], 1, [128, 4, 128]), op=ALU.mult),
                 r=[bkra, "cc%d" % s], w=["tmp1"])
            P.op("dve", lambda e: e.tensor_tensor(out=tmp2[:], in0=brb[:].rearrange("p (a b) -> p a b", b=128),
                                                  in1=bcast(sst[s][:], 1, [128, 4, 128]), op=ALU.mult),
                 r=[bkrb, "ss%d" % s], w=["tmp2"])
            P.op("pool", lambda e: e.tensor_tensor(out=rot[:], in0=tmp1[:], in1=tmp2[:], op=ALU.add),
                 r=["tmp1", "tmp2"], w=["rot"])
            for hf in range(2):
                rows = slice(hf * 64, (hf + 1) * 64)
                P.op("pool", lambda e, hf=hf, rows=rows: e.tensor_tensor(
                    out=qspad[rows, :, hf, :], in0=rot[rows, 0:2, :], in1=wqt[rows, :, :], op=ALU.mult),
                    r=["rot", "wqt"], w=["qspad"])
                P.op("pool", lambda e, hf=hf, rows=rows: e.tensor_copy(out=kpad[rows, :, hf, :], in_=rot[rows, 2:4, :]),
                     r=["rot"], w=["kpad"])
            if A1_STAGE[0] == -4:
                return
            bankm, bkm = G["psA"].get()
            fm_group(s, [14, 15, 16, 17], bankm, bkm)
            P.op("act", lambda e: e.copy(out=mqT[:].rearrange("p a b -> p (a b)"), in_=bankm[:]), r=[bkm], w=["mqT"])
            if A1_STAGE[0] == -5:
                return
            bav, bkav = G["psA"].get()
            tm_group(s, FM, 128, bav, bkav)
            vk = "vaug%d" % s
            P.op("act", lambda e: e.copy(out=vaug[s][:, :, 0:64], in_=bav[:, 0:128].rearrange("p (g d) -> p g d", d=64)),
                 r=[bkav], w=[vk])
            if A1_STAGE[0] == -6:
                return
            brv, bkrv = G["psA"].get()
            tm_group(s, FM + 128, 512, brv, bkrv)
            P.op("act", lambda e: e.copy(out=vret[:].rearrange("p a b -> p (a b)"), in_=brv[:]), r=[bkrv], w=["vret"])
            if A1_STAGE[0] == -8:
                return
            VV = _os.environ.get("VV", "2")
            if VV == "0":
                P.op("dve", lambda e: e.tensor_tensor(out=vw[:], in0=brv[:].rearrange("p (a b) -> p a b", b=128),
                                                      in1=bcast(wkt[:], 2, [128, 4, 128]), op=ALU.mult),
                     r=[bkrv, "wkt"], w=["vw"])
            elif VV == "1":
                P.op("dve", lambda e: e.tensor_tensor(out=tmp1[:], in0=brv[:].rearrange("p (a b) -> p a b", b=128),
                                                      in1=bcast(wkt[:], 2, [128, 4, 128]), op=ALU.mult),
                     r=[bkrv, "wkt"], w=["tmp1"])
            elif VV == "2":
                P.op("dve", lambda e: e.tensor_tensor(out=vw[:], in0=vret[:],
                                                      in1=bcast(wkt[:], 2, [128, 4, 128]), op=ALU.mult),
                     r=["vret", "wkt"], w=["vw"])
            elif VV == "3":
                for h in range(4):
                    P.op("dve", lambda e, h=h: e.tensor_scalar(
                        out=vw[:, h, :], in0=brv[:, h * 128:(h + 1) * 128], scalar1=wkt[:, h:h + 1], scalar2=None,
                        op0=ALU.mult), r=[bkrv, "wkt"], w=["vw"])
            if A1_STAGE[0] == -7:
                return
            brg, bkrg = G["psA"].get()
            tm_group(s, FM + 640, 512, brg, bkrg)
            P.op("act", lambda e: e.activation(out=sg[:], in_=brg[:], func=AF.Silu), r=[bkrg], w=["sg"])
            P.op("pool", lambda e: e.tensor_tensor(out=sg[:], in0=sg[:], in1=gng[:], op=ALU.mult),
                 r=["sg", "gng"], w=["sg"])
            if A1_STAGE[0] == 1:
                return
            whichs = [(0, s)] + ([(1, sp_)] if t > 0 else [])
            for hb2 in range(2):
                for (wi, slot) in whichs:
                    bl, bkl = G["psA"].get()
                    for j in range(4):
                        h = 4 * hb2 + j
                        i, half = h // 2, h % 2
                        P.op("pe", lambda e, j=j, i=i, half=half, slot=slot, bl=bl, hb2=hb2: e.matmul(
                            bl[:, j * 128:(j + 1) * 128], lhsT=kT[slot][:, hb2, half, :],
                            rhs=qT[:, i, :], start=True, stop=True),
                            r=["kT%d" % slot, "qT"], w=[bkl])
                    pk = "pT%d%d" % (wi, hb2)
                    P.op("act", lambda e, wi=wi, bl=bl, hb2=hb2: e.activation(
                        out=pT[:, wi, hb2 * 4:(hb2 + 1) * 4, :].rearrange("p a b -> p (a b)"), in_=bl[:],
                        func=AF.Exp, scale=0.125), r=[bkl], w=[pk])
                    if A1_STAGE[0] == 11:
                        continue
                    P.op("pool", lambda e, wi=wi, hb2=hb2: e.tensor_tensor(
                        out=pT[:, wi, hb2 * 4:(hb2 + 1) * 4, :], in0=pT[:, wi, hb2 * 4:(hb2 + 1) * 4, :],
                        in1=bcast(msk[:, wi, :], 1, [128, 4, 128]), op=ALU.mult), r=[pk, "msk"], w=[pk])
            if A1_STAGE[0] in (11, 12):
                return
            for hb2 in range(2):
                bo, bko = G["psA"].get()
                for j in range(4):
                    h = 4 * hb2 + j
                    if t > 0:
                        P.op("pe", lambda e, j=j, h=h, bo=bo, hb2=hb2: e.matmul(
                            bo[:, j * 65:(j + 1) * 65], lhsT=pT[:, 1, h, :], rhs=vaug[sp_][:, hb2, :],
                            start=True, stop=False), r=["pT1%d" % hb2, "vaug%d" % sp_], w=[bko])
                    P.op("pe", lambda e, j=j, h=h, bo=bo, hb2=hb2: e.matmul(
                        bo[:, j * 65:(j + 1) * 65], lhsT=pT[:, 0, h, :], rhs=vaug[s][:, hb2, :],
                        start=(t == 0), stop=True), r=["pT0%d" % hb2, vk], w=[bko])
                bo3 = bo[:, 0:260].rearrange("p (j d) -> p j d", d=65)
                if A1_STAGE[0] == 13:
                    continue
                dk = "den%d" % hb2
                P.op("dve", lambda e, bo3=bo3, hb2=hb2: e.tensor_tensor(
                    out=den[:, hb2 * 4:(hb2 + 1) * 4], in0=bo3[:, :, 64], in1=esink[:, hb2 * 4:(hb2 + 1) * 4],
                    op=ALU.add), r=[bko, "esink"], w=[dk])
                P.op("dve", lambda e, hb2=hb2: e.reciprocal(out=den[:, hb2 * 4:(hb2 + 1) * 4], in_=den[:, hb2 * 4:(hb2 + 1) * 4]),
                     r=[dk], w=[dk])
                P.op("dve", lambda e, bo3=bo3, hb2=hb2: e.tensor_tensor(
                    out=br[:, 0, hb2 * 256:(hb2 + 1) * 256].rearrange("p (j d) -> p j d", d=64),
                    in0=bo3[:, :, 0:64], in1=bcast(den[:, hb2 * 4:(hb2 + 1) * 4], 2, [128, 4, 64]), op=ALU.mult),
                    r=[bko, dk], w=["br0"])
            if A1_STAGE[0] == 2:
                return
            for mc in range(2):
                bl, bkl = G["psA"].get()
                for h in range(4):
                    P.op("pe", lambda e, h=h, mc=mc, bl=bl: e.matmul(
                        bl[:, h * 128:(h + 1) * 128], lhsT=mkT[:, h, mc * 128:(mc + 1) * 128], rhs=mqT[:, h, :],
                        start=True, stop=True), r=["mkT", "mqT"], w=[bkl])
                P.op("act", lambda e, mc=mc, bl=bl: e.activation(
                    out=pTc[:, mc, :, :].rearrange("p a b -> p (a b)"), in_=bl[:], func=AF.Exp,
                    scale=float(128 ** -0.5)), r=[bkl], w=["pTc%d" % mc])
            for hp2 in range(2):
                bo, bko = G["psA"].get()
                for j in range(2):
                    h = 2 * hp2 + j
                    for mc in range(2):
                        P.op("pe", lambda e, j=j, h=h, mc=mc, bo=bo: e.matmul(
                            bo[:, j * 129:(j + 1) * 129], lhsT=pTc[:, mc, h, :], rhs=mv_aug[:, mc, h, :],
                            start=(mc == 0), stop=(mc == 1)), r=["pTc%d" % mc, "mv_aug"], w=[bko])
                bo3 = bo[:, 0:258].rearrange("p (j d) -> p j d", d=129)
                dk = "denc%d" % hp2
                P.op("dve", lambda e, bo3=bo3, hp2=hp2: e.reciprocal(out=denc[:, hp2 * 2:(hp2 + 1) * 2], in_=bo3[:, :, 128]),
                     r=[bko], w=[dk])
                P.op("dve", lambda e, bo3=bo3, hp2=hp2: e.tensor_tensor(
                    out=br[:, 2, hp2 * 256:(hp2 + 1) * 256].rearrange("p (j d) -> p j d", d=128),
                    in0=bo3[:, :, 0:128], in1=bcast(denc[:, hp2 * 2:(hp2 + 1) * 2], 2, [128, 2, 128]), op=ALU.mult),
                    r=[bko, dk], w=["br2"])
            if A1_STAGE[0] == 3:
                return
            pb, pbk = G["psb"].get()
            for blk in range(2):
                P.op("pe", lambda e, blk=blk: e.transpose(out=pb[:, blk * 128:(blk + 1) * 128], in_=rot[:, 2 + blk, :],
                                                          identity=ident[:]), r=["rot", "ident"], w=[pbk])
            P.op("act", lambda e: e.copy(out=ktok[:], in_=pb[:, 0:256]), r=[pbk], w=["ktok"])
            bi, bki = G["psA"].get()
            for h in range(4):
                blk, half = h // 2, h % 2
                P.op("pe", lambda e, h=h, blk=blk, half=half: e.matmul(
                    bi[:, h * 128:(h + 1) * 128], lhsT=kpad[:, blk, half, :],
                    rhs=rot[:, blk, :], start=True, stop=True), r=["rot", "kpad"], w=[bki])
            P.op("dve", lambda e: e.tensor_tensor(out=innerTm[:], in0=bi[:].rearrange("p (a b) -> p a b", b=128),
                                                  in1=dtab[:], op=ALU.mult), r=[bki, "dtab"], w=["innerTm"])
            bo, bko = G["psA"].get()
            for h in range(4):
                blk, half = h // 2, h % 2
                P.op("pe", lambda e, h=h: e.matmul(
                    bo[:, h * 128:(h + 1) * 128], lhsT=innerTm[:, h, :], rhs=vret[:, h, :],
                    start=True, stop=(t == 0)), r=["innerTm", "vret"], w=[bko])
                if t > 0:
                    P.op("pe", lambda e, h=h, blk=blk, half=half: e.matmul(
                        bo[:, h * 128:(h + 1) * 128], lhsT=qspad[:, blk, half, :],
                        rhs=state_bf[:, blk, :], start=False, stop=True),
                        r=["qspad", "state_bf"], w=[bko])
            bkv, bkkv = G["psA"].get()
            for h in range(4):
                blk, half = h // 2, h % 2
                P.op("pe", lambda e, h=h, blk=blk, half=half: e.matmul(
                    bkv[:, h * 128:(h + 1) * 128], lhsT=ktok[:, blk * 128:(blk + 1) * 128],
                    rhs=vw[:, h, :], start=True, stop=True), r=["ktok", "vw"], w=[bkkv])
            for h in range(4):
                blk, half = h // 2, h % 2
                rows = slice(half * 64, (half + 1) * 64)
                P.op("dve", lambda e, h=h, blk=blk, rows=rows: e.scalar_tensor_tensor(
                    out=state[rows, blk, :], in0=state[rows, blk, :], scalar=cdt[rows, blk:blk + 1],
                    in1=bkv[rows, h * 128:(h + 1) * 128], op0=ALU.mult, op1=ALU.add),
                    r=["state", "cdt", bkkv], w=["state"])
            P.op("act", lambda e: e.copy(out=state_bf[:], in_=state[:]), r=["state"], w=["state_bf"])
            for h in range(4):
                P.op("dve", lambda e, h=h: e.bn_stats(out=gst[:, h * 6:(h + 1) * 6], in_=bo[:, h * 128:(h + 1) * 128]),
                     r=[bko], w=["gst%d" % h])
                P.op("dve", lambda e, h=h: e.bn_aggr(out=gmv[:, h, :], in_=gst[:, h * 6:(h + 1) * 6]),
                     r=["gst%d" % h], w=["gmv"])
            P.op("act", lambda e: e.activation(out=grs[:], in_=gmv[:, :, 1], func=AF.Sqrt, bias=EPS_T[0][:], scale=1.0),
                 r=["gmv", "eps"], w=["grs"])
            P.op("dve", lambda e: e.reciprocal(out=grs[:], in_=grs[:]), r=["grs"], w=["grs"])
            for h in range(4):
                P.op("dve", lambda e, h=h: e.tensor_scalar(
                    out=xn[:, h * 128:(h + 1) * 128], in0=bo[:, h * 128:(h + 1) * 128], scalar1=gmv[:, h, 0:1],
                    scalar2=grs[:, h:h + 1], op0=ALU.subtract, op1=ALU.mult), r=[bko, "gmv", "grs"], w=["xn"])
            P.op("pool", lambda e: e.tensor_tensor(out=br[:, 1, :], in0=xn[:], in1=sg[:], op=ALU.mult),
                 r=["xn", "sg"], w=["br1"])
            if A1_STAGE[0] == 4:
                return
            pba, pbka = G["psb"].get()
            for b in range(2):
                for kc in range(4):
                    P.op("pe", lambda e, b=b, kc=kc: e.transpose(
                        out=pba[:, (b * 4 + kc) * 128:(b * 4 + kc + 1) * 128], in_=br[:, b, kc * 128:(kc + 1) * 128],
                        identity=ident[:]), r=["br%d" % b, "ident"], w=[pbka])
            bt = "brT%d" % s
            P.op("act", lambda e: e.copy(out=brT[s][:, 0:8, :].rearrange("p a b -> p (a b)"), in_=pba[:]),
                 r=[pbka], w=[bt])
            pbc, pbkc = G["psb"].get()
            for kc in range(4):
                P.op("pe", lambda e, kc=kc: e.transpose(out=pbc[:, kc * 128:(kc + 1) * 128],
                                                        in_=br[:, 2, kc * 128:(kc + 1) * 128], identity=ident[:]),
                     r=["br2", "ident"], w=[pbkc])
            P.op("dve", lambda e: e.tensor_copy(out=brT[s][:, 8:12, :].rearrange("p a b -> p (a b)"), in_=pbc[:, 0:512]),
                 r=[pbkc], w=[bt])
            P.op("sp", lambda e: e.dma_start(out=brT_scr[t, :, :], in_=brT[s][:].rearrange("p a b -> p (a b)")),
                 r=[bt], dma="a1s%d" % s)

        loads(0)
        for t in range(n_tiles):
            do_tile(t)
        P.emit()


def phase_a2(nc, sync, G, n_tiles, x_d, xT_d, wg_d, bg_d, wbr_d, wout_d, ln1g_d, ln1b_d, brT_scr, h_scr):
    with ExitStack() as es:
        def sb(name, shape, dt):
            return es.enter_context(nc.sbuf_tensor("a2_" + name, shape, dt))
        ident = G["ident"]
        wg = sb("wg", [128, 8, NCOL_G], BF16)
        bg = sb("bg", [1, NCOL_G], BF16)
        ones = sb("ones", [1, 128], BF16)
        wbr = sb("wbr", [128, 12, D], BF16)
        wout = sb("wout", [128, 8, D], BF16)
        g1 = sb("g1", [128, D], F32)
        b1 = sb("b1", [128, D], F32)
        xT = [sb("xT%d" % i, [128, 8, 128], BF16) for i in range(2)]
        xt = [sb("x%d" % i, [128, D], F32) for i in range(2)]
        brT = [sb("brT%d" % i, [128, 12, 128], BF16) for i in range(2)]
        gsb = [sb("gsb%d" % i, [128, 512], F32) for i in range(2)]
        acc = sb("acc", [128, D], F32)
        tmp = [sb("tmp%d" % i, [128, 512], F32) for i in range(2)]
        mbf = sb("mbf", [128, D], BF16)
        mT = sb("mT", [128, 8, 128], BF16)
        z = [sb("z%d" % i, [128, D], F32) for i in range(2)]
        st6 = sb("st6", [128, 12], F32)
        mv2 = sb("mv2", [128, 2], F32)
        rstd = sb("rstd", [128, 1], F32)
        nmr = sb("nmr", [128, 1], F32)

        P = Prog(nc, sync)
        for c in range(8):
            P.op("pool", lambda e, c=c: e.dma_start(out=wg[:, c, :], in_=wg_d[c * 128:(c + 1) * 128, :],
                                                    max_dma_last_dim=4096), w=["wg"], dma="a2w")
        P.op("pool", lambda e: e.dma_start(out=bg[:], in_=bg_d[:, :], max_dma_last_dim=4096), w=["bg"], dma="a2w")
        for c in range(12):
            P.op("pool", lambda e, c=c: e.dma_start(out=wbr[:, c, :], in_=wbr_d[c * 128:(c + 1) * 128, :]),
                 w=["wbr"], dma="a2w")
        for c in range(8):
            P.op("pool", lambda e, c=c: e.dma_start(out=wout[:, c, :], in_=wout_d[c * 128:(c + 1) * 128, :]),
                 w=["wout"], dma="a2w")
        P.op("sp", lambda e: e.dma_start(out=g1[:], in_=ln1g_d[:, :]), w=["g1"], dma="a2p")
        P.op("sp", lambda e: e.dma_start(out=b1[:], in_=ln1b_d[:, :]), w=["b1"], dma="a2p")
        P.op("dve", lambda e: e.memset(ones[:], 1.0), w=["ones"])
        cnt = dict(g=0)

        def loads(t):
            s = t % 2
            P.op("pool", lambda e: e.dma_start(
                out=xT[s][:], in_=xT_d.rearrange("(c p) t -> p c t", p=128)[:, :, t * 128:(t + 1) * 128]),
                w=["xT%d" % s], dma="a2x%d" % s)
            P.op("sp", lambda e: e.dma_start(out=xt[s][:], in_=x_d[t * 128:(t + 1) * 128, :]),
                 w=["x%d" % s], dma="a2l%d" % s)
            P.op("sp", lambda e: e.dma_start(out=brT[s][:].rearrange("p a b -> p (a b)"), in_=brT_scr[t, :, :]),
                 w=["brT%d" % s], dma="a2l%d" % s)

        def do_tile(t):
            s = t % 2
            if t + 1 < n_tiles:
                loads(t + 1)
            xk, xtk, btk = "xT%d" % s, "x%d" % s, "brT%d" % s
            for b in range(3):
                for half in range(2):
                    col0 = b * 1024 + half * 512
                    gi = cnt["g"] % 2
                    cnt["g"] += 1
                    bgk, bkg = G["psA"].get()
                    for c in range(8):
                        P.op("pe", lambda e, c=c, col0=col0, bgk=bgk: e.matmul(
                            bgk[:], lhsT=xT[s][:, c, :], rhs=wg[:, c, col0:col0 + 512], start=(c == 0), stop=False),
                            r=["wg", xk], w=[bkg])
                    P.op("pe", lambda e, col0=col0, bgk=bgk: e.matmul(
                        bgk[:], lhsT=ones[0:1, :], rhs=bg[0:1, col0:col0 + 512], start=False, stop=True),
                        r=["bg", "ones"], w=[bkg])
                    P.op("act", lambda e, gi=gi, bgk=bgk: e.activation(out=gsb[gi][:], in_=bgk[:], func=AF.Sigmoid),
                         r=[bkg], w=["gsb%d" % gi])
                    by, bky = G["psA"].get()
                    for kc in range(4):
                        P.op("pe", lambda e, kc=kc, b=b, half=half, by=by: e.matmul(
                            by[:], lhsT=brT[s][:, b * 4 + kc, :], rhs=wbr[:, b * 4 + kc, half * 512:(half + 1) * 512],
                            start=(kc == 0), stop=(kc == 3)), r=["wbr", btk], w=[bky])
                    ak = "acc%d" % half
                    if b == 0:
                        P.op("dve", lambda e, gi=gi, half=half, by=by: e.tensor_tensor(
                            out=acc[:, half * 512:(half + 1) * 512], in0=by[:], in1=gsb[gi][:], op=ALU.mult),
                            r=[bky, "gsb%d" % gi], w=[ak])
                    else:
                        P.op("dve", lambda e, gi=gi, by=by: e.tensor_tensor(
                            out=tmp[gi][:], in0=by[:], in1=gsb[gi][:], op=ALU.mult),
                            r=[bky, "gsb%d" % gi], w=["tmp%d" % gi])
                        if b == 1:
                            P.op("pool", lambda e, gi=gi, half=half: e.tensor_tensor(
                                out=acc[:, half * 512:(half + 1) * 512], in0=acc[:, half * 512:(half + 1) * 512],
                                in1=tmp[gi][:], op=ALU.add), r=[ak, "tmp%d" % gi], w=[ak])
                        else:
                            P.op("pool", lambda e, gi=gi, half=half: e.tensor_tensor(
                                out=mbf[:, half * 512:(half + 1) * 512], in0=acc[:, half * 512:(half + 1) * 512],
                                in1=tmp[gi][:], op=ALU.add), r=[ak, "tmp%d" % gi], w=["mbf"])
            pb, pbk = G["psb"].get()
            for c in range(8):
                P.op("pe", lambda e, c=c: e.transpose(out=pb[:, c * 128:(c + 1) * 128], in_=mbf[:, c * 128:(c + 1) * 128],
                                                      identity=ident[:]), r=["mbf", "ident"], w=[pbk])
            P.op("act", lambda e: e.copy(out=mT[:].rearrange("p a b -> p (a b)"), in_=pb[:]), r=[pbk], w=["mT"])
            zk = "z%d" % s
            for half in range(2):
                bz, bkz = G["psA"].get()
                for c in range(8):
                    P.op("pe", lambda e, c=c, half=half, bz=bz: e.matmul(
                        bz[:], lhsT=mT[:, c, :], rhs=wout[:, c, half * 512:(half + 1) * 512],
                        start=(c == 0), stop=(c == 7)), r=["mT", "wout"], w=[bkz])
                P.op("dve", lambda e, half=half, bz=bz: e.scalar_tensor_tensor(
                    out=z[s][:, half * 512:(half + 1) * 512], in0=xt[s][:, half * 512:(half + 1) * 512],
                    scalar=ALPHA, in1=bz[:], op0=ALU.mult, op1=ALU.add), r=[xtk, bkz], w=[zk])
            layer_norm_tail(P, z[s], zk, g1, "g1", b1, "b1", st6, mv2, rstd, nmr, "a2", gb_eng="pool")
            P.op("sp", lambda e: e.dma_start(out=h_scr[t * 128:(t + 1) * 128, :], in_=z[s][:]), r=[zk],
                 dma="a2s%d" % s)

        loads(0)
        for t in range(n_tiles):
            do_tile(t)
        P.emit()


IN_SPECS = [
    ("x", lambda T: [T, D], F32), ("xT", lambda T: [D, T], F32), ("mem", lambda T: [256, D], F32),
    ("memg", lambda T: [128, D], F32), ("memb", lambda T: [128, D], F32), ("wkv", lambda T: [D, D], F32),
    ("w1", lambda T: [D, NCOL_A1], F32), ("b1", lambda T: [1, NCOL_A1], F32),
    ("cc", lambda T: [128, T], F32), ("ss", lambda T: [128, T], F32),
    ("dtab", lambda T: [128, 512], F32), ("wqt", lambda T: [128, 256], F32), ("wkt", lambda T: [128, 4], F32),
    ("cdt", lambda T: [128, 2], F32), ("mask", lambda T: [128, 256], F32), ("gng", lambda T: [128, 512], F32),
    ("sink", lambda T: [128, 8], F32),
    ("wg", lambda T: [D, NCOL_G], F32), ("bg", lambda T: [1, NCOL_G], F32), ("wbr", lambda T: [1536, D], F32),
    ("wout", lambda T: [D, D], F32), ("ln1g", lambda T: [128, D], F32), ("ln1b", lambda T: [128, D], F32),
    ("wpqT", lambda T: [2048, D], F32), ("skT", lambda T: [2048, 128], F32),
    ("peer_u", lambda T: [N_EXP, D], F32), ("peer_v", lambda T: [N_EXP, D], F32),
    ("ln2g", lambda T: [128, D], F32), ("ln2b", lambda T: [128, D], F32),
    ("identf", lambda T: [128, 128], F32), ("iota16", lambda T: [128, 16], F32),
]


def build_full(n_tiles, debug_h=False, stop_after=None):
    nc = bass.Bass("TRN2", target_bir_lowering=False)
    T = n_tiles * 128
    d = {}
    for name, shp, dtp in IN_SPECS:
        d[name] = nc.dram_tensor(name, shp(T), dtp, kind="ExternalInput").ap()
    y_d = nc.dram_tensor("y", [T, D], F32, kind="ExternalOutput").ap()
    h_scr = nc.dram_tensor("h_scr", [T, D], F32, kind="ExternalOutput" if debug_h else "Internal").ap()
    brT_scr = nc.dram_tensor("brT_scr", [n_tiles, 128, 1536], BF16,
                             kind="ExternalOutput" if debug_h else "Internal").ap()
    with ExitStack() as es:
        sync = Sync(nc, es)
        G = alloc_globals(nc, es, sync)
        G["psA"] = PsumPool(G["psf"].t + G["pv"], "psA")
        phase_const(nc, sync, G, d["identf"], d["iota16"])
        with ExitStack() as es1:
            mkT = es1.enter_context(nc.sbuf_tensor("sb_mkT", [128, 4, 256], BF16))
            mv_aug = es1.enter_context(nc.sbuf_tensor("sb_mvaug", [128, 2, 4, 129], BF16))
            phase_a0(nc, sync, G, mkT, mv_aug, d["mem"], d["memg"], d["memb"], d["wkv"])
            if stop_after == "a0":
                return nc
            phase_a1(nc, sync, G, n_tiles, mkT, mv_aug, d["xT"], d["w1"], d["b1"], d["cc"], d["ss"], d["dtab"],
                     d["wqt"], d["wkt"], d["cdt"], d["mask"], d["gng"], d["sink"], brT_scr)
        if stop_after == "a1":
            return nc
        phase_a2(nc, sync, G, n_tiles, d["x"], d["xT"], d["wg"], d["bg"], d["wbr"], d["wout"], d["ln1g"],
                 d["ln1b"], brT_scr, h_scr)
        if stop_after == "a2":
            return nc
        uv_scr = nc.dram_tensor("uv_scr", [N_EXP, 2 * D], BF16, kind="Internal").ap()
        phase_bt(nc, sync, G, d["peer_u"], d["peer_v"], uv_scr)
        with ExitStack() as es2:
            weff = es2.enter_context(nc.sbuf_tensor("sb_weff", [128, 8, 2048], BF16))
            phase_b0(nc, sync, G, weff, d["wpqT"], d["skT"])
            phase_b(nc, sync, G, n_tiles, weff, h_scr, uv_scr, d["ln2g"], d["ln2b"], y_d)
    return nc


def _w_in_cols():
    fm = list(range(0, 512))
    fm += list(range(512, 576)) * 2 + list(range(576, 640)) * 2
    rq0, rk0 = 768, 1024
    fm += list(range(rq0, rq0 + 256)) + list(range(rk0, rk0 + 256))
    sw = []
    for base in (rq0, rk0):
        for h in range(4):
            hb = base + 64 * h
            sw += list(range(hb + 32, hb + 64)) + list(range(hb, hb + 32))
    fm += sw
    fm += list(range(2304, 2816))
    tm = list(range(640, 768)) + list(range(1280, 1792)) + list(range(1792, 2304))
    return np.array(fm + tm), np.arange(2816, 5888)


def _const_tables(T):
    half = 32
    theta = (1.0 / np.power(np.float32(10000.0), np.linspace(0.0, 1.0, half, dtype=np.float32))).astype(np.float32)
    pos = np.arange(T, dtype=np.float32)
    ang = (pos[:, None] * theta[None, :]).astype(np.float32)
    cos, sin = np.cos(ang).astype(np.float32), np.sin(ang).astype(np.float32)
    p = np.arange(128)
    cc = np.ascontiguousarray(cos[:, p % 32].T)
    sgn = np.where((p % 64) < 32, -1.0, 1.0).astype(np.float32)
    ss = np.ascontiguousarray((sin[:, p % 32] * sgn[None, :]).T)
    lg = np.log(1.0 - 2.0 ** (-5.0 - np.arange(4, dtype=np.float64)))
    i = np.arange(128, dtype=np.float64)
    diff = i[None, :] - i[:, None]
    dt = np.zeros((128, 4, 128), np.float64)
    for h in range(4):
        dt[:, h, :] = np.where(diff >= 0, np.exp(lg[h] * np.maximum(diff, 0.0)), 0.0) * 0.125
    wq = np.zeros((128, 2, 128), np.float64)
    cd = np.zeros((128, 2), np.float64)
    for blk in range(2):
        for hf in range(2):
            h = blk * 2 + hf
            wq[hf * 64:(hf + 1) * 64, blk, :] = np.exp(lg[h] * (i + 1.0))[None, :]
            cd[hf * 64:(hf + 1) * 64, blk] = np.exp(lg[h] * 128.0)
    wk = np.zeros((128, 4), np.float64)
    for h in range(4):
        wk[:, h] = np.exp(lg[h] * (127.0 - i)) * 0.125
    k = np.arange(128)[:, None]
    q = np.arange(128)[None, :]
    mask = np.concatenate([(k <= q), (k > q)], axis=1).astype(np.float32)
    f = lambda a: np.ascontiguousarray(a.astype(np.float32))
    return dict(cc=cc, ss=ss, dtab=f(dt.reshape(128, 512)), wqt=f(wq.reshape(128, 256)), wkt=f(wk), cdt=f(cd),
                mask=mask, identf=np.eye(128, dtype=np.float32),
                iota16=f(np.broadcast_to(np.arange(16.0), (128, 16))))


def _rep(v, n=128):
    return np.ascontiguousarray(np.broadcast_to(np.asarray(v, np.float32)[None, :], (n, v.shape[-1])))


def host_prep(inputs, T):
    g = lambda n: np.asarray(inputs[n], np.float32)[0]
    c1, cg = _w_in_cols()
    w_in, b_in = g("w_in"), g("b_in")
    sh = dict(
        memg=_rep(g("mem_ln_g")), memb=_rep(g("mem_ln_b")), wkv=g("w_mem_kv"),
        w1=np.ascontiguousarray(w_in[:, c1]), b1=np.ascontiguousarray(b_in[c1][None, :]),
        gng=_rep(g("ret_gn_g")), sink=_rep(g("attn_sinks")),
        wg=np.ascontiguousarray(w_in[:, cg]), bg=np.ascontiguousarray(b_in[cg][None, :]),
        wbr=np.ascontiguousarray(np.concatenate([g("w_branch_attn"), g("w_branch_ret"), g("w_branch_mem")], axis=0)),
        wout=g("w_out"), ln1g=_rep(g("ln1_g")), ln1b=_rep(g("ln1_b")),
        wpqT=np.ascontiguousarray(g("w_peer_q").T),
        skT=np.ascontiguousarray(g("peer_sub_keys").transpose(0, 1, 3, 2).reshape(2048, 128)),
        peer_u=g("peer_u"), peer_v=g("peer_v"), ln2g=_rep(g("ln2_g")), ln2b=_rep(g("ln2_b")),
    )
    sh.update(_const_tables(T))
    return sh


def kernel(**inputs):
    x = np.asarray(inputs["x"], np.float32)
    mem = np.asarray(inputs["mem"], np.float32)
    B, S, _ = x.shape
    n_tiles = S // 128
    sh = host_prep(inputs, S)
    in_maps = []
    for b in range(B):
        m = dict(sh)
        m["x"] = np.ascontiguousarray(x[b])
        m["xT"] = np.ascontiguousarray(x[b].T)
        m["mem"] = np.ascontiguousarray(mem[b])
        in_maps.append(m)
    nc = build_full(n_tiles, stop_after=_os.environ.get('MK_STOP'))
    res = run_bass_kernel_spmd(nc, in_maps, core_ids=list(range(B)))
    return np.stack([r["y"] for r in res.results], axis=0).astype(np.float32)
```

```python
import numpy as np
import ml_dtypes
from contextlib import ExitStack

import concourse.bass as bass
import concourse.mybir as mybir
from concourse.bass_utils import run_bass_kernel_spmd

F32 = mybir.dt.float32
BF16 = mybir.dt.bfloat16
I32 = mybir.dt.int32
U32 = mybir.dt.uint32
AF = mybir.ActivationFunctionType
ALU = mybir.AluOpType
AX = mybir.AxisListType

D = 1024
SEQ = 8192
NCORES = 8
ALPHA = 2.0 ** 0.25
LN_EPS = 1e-5
N_EXP = 16384

FM_COLS = 18 * 128
TM_A1 = 128 + 512 + 512
NCOL_A1 = FM_COLS + TM_A1
NCOL_G = 3072

SAME_ENGINE_SYNC = True
EPS_T = [None]
import os as _os
A1_STAGE = [int(_os.environ.get('A1_STAGE', '0'))]


class Sync:
    def __init__(self, nc, es):
        self.nc = nc
        self.es = es
        self.sems = {}
        self.cnt = {}

    def sem(self, name):
        if name not in self.sems:
            self.sems[name] = self.es.enter_context(self.nc.semaphore("s_" + name))
            self.cnt[name] = 0
        return self.sems[name]


ENGS = ("pe", "act", "dve", "pool", "sp")


class Prog:
    def __init__(self, nc, sync):
        self.nc = nc
        self.sync = sync
        self.ops = []
        self.lastw = {}
        self.readers = {}

    def op(self, eng, fn, r=(), w=(), dma=None):
        i = len(self.ops)
        deps = set()
        for k in r:
            if k in self.lastw:
                deps.add(self.lastw[k])
        for k in w:
            if k in self.lastw:
                deps.add(self.lastw[k])
            deps.update(self.readers.get(k, ()))
        for k in r:
            self.readers.setdefault(k, []).append(i)
        for k in w:
            self.lastw[k] = i
            self.readers[k] = []
        deps.discard(i)
        self.ops.append(dict(eng=eng, fn=fn, deps=deps, dma=dma))
        return i

    def emit(self):
        nc, sync, ops = self.nc, self.sync, self.ops
        n = len(ops)
        per_eng = {e: [] for e in ENGS}
        for i, o in enumerate(ops):
            per_eng[o["eng"]].append(i)
        need = [False] * n
        for i, o in enumerate(ops):
            for d in o["deps"]:
                od = ops[d]
                if od["dma"] is not None:
                    continue
                if od["eng"] == o["eng"] and o["dma"] is None:
                    if o["eng"] == "pe" or not SAME_ENGINE_SYNC:
                        continue
                need[d] = True
        for e in ENGS:
            for i in reversed(per_eng[e]):
                if ops[i]["dma"] is None:
                    need[i] = True
                    break
        mark = [None] * n
        for i, o in enumerate(ops):
            if o["dma"] is not None:
                nm = "d_" + o["dma"]
                sync.sem(nm)
                sync.cnt[nm] += 16
                mark[i] = (nm, sync.cnt[nm])
            elif need[i]:
                nm = "e_" + o["eng"]
                sync.sem(nm)
                sync.cnt[nm] += 1
                mark[i] = (nm, sync.cnt[nm])
        final = {nm: sync.cnt[nm] for nm in sync.cnt}
        waits = [None] * n
        for i, o in enumerate(ops):
            wl = {}
            for d in o["deps"]:
                od = ops[d]
                if od["dma"] is None and od["eng"] == o["eng"] and o["dma"] is None:
                    if o["eng"] == "pe" or not SAME_ENGINE_SYNC:
                        continue
                nm, v = mark[d]
                wl[nm] = max(wl.get(nm, 0), v)
            waits[i] = wl
        with nc.Block() as block:
            decos = {"pe": block.tensor, "act": block.scalar, "dve": block.vector,
                     "pool": block.gpsimd, "sp": block.sync}
            for eng in ENGS:
                idxs = per_eng[eng]

                def body(e, idxs=idxs):
                    waited = {}
                    for i in idxs:
                        o = ops[i]
                        for nm, v in waits[i].items():
                            if waited.get(nm, 0) >= v:
                                continue
                            e.wait_ge(sync.sems[nm], v)
                            waited[nm] = v
                        ins = o["fn"](e)
                        if mark[i] is not None:
                            nm, v = mark[i]
                            ins.then_inc(sync.sems[nm], 16 if o["dma"] is not None else 1)
                    for nm, v in final.items():
                        if v > 0 and waited.get(nm, 0) < v:
                            e.wait_ge(sync.sems[nm], v)

                decos[eng](body)


class PsumPool:
    def __init__(self, tensors, prefix):
        self.t = tensors
        self.prefix = prefix
        self.i = 0

    def get(self):
        k = self.i % len(self.t)
        self.i += 1
        return self.t[k], "%s%d" % (self.prefix, k)


def bcast(ap, axis, shape):
    return ap.unsqueeze(axis).to_broadcast(list(shape))


def phase_b0(nc, sync, G, weff, wpqT_d, skT_d):
    with ExitStack() as es:
        wq = es.enter_context(nc.sbuf_tensor("b0_wq", [128, 16, 1024], F32))
        sk = es.enter_context(nc.sbuf_tensor("b0_sk", [128, 16, 128], F32))
        P = Prog(nc, sync)
        for q in range(4):
            P.op("sp", lambda e, q=q: e.dma_start(
                out=wq[:, q * 4:(q + 1) * 4, :],
                in_=wpqT_d.rearrange("(g p) m -> p g m", p=128)[:, q * 4:(q + 1) * 4, :]),
                w=["wq"], dma="b0w")
        P.op("sp", lambda e: e.dma_start(out=sk[:], in_=skT_d.rearrange("(g p) n -> p g n", p=128)),
             w=["sk"], dma="b0w")
        for mc in range(8):
            for q in range(4):
                bank, bk = G["psf"].get()
                for j in range(4):
                    hp = q * 4 + j
                    P.op("pe", lambda e, bank=bank, hp=hp, mc=mc, j=j: e.matmul(
                        bank[:, j * 128:(j + 1) * 128], lhsT=wq[:, hp, mc * 128:(mc + 1) * 128],
                        rhs=sk[:, hp, :], start=True, stop=True), r=["wq", "sk"], w=[bk])
                eng = "act" if (mc * 4 + q) % 2 == 0 else "dve"
                if eng == "act":
                    P.op("act", lambda e, bank=bank, mc=mc, q=q: e.copy(
                        out=weff[:, mc, q * 512:(q + 1) * 512], in_=bank[:]), r=[bk], w=["weff"])
                else:
                    P.op("dve", lambda e, bank=bank, mc=mc, q=q: e.tensor_copy(
                        out=weff[:, mc, q * 512:(q + 1) * 512], in_=bank[:]), r=[bk], w=["weff"])
        P.emit()


def phase_bt(nc, sync, G, peer_u, peer_v, uv_scr):
    with ExitStack() as es:
        uv = [es.enter_context(nc.sbuf_tensor("bt_uv%d" % i, [128, 8, 2 * D], BF16)) for i in range(2)]
        P = Prog(nc, sync)
        for ch in range(16):
            bt_chunk(P, ch, uv, peer_u, peer_v, uv_scr)
        P.emit()


def bt_chunk(P, ch, uv, peer_u, peer_v, uv_scr):
    s = ch % 2
    k = "uv%d" % s
    rows = slice(ch * 1024, (ch + 1) * 1024)
    P.op("pool", lambda e: e.dma_start(
        out=uv[s][:, :, 0:D], in_=peer_u[rows, :].rearrange("(p r) d -> p r d", r=8)), w=[k], dma="btl%d" % s)
    P.op("pool", lambda e: e.dma_start(
        out=uv[s][:, :, D:2 * D], in_=peer_v[rows, :].rearrange("(p r) d -> p r d", r=8)), w=[k], dma="btl%d" % s)
    P.op("sp", lambda e: e.dma_start(
        out=uv_scr[rows, :].rearrange("(p r) d -> p r d", r=8), in_=uv[s][:]), r=[k], dma="bts%d" % s)


def phase_b(nc, sync, G, n_tiles, weff, h_scr, uv_scr, ln2g_d, ln2b_d, y_d, NS=int(_os.environ.get('NS', '16')), ND=4, GS=4, NP=int(_os.environ.get('NP', '4')),
            DSPLIT=int(_os.environ.get('DSPLIT', '0'))):
    with ExitStack() as es:
        def sb(name, shape, dt):
            return es.enter_context(nc.sbuf_tensor("b_" + name, shape, dt))
        ident, identf, iota16 = G["ident"], G["identf"], G["iota16"]
        g2 = sb("g2", [128, D], F32)
        b2 = sb("b2", [128, D], F32)
        h_t = [sb("h%d" % i, [128, D], F32) for i in range(2)]
        hb = [sb("hb%d" % i, [128, D], BF16) for i in range(2)]
        prod = [sb("prod%d" % i, [128, D], BF16) for i in range(NP)]
        junk2 = sb("junk2", [128, D], BF16)
        hT = sb("hT", [128, 8, 128], BF16)
        sc = sb("sc", [128, 16, 128], F32)
        sv = sb("sv", [128, 16, 16], F32)
        si = sb("si", [128, 16, 16], U32)
        sif = sb("sif", [128, 16, 16], F32)
        cand = sb("cand", [128, 8, 256], F32)
        ts = sb("ts", [128, 8, 16], F32)
        pos = sb("pos", [128, 8, 16], U32)
        pij = sb("pij", [128, 2, 128], U32)
        pijf = sb("pijf", [128, 2, 8, 16], F32)
        oh = [sb("oh%d" % i, [128, 8, 16, 16], BF16) for i in range(2)]
        ee = sb("ee", [128, 2, 128], F32)
        ef = sb("ef", [128, 128], F32)
        eidx = [sb("eidx%d" % i, [128, 128], I32) for i in range(2)]
        dsm = sb("dsm", [128, 8, 16], F32)
        ex = sb("ex", [128, 8, 16], F32)
        ssum = sb("ssum", [128, 8], F32)
        gate = [sb("gate%d" % i, [128, 128], F32) for i in range(2)]
        dots = sb("dots", [128, 128], F32)
        actt = sb("actt", [128, 128], F32)
        wt = sb("wt", [128, 128], F32)
        junk = sb("junk", [128, D], BF16)
        uvb = [sb("uvb%d" % i, [128, 2 * D], BF16) for i in range(NS)]
        dg = [sb("dg%d" % i, [128, 128], BF16) for i in range(ND)]
        y_t = [sb("y%d" % i, [128, D], F32) for i in range(2)]
        st6 = sb("st6", [128, 12], F32)
        mv2 = sb("mv2", [128, 2], F32)
        rstd = sb("rstd", [128, 1], F32)
        nmr = sb("nmr", [128, 1], F32)
        pv = G["pv"]
        NG = 128 // GS

        P = Prog(nc, sync)
        P.op("sp", lambda e: e.dma_start(out=g2[:], in_=ln2g_d[:, :]), w=["g2"], dma="bw")
        P.op("sp", lambda e: e.dma_start(out=b2[:], in_=ln2b_d[:, :]), w=["b2"], dma="bw")
        cnt = dict(u=0, d=0, p=0, q=0)
        scpool = PsumPool(G["psf"].t[0:2], "psf")
        dq = [(G["psf"].t[2 + i // 4], (i % 4) * 128) for i in range(8)]

        def load_h(t):
            s = t % 2
            P.op("sp", lambda e: e.dma_start(out=h_t[s][:], in_=h_scr[t * 128:(t + 1) * 128, :]),
                 w=["h%d" % s], dma="hld%d" % s)

        def front(t, OP=None):
            OP = OP or P.op
            s = t % 2
            hk_ = "h%d" % s
            OP("act", lambda e: e.copy(out=hb[s][:], in_=h_t[s][:]), r=[hk_], w=["hb%d" % s])
            pb, pbk = G["psb"].get()
            for c in range(8):
                OP("pe", lambda e, c=c: e.transpose(out=pb[:, c * 128:(c + 1) * 128],
                                                       in_=hb[s][:, c * 128:(c + 1) * 128], identity=ident[:]),
                     r=["hb%d" % s, "ident"], w=[pbk])
            OP("act", lambda e: e.copy(out=hT[:].rearrange("p c t -> p (c t)"), in_=pb[:]), r=[pbk], w=["hT"])
            for nb in range(4):
                bank, bk = scpool.get()
                for c in range(8):
                    OP("pe", lambda e, c=c, nb=nb, bank=bank: e.matmul(
                        bank[:], lhsT=hT[:, c, :], rhs=weff[:, c, nb * 512:(nb + 1) * 512],
                        start=(c == 0), stop=(c == 7)), r=["hT", "weff"], w=[bk])
                for q in range(4):
                    g = nb * 4 + q
                    OP("act", lambda e, q=q, g=g, bank=bank: e.copy(out=sc[:, g, :], in_=bank[:, q * 128:(q + 1) * 128]),
                         r=[bk], w=["sc%d" % g])
            for g in range(16):
                OP("dve", lambda e, g=g: e.max(out=sv[:, g, 0:8], in_=sc[:, g, :]), r=["sc%d" % g], w=["sva%d" % g])
            for g in range(16):
                OP("dve", lambda e, g=g: e.max_index(out=si[:, g, 0:8], in_max=sv[:, g, 0:8], in_values=sc[:, g, :]),
                     r=["sc%d" % g, "sva%d" % g], w=["sia%d" % g])
            for g in range(16):
                OP("dve", lambda e, g=g: e.match_replace(out=sc[:, g, :], in_to_replace=sv[:, g, 0:8],
                                                           in_values=sc[:, g, :], imm_value=-1e30),
                     r=["sva%d" % g], w=["sc%d" % g])
            for g in range(16):
                OP("dve", lambda e, g=g: e.max(out=sv[:, g, 8:16], in_=sc[:, g, :]), r=["sc%d" % g], w=["svb%d" % g])
            for g in range(16):
                OP("dve", lambda e, g=g: e.max_index(out=si[:, g, 8:16], in_max=sv[:, g, 8:16], in_values=sc[:, g, :]),
                     r=["sc%d" % g, "svb%d" % g], w=["sib%d" % g])
            allsv = ["sva%d" % g for g in range(16)] + ["svb%d" % g for g in range(16)]
            allsi = ["sia%d" % g for g in range(16)] + ["sib%d" % g for g in range(16)]
            OP("dve", lambda e: e.tensor_copy(out=sif[:], in_=si[:]), r=allsi, w=["sif"])
            sv4 = sv[:].rearrange("p (h two) k -> p h two k", two=2)
            OP("dve", lambda e: e.tensor_tensor(
                out=cand[:].rearrange("p h (i j) -> p h i j", j=16),
                in0=bcast(sv4[:, :, 0, :], 3, [128, 8, 16, 16]),
                in1=bcast(sv4[:, :, 1, :], 2, [128, 8, 16, 16]), op=ALU.add), r=allsv,
                w=["cand%d" % h for h in range(8)])
            for h in range(8):
                OP("dve", lambda e, h=h: e.max(out=ts[:, h, 0:8], in_=cand[:, h, :]), r=["cand%d" % h], w=["tsa%d" % h])
            for h in range(8):
                OP("dve", lambda e, h=h: e.max_index(out=pos[:, h, 0:8], in_max=ts[:, h, 0:8], in_values=cand[:, h, :]),
                     r=["cand%d" % h, "tsa%d" % h], w=["posa%d" % h])
            for h in range(8):
                OP("dve", lambda e, h=h: e.match_replace(out=cand[:, h, :], in_to_replace=ts[:, h, 0:8],
                                                           in_values=cand[:, h, :], imm_value=-1e30),
                     r=["tsa%d" % h], w=["cand%d" % h])
            for h in range(8):
                OP("dve", lambda e, h=h: e.max(out=ts[:, h, 8:16], in_=cand[:, h, :]), r=["cand%d" % h], w=["tsb%d" % h])
            for h in range(8):
                OP("dve", lambda e, h=h: e.max_index(out=pos[:, h, 8:16], in_max=ts[:, h, 8:16], in_values=cand[:, h, :]),
                     r=["cand%d" % h, "tsb%d" % h], w=["posb%d" % h])
            allts = ["tsa%d" % h for h in range(8)] + ["tsb%d" % h for h in range(8)]
            allpos = ["posa%d" % h for h in range(8)] + ["posb%d" % h for h in range(8)]
            posf = pos[:].rearrange("p h k -> p (h k)")
            OP("dve", lambda e: e.tensor_single_scalar(out=pij[:, 0, :], in_=posf, scalar=4,
                                                         op=ALU.logical_shift_right), r=allpos, w=["pij0"])
            OP("dve", lambda e: e.tensor_single_scalar(out=pij[:, 1, :], in_=posf, scalar=15,
                                                         op=ALU.bitwise_and), r=allpos, w=["pij1"])
            OP("dve", lambda e: e.tensor_copy(out=pijf[:].rearrange("p a h k -> p a (h k)"), in_=pij[:]),
                 r=["pij0", "pij1"], w=["pijf"])
            sif4 = sif[:].rearrange("p (h two) k -> p h two k", two=2)
            for a in range(2):
                OP("dve", lambda e, a=a: e.tensor_tensor(
                    out=oh[a][:], in0=bcast(pijf[:, a, :, :], 3, [128, 8, 16, 16]),
                    in1=iota16[:].unsqueeze(1).unsqueeze(1).to_broadcast([128, 8, 16, 16]),
                    op=ALU.is_equal), r=["pijf"], w=["oh%d" % a])
            for a in range(2):
                OP("dve", lambda e, a=a: e.tensor_tensor(
                    out=oh[a][:], in0=oh[a][:], in1=bcast(sif4[:, :, a, :], 2, [128, 8, 16, 16]),
                    op=ALU.mult), r=["sif"], w=["oh%d" % a])
            for a in range(2):
                OP("dve", lambda e, a=a: e.tensor_reduce(
                    out=ee[:, a, :], in_=oh[a][:].rearrange("p h k i -> p (h k) i"), axis=AX.X, op=ALU.add),
                    r=["oh%d" % a], w=["ee%d" % a])
            OP("dve", lambda e: e.scalar_tensor_tensor(out=ef[:], in0=ee[:, 0, :], scalar=128.0, in1=ee[:, 1, :],
                                                         op0=ALU.mult, op1=ALU.add), r=["ee0", "ee1"], w=["ef"])
            ek = "eidx%d" % s
            OP("dve", lambda e: e.tensor_copy(out=eidx[s][:], in_=ef[:]), r=["ef"], w=[ek])
            OP("dve", lambda e: e.tensor_tensor(out=dsm[:], in0=ts[:], in1=ts[:, :, 0:1].to_broadcast([128, 8, 16]),
                                                  op=ALU.subtract), r=allts, w=["dsm"])
            OP("act", lambda e: e.activation(out=ex[:], in_=dsm[:], func=AF.Exp), r=["dsm"], w=["ex"])
            OP("dve", lambda e: e.tensor_reduce(out=ssum[:], in_=ex[:], axis=AX.X, op=ALU.add), r=["ex"], w=["ssum"])
            OP("dve", lambda e: e.reciprocal(out=ssum[:], in_=ssum[:]), r=["ssum"], w=["ssum"])
            OP("dve", lambda e: e.tensor_tensor(out=gate[s][:].rearrange("p (h k) -> p h k", k=16), in0=ex[:],
                                                  in1=ssum[:].unsqueeze(2).to_broadcast([128, 8, 16]), op=ALU.mult),
                 r=["ex", "ssum"], w=["gate%d" % s])

        def s1(t, gq):
            s = t % 2
            ek = "eidx%d" % s
            cols = slice(gq * GS, (gq + 1) * GS)
            slots = []
            accs = []
            for hk in range(gq * GS, (gq + 1) * GS):
                u = cnt["u"] % NS
                cnt["u"] += 1
                slots.append(u)
                pi = cnt["p"] % NP
                cnt["p"] += 1
                P.op("pool", lambda e, u=u, hk=hk: e.indirect_dma_start(
                    out=uvb[u][:], out_offset=None, in_=uv_scr[:, :],
                    in_offset=bass.IndirectOffsetOnAxis(ap=eidx[s][:, hk:hk + 1], axis=0)),
                    r=[ek], w=["uvb%d" % u], dma="gu%d" % u)
                hk_ = "h%d" % s
                if (hk % GS) < DSPLIT:
                    P.op("dve", lambda e, u=u, pi=pi: e.tensor_tensor(
                        out=prod[pi][:], in0=uvb[u][:, 0:D], in1=hb[s][:], op=ALU.mult),
                        r=["uvb%d" % u, "hb%d" % s], w=["prod%d" % pi])
                    accs.append((pi, hk))
                else:
                    P.op("dve", lambda e, u=u, hk=hk: e.scalar_tensor_tensor(
                        out=junk[:], in0=uvb[u][:, 0:D], scalar=1.0, in1=h_t[s][:],
                        op0=ALU.mult, op1=ALU.mult, accum_out=dots[:, hk:hk + 1]),
                        r=["uvb%d" % u, hk_], w=["dots%d" % gq])
            for (pi, hk) in accs:
                P.op("act", lambda e, pi=pi, hk=hk: e.activation(
                    out=junk2[:], in_=prod[pi][:], func=AF.Copy, accum_out=dots[:, hk:hk + 1]),
                    r=["prod%d" % pi], w=["dots%d" % gq])
            P.op("act", lambda e: e.activation(out=actt[:, cols], in_=dots[:, cols], func=AF.Gelu),
                 r=["dots%d" % gq], w=["actt%d" % gq])
            return slots

        def s2(t, gq, slots):
            s = t % 2
            cols = slice(gq * GS, (gq + 1) * GS)
            P.op("dve", lambda e: e.tensor_tensor(out=wt[:, cols], in0=gate[s][:, cols], in1=actt[:, cols], op=ALU.mult),
                 r=["gate%d" % s, "actt%d" % gq], w=["wt%d" % gq])
            for j, hk in enumerate(range(gq * GS, (gq + 1) * GS)):
                u = slots[j]
                d = cnt["d"] % ND
                cnt["d"] += 1
                P.op("act", lambda e, d=d, hk=hk: e.activation(out=dg[d][:], in_=identf[:], func=AF.Copy,
                                                               scale=wt[:, hk:hk + 1]),
                     r=["wt%d" % gq, "identf"], w=["dg%d" % d])
                for half in range(2):
                    P.op("pe", lambda e, d=d, u=u, half=half, hk=hk: e.matmul(
                        pv[half][:], lhsT=dg[d][:], rhs=uvb[u][:, D + half * 512:D + (half + 1) * 512],
                        start=(hk == 0), stop=(hk == 127)), r=["dg%d" % d, "uvb%d" % u], w=["pv%d" % half])

        def tail(t):
            s = t % 2
            hk_, yk = "h%d" % s, "y%d" % s
            for half in range(2):
                P.op("dve", lambda e, half=half: e.scalar_tensor_tensor(
                    out=y_t[s][:, half * 512:(half + 1) * 512], in0=h_t[s][:, half * 512:(half + 1) * 512],
                    scalar=ALPHA, in1=pv[half][:], op0=ALU.mult, op1=ALU.add),
                    r=[hk_, "pv%d" % half], w=[yk])
            layer_norm_tail(P, y_t[s], yk, g2, "g2", b2, "b2", st6, mv2, rstd, nmr, "b", gb_eng="dve")
            P.op("sp", lambda e: e.dma_start(out=y_d[t * 128:(t + 1) * 128, :], in_=y_t[s][:]),
                 r=[yk], dma="yst%d" % s)

        load_h(0)
        front(0)
        for t in range(n_tiles):
            if t + 1 < n_tiles:
                load_h(t + 1)
            pend = None
            todo = []
            if t + 1 < n_tiles:
                front(t + 1, OP=lambda *a, **k: todo.append((a, k)))
            per = -(-len(todo) // max(1, NG - 4))
            for gq in range(NG):
                slots = s1(t, gq)
                if pend is not None:
                    s2(t, *pend)
                pend = (gq, slots)
                for (a, k) in todo[:per]:
                    P.op(*a, **k)
                del todo[:per]
            s2(t, *pend)
            for (a, k) in todo:
                P.op(*a, **k)
            tail(t)
        P.emit()


def layer_norm_tail(P, z, zk, g, gk, b, bk, st6, mv2, rstd, nmr, pfx, gb_eng="pool"):
    k6, k2, kr, kn = pfx + "st6", pfx + "mv2", pfx + "rstd", pfx + "nmr"
    for half in range(2):
        P.op("dve", lambda e, half=half: e.bn_stats(out=st6[:, half * 6:(half + 1) * 6],
                                                     in_=z[:, half * 512:(half + 1) * 512]), r=[zk], w=[k6])
    P.op("dve", lambda e: e.bn_aggr(out=mv2[:], in_=st6[:]), r=[k6], w=[k2])
    P.op("act", lambda e: e.activation(out=rstd[:], in_=mv2[:, 1:2], func=AF.Sqrt, bias=EPS_T[0][:], scale=1.0),
         r=[k2, "eps"], w=[kr])
    P.op("dve", lambda e: e.reciprocal(out=rstd[:], in_=rstd[:]), r=[kr], w=[kr])
    P.op("dve", lambda e: e.scalar_tensor_tensor(out=nmr[:], in0=mv2[:, 0:1], scalar=-1.0, in1=rstd[:],
                                                 op0=ALU.mult, op1=ALU.mult), r=[k2, kr], w=[kn])
    P.op("act", lambda e: e.activation(out=z[:], in_=z[:], func=AF.Identity, bias=nmr[:], scale=rstd[:]),
         r=[zk, kr, kn], w=[zk])
    P.op(gb_eng, lambda e: e.tensor_tensor(out=z[:], in0=z[:], in1=g[:], op=ALU.mult), r=[zk, gk], w=[zk])
    P.op(gb_eng, lambda e: e.tensor_tensor(out=z[:], in0=z[:], in1=b[:], op=ALU.add), r=[zk, bk], w=[zk])


def alloc_globals(nc, es, sync):
    G = {}
    psf = [es.enter_context(nc.psum_tensor("psf%d" % i, [128, 512], F32)) for i in range(4)]
    pv = [es.enter_context(nc.psum_tensor("pv%d" % i, [128, 512], F32)) for i in range(2)]
    psb = [es.enter_context(nc.psum_tensor("psb%d" % i, [128, 1024], BF16)) for i in range(2)]
    G["psf"] = PsumPool(psf, "psf")
    G["psb"] = PsumPool(psb, "psb")
    G["pv"] = pv
    G["ident"] = es.enter_context(nc.sbuf_tensor("sb_ident", [128, 128], BF16))
    G["identf"] = es.enter_context(nc.sbuf_tensor("sb_identf", [128, 128], F32))
    G["iota16"] = es.enter_context(nc.sbuf_tensor("sb_iota16", [128, 16], F32))
    G["eps"] = es.enter_context(nc.sbuf_tensor("sb_eps", [128, 1], F32))
    EPS_T[0] = G["eps"]
    return G


def phase_const(nc, sync, G, identf_d, iota16_d):
    P = Prog(nc, sync)
    P.op("sp", lambda e: e.dma_start(out=G["identf"][:], in_=identf_d[:, :]), w=["identf"], dma="c0")
    P.op("sp", lambda e: e.dma_start(out=G["iota16"][:], in_=iota16_d[:, :]), w=["iota16"], dma="c0")
    P.op("dve", lambda e: e.tensor_copy(out=G["ident"][:], in_=G["identf"][:]), r=["identf"], w=["ident"])
    P.op("dve", lambda e: e.memset(G["eps"][:], LN_EPS), w=["eps"])
    P.emit()


def build_b_only(n_tiles, tab_dt=F32):
    nc = bass.Bass("TRN2", target_bir_lowering=False)
    T = n_tiles * 128
    dt = lambda name, shape, dtype, kind="ExternalInput": nc.dram_tensor(name, shape, dtype, kind=kind).ap()
    h_d = dt("h_in", [T, D], F32)
    wpqT_d = dt("wpqT", [2048, D], F32)
    skT_d = dt("skT", [2048, 128], F32)
    pu = dt("peer_u", [N_EXP, D], tab_dt)
    pvv = dt("peer_v", [N_EXP, D], tab_dt)
    g2 = dt("ln2g", [128, D], F32)
    b2 = dt("ln2b", [128, D], F32)
    identf_d = dt("identf", [128, 128], F32)
    iota_d = dt("iota16", [128, 16], F32)
    y_d = dt("y", [T, D], F32, kind="ExternalOutput")
    with ExitStack() as es:
        sync = Sync(nc, es)
        G = alloc_globals(nc, es, sync)
        phase_const(nc, sync, G, identf_d, iota_d)
        weff = es.enter_context(nc.sbuf_tensor("sb_weff", [128, 8, 2048], BF16))
        phase_b0(nc, sync, G, weff, wpqT_d, skT_d)
        uv_scr = nc.dram_tensor("uv_scr", [N_EXP, 2 * D], BF16, kind="Internal").ap()
        phase_bt(nc, sync, G, pu, pvv, uv_scr)
        phase_b(nc, sync, G, n_tiles, weff, h_d, uv_scr, g2, b2, y_d)
    return nc


def phase_a0(nc, sync, G, mkT, mv_aug, mem_d, memg_d, memb_d, wkv_d):
    with ExitStack() as es:
        def sb(name, shape, dt):
            return es.enter_context(nc.sbuf_tensor("a0_" + name, shape, dt))
        ident = G["ident"]
        wkv = sb("wkv", [128, 8, 1024], BF16)
        mg = sb("mg", [128, D], F32)
        mb = sb("mb", [128, D], F32)
        mt = [sb("mt%d" % i, [128, D], F32) for i in range(2)]
        mn = sb("mn", [128, D], BF16)
        mnT = sb("mnT", [128, 8, 256], BF16)
        st6 = sb("st6", [128, 12], F32)
        mv2 = sb("mv2", [128, 2], F32)
        rstd = sb("rstd", [128, 1], F32)
        nmr = sb("nmr", [128, 1], F32)
        P = Prog(nc, sync)
        for c in range(8):
            P.op("pool", lambda e, c=c: e.dma_start(out=wkv[:, c, :], in_=wkv_d[c * 128:(c + 1) * 128, :]),
                 w=["wkv"], dma="a0w")
        P.op("sp", lambda e: e.dma_start(out=mg[:], in_=memg_d[:, :]), w=["mg"], dma="a0p")
        P.op("sp", lambda e: e.dma_start(out=mb[:], in_=memb_d[:, :]), w=["mb"], dma="a0p")
        P.op("dve", lambda e: e.memset(mv_aug[:], 1.0), w=["mv_aug"])
        for mc in range(2):
            mk_ = "mt%d" % mc
            P.op("sp", lambda e, mc=mc: e.dma_start(out=mt[mc][:], in_=mem_d[mc * 128:(mc + 1) * 128, :]),
                 w=[mk_], dma="a0m%d" % mc)
            layer_norm_tail(P, mt[mc], mk_, mg, "mg", mb, "mb", st6, mv2, rstd, nmr, "a0", gb_eng="dve")
            P.op("act", lambda e, mc=mc: e.copy(out=mn[:], in_=mt[mc][:]), r=[mk_], w=["mn"])
            pb, pbk = G["psb"].get()
            for c in range(8):
                P.op("pe", lambda e, c=c, pb=pb: e.transpose(out=pb[:, c * 128:(c + 1) * 128],
                                                             in_=mn[:, c * 128:(c + 1) * 128], identity=ident[:]),
                     r=["mn", "ident"], w=[pbk])
            P.op("act", lambda e, mc=mc, pb=pb: e.copy(out=mnT[:, :, mc * 128:(mc + 1) * 128],
                                                       in_=pb[:].rearrange("p (c t) -> p c t", t=128)),
                 r=[pbk], w=["mnT"])
        for h in range(4):
            bank, bk = G["psA"].get()
            for c in range(8):
                P.op("pe", lambda e, c=c, h=h, bank=bank: e.matmul(
                    bank[:, 0:256], lhsT=wkv[:, c, h * 128:(h + 1) * 128], rhs=mnT[:, c, :],
                    start=(c == 0), stop=(c == 7)), r=["wkv", "mnT"], w=[bk])
            P.op("act", lambda e, h=h, bank=bank: e.copy(out=mkT[:, h, :], in_=bank[:, 0:256]), r=[bk], w=["mkT"])
        for mc in range(2):
            bank, bk = G["psA"].get()
            for c in range(8):
                P.op("pe", lambda e, c=c, mc=mc, bank=bank: e.matmul(
                    bank[:], lhsT=mnT[:, c, mc * 128:(mc + 1) * 128], rhs=wkv[:, c, 512:1024],
                    start=(c == 0), stop=(c == 7)), r=["wkv", "mnT"], w=[bk])
            P.op("dve", lambda e, mc=mc, bank=bank: e.tensor_copy(
                out=mv_aug[:, mc, :, 0:128], in_=bank[:].rearrange("p (h d) -> p h d", d=128)),
                r=[bk], w=["mv_aug"])
        P.emit()


def phase_a1(nc, sync, G, n_tiles, mkT, mv_aug, xT_d, w1_d, b1_d, cc_d, ss_d, dt_d, wq_d, wk_d, cd_d,
             mask_d, gng_d, sink_d, brT_scr, bt=None):
    with ExitStack() as es:
        def sb(name, shape, dt):
            return es.enter_context(nc.sbuf_tensor("a1_" + name, shape, dt))
        ident = G["ident"]
        FM = FM_COLS
        w1 = sb("w1", [128, 8, NCOL_A1], BF16)
        b1 = sb("b1", [1, NCOL_A1], BF16)
        ones = sb("ones", [1, 128], BF16)
        dtab = sb("dtab", [128, 4, 128], F32)
        wqt = sb("wqt", [128, 2, 128], F32)
        wkt = sb("wkt", [128, 4], F32)
        cdt = sb("cdt", [128, 2], F32)
        msk = sb("msk", [128, 2, 128], F32)
        gng = sb("gng", [128, 512], F32)
        esink = sb("esink", [128, 8], F32)
        state = sb("state", [128, 2, 128], F32)
        state_bf = sb("state_bf", [128, 2, 128], BF16)
        xT = [sb("xT%d" % i, [128, 8, 128], BF16) for i in range(2)]
        cct = [sb("cc%d" % i, [128, 128], F32) for i in range(2)]
        sst = [sb("ss%d" % i, [128, 128], F32) for i in range(2)]
        qT = sb("qT", [128, 4, 128], BF16)
        kT = [sb("kT%d" % i, [128, 2, 2, 128], BF16) for i in range(2)]
        kpad = sb("kpad", [128, 2, 2, 128], BF16)
        qspad = sb("qspad", [128, 2, 2, 128], BF16)
        vaug = [sb("vaug%d" % i, [128, 2, 65], BF16) for i in range(2)]
        mqT = sb("mqT", [128, 4, 128], BF16)
        tmp1 = sb("tmp1", [128, 4, 128], F32)
        tmp2 = sb("tmp2", [128, 4, 128], F32)
        rot = sb("rot", [128, 4, 128], BF16)
        ktok = sb("ktok", [128, 256], BF16)
        vret = sb("vret", [128, 4, 128], BF16)
        vw = sb("vw", [128, 4, 128], BF16)
        sg = sb("sg", [128, 512], F32)
        pT = sb("pT", [128, 2, 8, 128], BF16)
        pTc = sb("pTc", [128, 2, 4, 128], BF16)
        innerTm = sb("innerTm", [128, 4, 128], BF16)
        den = sb("den", [128, 8], F32)
        denc = sb("denc", [128, 4], F32)
        br = sb("br", [128, 3, 512], BF16)
        xn = sb("xn", [128, 512], F32)
        gst = sb("gst", [128, 24], F32)
        gmv = sb("gmv", [128, 4, 2], F32)
        grs = sb("grs", [128, 4], F32)
        brT = [sb("brT%d" % i, [128, 12, 128], BF16) for i in range(2)]
        uvt = [sb("uvt%d" % i, [128, 8, 2 * D], BF16) for i in range(2)] if bt is not None else None
        btc = dict(ch=0)

        P = Prog(nc, sync)
        for c in range(8):
            P.op("pool", lambda e, c=c: e.dma_start(out=w1[:, c, :], in_=w1_d[c * 128:(c + 1) * 128, :],
                                                    max_dma_last_dim=4096), w=["w1"], dma="a1w")
        P.op("pool", lambda e: e.dma_start(out=b1[:], in_=b1_d[:, :], max_dma_last_dim=4096), w=["b1"], dma="a1w")
        for (tt, dd, kk) in ((dtab, dt_d, "dtab"), (wqt, wq_d, "wqt")):
            P.op("sp", lambda e, tt=tt, dd=dd: e.dma_start(out=tt[:].rearrange("p a b -> p (a b)"), in_=dd[:, :]),
                 w=[kk], dma="a1p")
        P.op("sp", lambda e: e.dma_start(out=msk[:].rearrange("p a b -> p (a b)"), in_=mask_d[:, :]),
             w=["msk"], dma="a1p")
        for (tt, dd, kk) in ((wkt, wk_d, "wkt"), (cdt, cd_d, "cdt"), (gng, gng_d, "gng"), (esink, sink_d, "esink")):
            P.op("sp", lambda e, tt=tt, dd=dd: e.dma_start(out=tt[:], in_=dd[:, :]), w=[kk], dma="a1p")
        P.op("act", lambda e: e.activation(out=esink[:], in_=esink[:], func=AF.Exp), r=["esink"], w=["esink"])
        P.op("dve", lambda e: e.memset(ones[:], 1.0), w=["ones"])
        P.op("dve", lambda e: e.memset(state[:], 0.0), w=["state"])
        P.op("dve", lambda e: e.memset(state_bf[:], 0.0), w=["state_bf"])
        for i in range(2):
            P.op("dve", lambda e, i=i: e.memset(vaug[i][:], 1.0), w=["vaug%d" % i])
            P.op("dve", lambda e, i=i: e.memset(kT[i][:], 0.0), w=["kT%d" % i])
        P.op("dve", lambda e: e.memset(kpad[:], 0.0), w=["kpad"])
        P.op("dve", lambda e: e.memset(qspad[:], 0.0), w=["qspad"])

        def loads(t):
            s = t % 2
            P.op("pool", lambda e: e.dma_start(
                out=xT[s][:], in_=xT_d.rearrange("(c p) t -> p c t", p=128)[:, :, t * 128:(t + 1) * 128]),
                w=["xT%d" % s], dma="a1x%d" % s)
            P.op("sp", lambda e: e.dma_start(out=cct[s][:], in_=cc_d[:, t * 128:(t + 1) * 128]),
                 w=["cc%d" % s], dma="a1c%d" % s)
            P.op("sp", lambda e: e.dma_start(out=sst[s][:], in_=ss_d[:, t * 128:(t + 1) * 128]),
                 w=["ss%d" % s], dma="a1c%d" % s)

        def fm_group(s, fbs, bank, bk):
            xk = "xT%d" % s
            for j, fb in enumerate(fbs):
                for c in range(8):
                    P.op("pe", lambda e, c=c, fb=fb, j=j: e.matmul(
                        bank[:, j * 128:(j + 1) * 128], lhsT=w1[:, c, fb * 128:(fb + 1) * 128], rhs=xT[s][:, c, :],
                        start=(c == 0), stop=False), r=["w1", xk], w=[bk])
                P.op("pe", lambda e, fb=fb, j=j: e.matmul(
                    bank[:, j * 128:(j + 1) * 128], lhsT=b1[0:1, fb * 128:(fb + 1) * 128], rhs=ones[0:1, :],
                    start=False, stop=True), r=["b1", "ones"], w=[bk])

        def tm_group(s, col0, ncols, bank, bk):
            xk = "xT%d" % s
            for c in range(8):
                P.op("pe", lambda e, c=c: e.matmul(
                    bank[:, 0:ncols], lhsT=xT[s][:, c, :], rhs=w1[:, c, col0:col0 + ncols],
                    start=(c == 0), stop=False), r=["w1", xk], w=[bk])
            P.op("pe", lambda e: e.matmul(bank[:, 0:ncols], lhsT=ones[0:1, :], rhs=b1[0:1, col0:col0 + ncols],
                                          start=False, stop=True), r=["b1", "ones"], w=[bk])

        def do_tile(t):
            s = t % 2
            sp_ = 1 - s
            if t + 1 < n_tiles:
                loads(t + 1)
            if A1_STAGE[0] == -1:
                return
            bank, bk = G["psA"].get()
            fm_group(s, [0, 1, 2, 3], bank, bk)
            P.op("act", lambda e: e.copy(out=qT[:].rearrange("p a b -> p (a b)"), in_=bank[:]), r=[bk], w=["qT"])
            if A1_STAGE[0] == -2:
                return
            bankk, bkk = G["psA"].get()
            fm_group(s, [4, 5], bankk, bkk)
            kTk = "kT%d" % s
            for hf in range(2):
                P.op("act", lambda e, hf=hf: e.copy(
                    out=kT[s][hf * 64:(hf + 1) * 64, :, hf, :],
                    in_=bankk[hf * 64:(hf + 1) * 64, 0:256].rearrange("p (g t) -> p g t", t=128)), r=[bkk], w=[kTk])
            if A1_STAGE[0] == -3:
                return
            bra, bkra = G["psA"].get()
            fm_group(s, [6, 7, 8, 9], bra, bkra)
            brb, bkrb = G["psA"].get()
            fm_group(s, [10, 11, 12, 13], brb, bkrb)
            P.op("dve", lambda e: e.tensor_tensor(out=tmp1[:], in0=bra[:].rearrange("p (a b) -> p a b", b=128),
                                                  in1=bcast(cct[s][:], 1, [128, 4, 128]), op=ALU.mult),
                 r=[bkra, "cc%d" % s], w=["tmp1"])
            P.op("dve", lambda e: e.tensor_tensor(out=tmp2[:], in0=brb[:].rearrange("p (a b) -> p a b", b=128),
                                                  in1=bcast(sst[s][:], 1, [128, 4, 128]), op=ALU.mult),
                 r=[bkrb, "ss%d" % s], w=["tmp2"])
            P.op("pool", lambda e: e.tensor_tensor(out=rot[:], in0=tmp1[:], in1=tmp2[:], op=ALU.add),
                 r=["tmp1", "tmp2"], w=["rot"])
            for hf in range(2):
                rows = slice(hf * 64, (hf + 1) * 64)
                P.op("pool", lambda e, hf=hf, rows=rows: e.tensor_tensor(
                    out=qspad[rows, :, hf, :], in0=rot[rows, 0:2, :], in1=wqt[rows, :, :], op=ALU.mult),
                    r=["rot", "wqt"], w=["qspad"])
                P.op("pool", lambda e, hf=hf, rows=rows: e.tensor_copy(out=kpad[rows, :, hf, :], in_=rot[rows, 2:4, :]),
                     r=["rot"], w=["kpad"])
            if A1_STAGE[0] == -4:
                return
            bankm, bkm = G["psA"].get()
            fm_group(s, [14, 15, 16, 17], bankm, bkm)
            P.op("act", lambda e: e.copy(out=mqT[:].rearrange("p a b -> p (a b)"), in_=bankm[:]), r=[bkm], w=["mqT"])
            if A1_STAGE[0] == -5:
                return
            bav, bkav = G["psA"].get()
            tm_group(s, FM, 128, bav, bkav)
            vk = "vaug%d" % s
            P.op("act", lambda e: e.copy(out=vaug[s][:, :, 0:64], in_=bav[:, 0:128].rearrange("p (g d) -> p g d", d=64)),
                 r=[bkav], w=[vk])
            if A1_STAGE[0] == -6:
                return
            brv, bkrv = G["psA"].get()
            tm_group(s, FM + 128, 512, brv, bkrv)
            P.op("act", lambda e: e.copy(out=vret[:].rearrange("p a b -> p (a b)"), in_=brv[:]), r=[bkrv], w=["vret"])
            if A1_STAGE[0] == -8:
                return
            VV = _os.environ.get("VV", "2")
            if VV == "0":
                P.op("dve", lambda e: e.tensor_tensor(out=vw[:], in0=brv[:].rearrange("p (a b) -> p a b", b=128),
                                                      in1=bcast(wkt[:], 2, [128, 4, 128]), op=ALU.mult),
                     r=[bkrv, "wkt"], w=["vw"])
            elif VV == "1":
                P.op("dve", lambda e: e.tensor_tensor(out=tmp1[:], in0=brv[:].rearrange("p (a b) -> p a b", b=128),
                                                      in1=bcast(wkt[:], 2, [128, 4, 128]), op=ALU.mult),
                     r=[bkrv, "wkt"], w=["tmp1"])
            elif VV == "2":
                P.op("dve", lambda e: e.tensor_tensor(out=vw[:], in0=vret[:],
                                                      in1=bcast(wkt[:], 2, [128, 4, 128]), op=ALU.mult),
                     r=["vret", "wkt"], w=["vw"])
            elif VV == "3":
                for h in range(4):
                    P.op("dve", lambda e, h=h: e.tensor_scalar(
                        out=vw[:, h, :], in0=brv[:, h * 128:(h + 1) * 128], scalar1=wkt[:, h:h + 1], scalar2=None,
                        op0=ALU.mult), r=[bkrv, "wkt"], w=["vw"])
            if A1_STAGE[0] == -7:
                return
            brg, bkrg = G["psA"].get()
            tm_group(s, FM + 640, 512, brg, bkrg)
            P.op("act", lambda e: e.activation(out=sg[:], in_=brg[:], func=AF.Silu), r=[bkrg], w=["sg"])
            P.op("pool", lambda e: e.tensor_tensor(out=sg[:], in0=sg[:], in1=gng[:], op=ALU.mult),
                 r=["sg", "gng"], w=["sg"])
            if A1_STAGE[0] == 1:
                return
            whichs = [(0, s)] + ([(1, sp_)] if t > 0 else [])
            for hb2 in range(2):
                for (wi, slot) in whichs:
                    bl, bkl = G["psA"].get()
                    for j in range(4):
                        h = 4 * hb2 + j
                        i, half = h // 2, h % 2
                        P.op("pe", lambda e, j=j, i=i, half=half, slot=slot, bl=bl, hb2=hb2: e.matmul(
                            bl[:, j * 128:(j + 1) * 128], lhsT=kT[slot][:, hb2, half, :],
                            rhs=qT[:, i, :], start=True, stop=True),
                            r=["kT%d" % slot, "qT"], w=[bkl])
                    pk = "pT%d%d" % (wi, hb2)
                    P.op("act", lambda e, wi=wi, bl=bl, hb2=hb2: e.activation(
                        out=pT[:, wi, hb2 * 4:(hb2 + 1) * 4, :].rearrange("p a b -> p (a b)"), in_=bl[:],
                        func=AF.Exp, scale=0.125), r=[bkl], w=[pk])
                    if A1_STAGE[0] == 11:
                        continue
                    P.op("pool", lambda e, wi=wi, hb2=hb2: e.tensor_tensor(
                        out=pT[:, wi, hb2 * 4:(hb2 + 1) * 4, :], in0=pT[:, wi, hb2 * 4:(hb2 + 1) * 4, :],
                        in1=bcast(msk[:, wi, :], 1, [128, 4, 128]), op=ALU.mult), r=[pk, "msk"], w=[pk])
            if A1_STAGE[0] in (11, 12):
                return
            for hb2 in range(2):
                bo, bko = G["psA"].get()
                for j in range(4):
                    h = 4 * hb2 + j
                    if t > 0:
                        P.op("pe", lambda e, j=j, h=h, bo=bo, hb2=hb2: e.matmul(
                            bo[:, j * 65:(j + 1) * 65], lhsT=pT[:, 1, h, :], rhs=vaug[sp_][:, hb2, :],
                            start=True, stop=False), r=["pT1%d" % hb2, "vaug%d" % sp_], w=[bko])
                    P.op("pe", lambda e, j=j, h=h, bo=bo, hb2=hb2: e.matmul(
                        bo[:, j * 65:(j + 1) * 65], lhsT=pT[:, 0, h, :], rhs=vaug[s][:, hb2, :],
                        start=(t == 0), stop=True), r=["pT0%d" % hb2, vk], w=[bko])
                bo3 = bo[:, 0:260].rearrange("p (j d) -> p j d", d=65)
                if A1_STAGE[0] == 13:
                    continue
                dk = "den%d" % hb2
                P.op("dve", lambda e, bo3=bo3, hb2=hb2: e.tensor_tensor(
                    out=den[:, hb2 * 4:(hb2 + 1) * 4], in0=bo3[:, :, 64], in1=esink[:, hb2 * 4:(hb2 + 1) * 4],
                    op=ALU.add), r=[bko, "esink"], w=[dk])
                P.op("dve", lambda e, hb2=hb2: e.reciprocal(out=den[:, hb2 * 4:(hb2 + 1) * 4], in_=den[:, hb2 * 4:(hb2 + 1) * 4]),
                     r=[dk], w=[dk])
                P.op("dve", lambda e, bo3=bo3, hb2=hb2: e.tensor_tensor(
                    out=br[:, 0, hb2 * 256:(hb2 + 1) * 256].rearrange("p (j d) -> p j d", d=64),
                    in0=bo3[:, :, 0:64], in1=bcast(den[:, hb2 * 4:(hb2 + 1) * 4], 2, [128, 4, 64]), op=ALU.mult),
                    r=[bko, dk], w=["br0"])
            if A1_STAGE[0] == 2:
                return
            for mc in range(2):
                bl, bkl = G["psA"].get()
                for h in range(4):
                    P.op("pe", lambda e, h=h, mc=mc, bl=bl: e.matmul(
                        bl[:, h * 128:(h + 1) * 128], lhsT=mkT[:, h, mc * 128:(mc + 1) * 128], rhs=mqT[:, h, :],
                        start=True, stop=True), r=["mkT", "mqT"], w=[bkl])
                P.op("act", lambda e, mc=mc, bl=bl: e.activation(
                    out=pTc[:, mc, :, :].rearrange("p a b -> p (a b)"), in_=bl[:], func=AF.Exp,
                    scale=float(128 ** -0.5)), r=[bkl], w=["pTc%d" % mc])
            for hp2 in range(2):
                bo, bko = G["psA"].get()
                for j in range(2):
                    h = 2 * hp2 + j
                    for mc in range(2):
                        P.op("pe", lambda e, j=j, h=h, mc=mc, bo=bo: e.matmul(
                            bo[:, j * 129:(j + 1) * 129], lhsT=pTc[:, mc, h, :], rhs=mv_aug[:, mc, h, :],
                            start=(mc == 0), stop=(mc == 1)), r=["pTc%d" % mc, "mv_aug"], w=[bko])
                bo3 = bo[:, 0:258].rearrange("p (j d) -> p j d", d=129)
                dk = "denc%d" % hp2
                P.op("dve", lambda e, bo3=bo3, hp2=hp2: e.reciprocal(out=denc[:, hp2 * 2:(hp2 + 1) * 2], in_=bo3[:, :, 128]),
                     r=[bko], w=[dk])
                P.op("dve", lambda e, bo3=bo3, hp2=hp2: e.tensor_tensor(
                    out=br[:, 2, hp2 * 256:(hp2 + 1) * 256].rearrange("p (j d) -> p j d", d=128),
                    in0=bo3[:, :, 0:128], in1=bcast(denc[:, hp2 * 2:(hp2 + 1) * 2], 2, [128, 2, 128]), op=ALU.mult),
                    r=[bko, dk], w=["br2"])
            if A1_STAGE[0] == 3:
                return
            pb, pbk = G["psb"].get()
            for blk in range(2):
                P.op("pe", lambda e, blk=blk: e.transpose(out=pb[:, blk * 128:(blk + 1) * 128], in_=rot[:, 2 + blk, :],
                                                          identity=ident[:]), r=["rot", "ident"], w=[pbk])
            P.op("act", lambda e: e.copy(out=ktok[:], in_=pb[:, 0:256]), r=[pbk], w=["ktok"])
            bi, bki = G["psA"].get()
            for h in range(4):
                blk, half = h // 2, h % 2
                P.op("pe", lambda e, h=h, blk=blk, half=half: e.matmul(
                    bi[:, h * 128:(h + 1) * 128], lhsT=kpad[:, blk, half, :],
                    rhs=rot[:, blk, :], start=True, stop=True), r=["rot", "kpad"], w=[bki])
            P.op("dve", lambda e: e.tensor_tensor(out=innerTm[:], in0=bi[:].rearrange("p (a b) -> p a b", b=128),
                                                  in1=dtab[:], op=ALU.mult), r=[bki, "dtab"], w=["innerTm"])
            bo, bko = G["psA"].get()
            for h in range(4):
                blk, half = h // 2, h % 2
                P.op("pe", lambda e, h=h: e.matmul(
                    bo[:, h * 128:(h + 1) * 128], lhsT=innerTm[:, h, :], rhs=vret[:, h, :],
                    start=True, stop=(t == 0)), r=["innerTm", "vret"], w=[bko])
                if t > 0:
                    P.op("pe", lambda e, h=h, blk=blk, half=half: e.matmul(
                        bo[:, h * 128:(h + 1) * 128], lhsT=qspad[:, blk, half, :],
                        rhs=state_bf[:, blk, :], start=False, stop=True),
                        r=["qspad", "state_bf"], w=[bko])
            bkv, bkkv = G["psA"].get()
            for h in range(4):
                blk, half = h // 2, h % 2
                P.op("pe", lambda e, h=h, blk=blk, half=half: e.matmul(
                    bkv[:, h * 128:(h + 1) * 128], lhsT=ktok[:, blk * 128:(blk + 1) * 128],
                    rhs=vw[:, h, :], start=True, stop=True), r=["ktok", "vw"], w=[bkkv])
            for h in range(4):
                blk, half = h // 2, h % 2
                rows = slice(half * 64, (half + 1) * 64)
                P.op("dve", lambda e, h=h, blk=blk, rows=rows: e.scalar_tensor_tensor(
                    out=state[rows, blk, :], in0=state[rows, blk, :], scalar=cdt[rows, blk:blk + 1],
                    in1=bkv[rows, h * 128:(h + 1) * 128], op0=ALU.mult, op1=ALU.add),
                    r=["state", "cdt", bkkv], w=["state"])
            P.op("act", lambda e: e.copy(out=state_bf[:], in_=state[:]), r=["state"], w=["state_bf"])
            for h in range(4):
                P.op("dve", lambda e, h=h: e.bn_stats(out=gst[:, h * 6:(h + 1) * 6], in_=bo[:, h * 128:(h + 1) * 128]),
                     r=[bko], w=["gst%d" % h])
                P.op("dve", lambda e, h=h: e.bn_aggr(out=gmv[:, h, :], in_=gst[:, h * 6:(h + 1) * 6]),
                     r=["gst%d" % h], w=["gmv"])
            P.op("act", lambda e: e.activation(out=grs[:], in_=gmv[:, :, 1], func=AF.Sqrt, bias=EPS_T[0][:], scale=1.0),
                 r=["gmv", "eps"], w=["grs"])
            P.op("dve", lambda e: e.reciprocal(out=grs[:], in_=grs[:]), r=["grs"], w=["grs"])
            for h in range(4):
                P.op("dve", lambda e, h=h: e.tensor_scalar(
                    out=xn[:, h * 128:(h + 1) * 128], in0=bo[:, h * 128:(h + 1) * 128], scalar1=gmv[:, h, 0:1],
                    scalar2=grs[:, h:h + 1], op0=ALU.subtract, op1=ALU.mult), r=[bko, "gmv", "grs"], w=["xn"])
            P.op("pool", lambda e: e.tensor_tensor(out=br[:, 1, :], in0=xn[:], in1=sg[:], op=ALU.mult),
                 r=["xn", "sg"], w=["br1"])
            if A1_STAGE[0] == 4:
                return
            pba, pbka = G["psb"].get()
            for b in range(2):
                for kc in range(4):
                    P.op("pe", lambda e, b=b, kc=kc: e.transpose(
                        out=pba[:, (b * 4 + kc) * 128:(b * 4 + kc + 1) * 128], in_=br[:, b, kc * 128:(kc + 1) * 128],
                        identity=ident[:]), r=["br%d" % b, "ident"], w=[pbka])
            bt = "brT%d" % s
            P.op("act", lambda e: e.copy(out=brT[s][:, 0:8, :].rearrange("p a b -> p (a b)"), in_=pba[:]),
                 r=[pbka], w=[bt])
            pbc, pbkc = G["psb"].get()
            for kc in range(4):
                P.op("pe", lambda e, kc=kc: e.transpose(out=pbc[:, kc * 128:(kc + 1) * 128],
                                                        in_=br[:, 2, kc * 128:(kc + 1) * 128], identity=ident[:]),
                     r=["br2", "ident"], w=[pbkc])
            P.op("dve", lambda e: e.tensor_copy(out=brT[s][:, 8:12, :].rearrange("p a b -> p (a b)"), in_=pbc[:, 0:512]),
                 r=[pbkc], w=[bt])
            P.op("sp", lambda e: e.dma_start(out=brT_scr[t, :, :], in_=brT[s][:].rearrange("p a b -> p (a b)")),
                 r=[bt], dma="a1s%d" % s)

        loads(0)
        every = max(1, n_tiles // 16)
        for t in range(n_tiles):
            if bt is not None and t % every == 0 and btc["ch"] < 16:
                bt_chunk(P, btc["ch"], uvt, *bt)
                btc["ch"] += 1
            do_tile(t)
        while bt is not None and btc["ch"] < 16:
            bt_chunk(P, btc["ch"], uvt, *bt)
            btc["ch"] += 1
        P.emit()


def phase_a2(nc, sync, G, n_tiles, x_d, xT_d, wg_d, bg_d, wbr_d, wout_d, ln1g_d, ln1b_d, brT_scr, h_scr):
    with ExitStack() as es:
        def sb(name, shape, dt):
            return es.enter_context(nc.sbuf_tensor("a2_" + name, shape, dt))
        ident = G["ident"]
        wg = sb("wg", [128, 8, NCOL_G], BF16)
        bg = sb("bg", [1, NCOL_G], BF16)
        ones = sb("ones", [1, 128], BF16)
        wbr = sb("wbr", [128, 12, D], BF16)
        wout = sb("wout", [128, 8, D], BF16)
        g1 = sb("g1", [128, D], F32)
        b1 = sb("b1", [128, D], F32)
        xT = [sb("xT%d" % i, [128, 8, 128], BF16) for i in range(2)]
        xt = [sb("x%d" % i, [128, D], F32) for i in range(2)]
        brT = [sb("brT%d" % i, [128, 12, 128], BF16) for i in range(2)]
        gsb = [sb("gsb%d" % i, [128, 512], F32) for i in range(2)]
        acc = sb("acc", [128, D], F32)
        tmp = [sb("tmp%d" % i, [128, 512], F32) for i in range(2)]
        mbf = sb("mbf", [128, D], BF16)
        mT = sb("mT", [128, 8, 128], BF16)
        z = [sb("z%d" % i, [128, D], F32) for i in range(2)]
        st6 = sb("st6", [128, 12], F32)
        mv2 = sb("mv2", [128, 2], F32)
        rstd = sb("rstd", [128, 1], F32)
        nmr = sb("nmr", [128, 1], F32)

        P = Prog(nc, sync)
        for c in range(8):
            P.op("pool", lambda e, c=c: e.dma_start(out=wg[:, c, :], in_=wg_d[c * 128:(c + 1) * 128, :],
                                                    max_dma_last_dim=4096), w=["wg"], dma="a2w")
        P.op("pool", lambda e: e.dma_start(out=bg[:], in_=bg_d[:, :], max_dma_last_dim=4096), w=["bg"], dma="a2w")
        for c in range(12):
            P.op("pool", lambda e, c=c: e.dma_start(out=wbr[:, c, :], in_=wbr_d[c * 128:(c + 1) * 128, :]),
                 w=["wbr"], dma="a2w")
        for c in range(8):
            P.op("pool", lambda e, c=c: e.dma_start(out=wout[:, c, :], in_=wout_d[c * 128:(c + 1) * 128, :]),
                 w=["wout"], dma="a2w")
        P.op("sp", lambda e: e.dma_start(out=g1[:], in_=ln1g_d[:, :]), w=["g1"], dma="a2p")
        P.op("sp", lambda e: e.dma_start(out=b1[:], in_=ln1b_d[:, :]), w=["b1"], dma="a2p")
        P.op("dve", lambda e: e.memset(ones[:], 1.0), w=["ones"])
        cnt = dict(g=0)

        def loads(t):
            s = t % 2
            P.op("pool", lambda e: e.dma_start(
                out=xT[s][:], in_=xT_d.rearrange("(c p) t -> p c t", p=128)[:, :, t * 128:(t + 1) * 128]),
                w=["xT%d" % s], dma="a2x%d" % s)
            P.op("sp", lambda e: e.dma_start(out=xt[s][:], in_=x_d[t * 128:(t + 1) * 128, :]),
                 w=["x%d" % s], dma="a2l%d" % s)
            P.op("sp", lambda e: e.dma_start(out=brT[s][:].rearrange("p a b -> p (a b)"), in_=brT_scr[t, :, :]),
                 w=["brT%d" % s], dma="a2l%d" % s)

        def do_tile(t):
            s = t % 2
            if t + 1 < n_tiles:
                loads(t + 1)
            xk, xtk, btk = "xT%d" % s, "x%d" % s, "brT%d" % s
            for b in range(3):
                for half in range(2):
                    col0 = b * 1024 + half * 512
                    gi = cnt["g"] % 2
                    cnt["g"] += 1
                    bgk, bkg = G["psA"].get()
                    for c in range(8):
                        P.op("pe", lambda e, c=c, col0=col0, bgk=bgk: e.matmul(
                            bgk[:], lhsT=xT[s][:, c, :], rhs=wg[:, c, col0:col0 + 512], start=(c == 0), stop=False),
                            r=["wg", xk], w=[bkg])
                    P.op("pe", lambda e, col0=col0, bgk=bgk: e.matmul(
                        bgk[:], lhsT=ones[0:1, :], rhs=bg[0:1, col0:col0 + 512], start=False, stop=True),
                        r=["bg", "ones"], w=[bkg])
                    P.op("act", lambda e, gi=gi, bgk=bgk: e.activation(out=gsb[gi][:], in_=bgk[:], func=AF.Sigmoid),
                         r=[bkg], w=["gsb%d" % gi])
                    by, bky = G["psA"].get()
                    for kc in range(4):
                        P.op("pe", lambda e, kc=kc, b=b, half=half, by=by: e.matmul(
                            by[:], lhsT=brT[s][:, b * 4 + kc, :], rhs=wbr[:, b * 4 + kc, half * 512:(half + 1) * 512],
                            start=(kc == 0), stop=(kc == 3)), r=["wbr", btk], w=[bky])
                    ak = "acc%d" % half
                    if b == 0:
                        P.op("dve", lambda e, gi=gi, half=half, by=by: e.tensor_tensor(
                            out=acc[:, half * 512:(half + 1) * 512], in0=by[:], in1=gsb[gi][:], op=ALU.mult),
                            r=[bky, "gsb%d" % gi], w=[ak])
                    else:
                        P.op("dve", lambda e, gi=gi, by=by: e.tensor_tensor(
                            out=tmp[gi][:], in0=by[:], in1=gsb[gi][:], op=ALU.mult),
                            r=[bky, "gsb%d" % gi], w=["tmp%d" % gi])
                        if b == 1:
                            P.op("pool", lambda e, gi=gi, half=half: e.tensor_tensor(
                                out=acc[:, half * 512:(half + 1) * 512], in0=acc[:, half * 512:(half + 1) * 512],
                                in1=tmp[gi][:], op=ALU.add), r=[ak, "tmp%d" % gi], w=[ak])
                        else:
                            P.op("pool", lambda e, gi=gi, half=half: e.tensor_tensor(
                                out=mbf[:, half * 512:(half + 1) * 512], in0=acc[:, half * 512:(half + 1) * 512],
                                in1=tmp[gi][:], op=ALU.add), r=[ak, "tmp%d" % gi], w=["mbf"])
            pb, pbk = G["psb"].get()
            for c in range(8):
                P.op("pe", lambda e, c=c: e.transpose(out=pb[:, c * 128:(c + 1) * 128], in_=mbf[:, c * 128:(c + 1) * 128],
                                                      identity=ident[:]), r=["mbf", "ident"], w=[pbk])
            P.op("act", lambda e: e.copy(out=mT[:].rearrange("p a b -> p (a b)"), in_=pb[:]), r=[pbk], w=["mT"])
            zk = "z%d" % s
            for half in range(2):
                bz, bkz = G["psA"].get()
                for c in range(8):
                    P.op("pe", lambda e, c=c, half=half, bz=bz: e.matmul(
                        bz[:], lhsT=mT[:, c, :], rhs=wout[:, c, half * 512:(half + 1) * 512],
                        start=(c == 0), stop=(c == 7)), r=["mT", "wout"], w=[bkz])
                P.op("dve", lambda e, half=half, bz=bz: e.scalar_tensor_tensor(
                    out=z[s][:, half * 512:(half + 1) * 512], in0=xt[s][:, half * 512:(half + 1) * 512],
                    scalar=ALPHA, in1=bz[:], op0=ALU.mult, op1=ALU.add), r=[xtk, bkz], w=[zk])
            layer_norm_tail(P, z[s], zk, g1, "g1", b1, "b1", st6, mv2, rstd, nmr, "a2", gb_eng="pool")
            P.op("sp", lambda e: e.dma_start(out=h_scr[t * 128:(t + 1) * 128, :], in_=z[s][:]), r=[zk],
                 dma="a2s%d" % s)

        loads(0)
        for t in range(n_tiles):
            do_tile(t)
        P.emit()


IN_SPECS = [
    ("x", lambda T: [T, D], F32), ("xT", lambda T: [D, T], F32), ("mem", lambda T: [256, D], F32),
    ("memg", lambda T: [128, D], F32), ("memb", lambda T: [128, D], F32), ("wkv", lambda T: [D, D], F32),
    ("w1", lambda T: [D, NCOL_A1], F32), ("b1", lambda T: [1, NCOL_A1], F32),
    ("cc", lambda T: [128, T], F32), ("ss", lambda T: [128, T], F32),
    ("dtab", lambda T: [128, 512], F32), ("wqt", lambda T: [128, 256], F32), ("wkt", lambda T: [128, 4], F32),
    ("cdt", lambda T: [128, 2], F32), ("mask", lambda T: [128, 256], F32), ("gng", lambda T: [128, 512], F32),
    ("sink", lambda T: [128, 8], F32),
    ("wg", lambda T: [D, NCOL_G], F32), ("bg", lambda T: [1, NCOL_G], F32), ("wbr", lambda T: [1536, D], F32),
    ("wout", lambda T: [D, D], F32), ("ln1g", lambda T: [128, D], F32), ("ln1b", lambda T: [128, D], F32),
    ("wpqT", lambda T: [2048, D], F32), ("skT", lambda T: [2048, 128], F32),
    ("peer_u", lambda T: [N_EXP, D], F32), ("peer_v", lambda T: [N_EXP, D], F32),
    ("ln2g", lambda T: [128, D], F32), ("ln2b", lambda T: [128, D], F32),
    ("identf", lambda T: [128, 128], F32), ("iota16", lambda T: [128, 16], F32),
]


def build_full(n_tiles, debug_h=False, stop_after=None):
    nc = bass.Bass("TRN2", target_bir_lowering=False)
    T = n_tiles * 128
    d = {}
    for name, shp, dtp in IN_SPECS:
        d[name] = nc.dram_tensor(name, shp(T), dtp, kind="ExternalInput").ap()
    y_d = nc.dram_tensor("y", [T, D], F32, kind="ExternalOutput").ap()
    h_scr = nc.dram_tensor("h_scr", [T, D], F32, kind="ExternalOutput" if debug_h else "Internal").ap()
    brT_scr = nc.dram_tensor("brT_scr", [n_tiles, 128, 1536], BF16,
                             kind="ExternalOutput" if debug_h else "Internal").ap()
    with ExitStack() as es:
        sync = Sync(nc, es)
        G = alloc_globals(nc, es, sync)
        G["psA"] = PsumPool(G["psf"].t + G["pv"], "psA")
        phase_const(nc, sync, G, d["identf"], d["iota16"])
        uv_scr = nc.dram_tensor("uv_scr", [N_EXP, 2 * D], BF16, kind="Internal").ap()
        with ExitStack() as es1:
            mkT = es1.enter_context(nc.sbuf_tensor("sb_mkT", [128, 4, 256], BF16))
            mv_aug = es1.enter_context(nc.sbuf_tensor("sb_mvaug", [128, 2, 4, 129], BF16))
            phase_a0(nc, sync, G, mkT, mv_aug, d["mem"], d["memg"], d["memb"], d["wkv"])
            if stop_after == "a0":
                return nc
            phase_a1(nc, sync, G, n_tiles, mkT, mv_aug, d["xT"], d["w1"], d["b1"], d["cc"], d["ss"], d["dtab"],
                     d["wqt"], d["wkt"], d["cdt"], d["mask"], d["gng"], d["sink"], brT_scr,
                     bt=(d["peer_u"], d["peer_v"], uv_scr))
        if stop_after == "a1":
            return nc
        phase_a2(nc, sync, G, n_tiles, d["x"], d["xT"], d["wg"], d["bg"], d["wbr"], d["wout"], d["ln1g"],
                 d["ln1b"], brT_scr, h_scr)
        if stop_after == "a2":
            return nc
        with ExitStack() as es2:
            weff = es2.enter_context(nc.sbuf_tensor("sb_weff", [128, 8, 2048], BF16))
            phase_b0(nc, sync, G, weff, d["wpqT"], d["skT"])
            phase_b(nc, sync, G, n_tiles, weff, h_scr, uv_scr, d["ln2g"], d["ln2b"], y_d)
    return nc


def _w_in_cols():
    fm = list(range(0, 512))
    fm += list(range(512, 576)) * 2 + list(range(576, 640)) * 2
    rq0, rk0 = 768, 1024
    fm += list(range(rq0, rq0 + 256)) + list(range(rk0, rk0 + 256))
    sw = []
    for base in (rq0, rk0):
        for h in range(4):
            hb = base + 64 * h
            sw += list(range(hb + 32, hb + 64)) + list(range(hb, hb + 32))
    fm += sw
    fm += list(range(2304, 2816))
    tm = list(range(640, 768)) + list(range(1280, 1792)) + list(range(1792, 2304))
    return np.array(fm + tm), np.arange(2816, 5888)


def _const_tables(T):
    half = 32
    theta = (1.0 / np.power(np.float32(10000.0), np.linspace(0.0, 1.0, half, dtype=np.float32))).astype(np.float32)
    pos = np.arange(T, dtype=np.float32)
    ang = (pos[:, None] * theta[None, :]).astype(np.float32)
    cos, sin = np.cos(ang).astype(np.float32), np.sin(ang).astype(np.float32)
    p = np.arange(128)
    cc = np.ascontiguousarray(cos[:, p % 32].T)
    sgn = np.where((p % 64) < 32, -1.0, 1.0).astype(np.float32)
    ss = np.ascontiguousarray((sin[:, p % 32] * sgn[None, :]).T)
    lg = np.log(1.0 - 2.0 ** (-5.0 - np.arange(4, dtype=np.float64)))
    i = np.arange(128, dtype=np.float64)
    diff = i[None, :] - i[:, None]
    dt = np.zeros((128, 4, 128), np.float64)
    for h in range(4):
        dt[:, h, :] = np.where(diff >= 0, np.exp(lg[h] * np.maximum(diff, 0.0)), 0.0) * 0.125
    wq = np.zeros((128, 2, 128), np.float64)
    cd = np.zeros((128, 2), np.float64)
    for blk in range(2):
        for hf in range(2):
            h = blk * 2 + hf
            wq[hf * 64:(hf + 1) * 64, blk, :] = np.exp(lg[h] * (i + 1.0))[None, :]
            cd[hf * 64:(hf + 1) * 64, blk] = np.exp(lg[h] * 128.0)
    wk = np.zeros((128, 4), np.float64)
    for h in range(4):
        wk[:, h] = np.exp(lg[h] * (127.0 - i)) * 0.125
    k = np.arange(128)[:, None]
    q = np.arange(128)[None, :]
    mask = np.concatenate([(k <= q), (k > q)], axis=1).astype(np.float32)
    f = lambda a: np.ascontiguousarray(a.astype(np.float32))
    return dict(cc=cc, ss=ss, dtab=f(dt.reshape(128, 512)), wqt=f(wq.reshape(128, 256)), wkt=f(wk), cdt=f(cd),
                mask=mask, identf=np.eye(128, dtype=np.float32),
                iota16=f(np.broadcast_to(np.arange(16.0), (128, 16))))


def _rep(v, n=128):
    return np.ascontiguousarray(np.broadcast_to(np.asarray(v, np.float32)[None, :], (n, v.shape[-1])))


def host_prep(inputs, T):
    g = lambda n: np.asarray(inputs[n], np.float32)[0]
    c1, cg = _w_in_cols()
    w_in, b_in = g("w_in"), g("b_in")
    sh = dict(
        memg=_rep(g("mem_ln_g")), memb=_rep(g("mem_ln_b")), wkv=g("w_mem_kv"),
        w1=np.ascontiguousarray(w_in[:, c1]), b1=np.ascontiguousarray(b_in[c1][None, :]),
        gng=_rep(g("ret_gn_g")), sink=_rep(g("attn_sinks")),
        wg=np.ascontiguousarray(w_in[:, cg]), bg=np.ascontiguousarray(b_in[cg][None, :]),
        wbr=np.ascontiguousarray(np.concatenate([g("w_branch_attn"), g("w_branch_ret"), g("w_branch_mem")], axis=0)),
        wout=g("w_out"), ln1g=_rep(g("ln1_g")), ln1b=_rep(g("ln1_b")),
        wpqT=np.ascontiguousarray(g("w_peer_q").T),
        skT=np.ascontiguousarray(g("peer_sub_keys").transpose(0, 1, 3, 2).reshape(2048, 128)),
        peer_u=g("peer_u"), peer_v=g("peer_v"), ln2g=_rep(g("ln2_g")), ln2b=_rep(g("ln2_b")),
    )
    sh.update(_const_tables(T))
    return sh


def kernel(**inputs):
    x = np.asarray(inputs["x"], np.float32)
    mem = np.asarray(inputs["mem"], np.float32)
    B, S, _ = x.shape
    n_tiles = S // 128
    sh = host_prep(inputs, S)
    in_maps = []
    for b in range(B):
        m = dict(sh)
        m["x"] = np.ascontiguousarray(x[b])
        m["xT"] = np.ascontiguousarray(x[b].T)
        m["mem"] = np.ascontiguousarray(mem[b])
        in_maps.append(m)
    nc = build_full(n_tiles, stop_after=_os.environ.get('MK_STOP'))
    res = run_bass_kernel_spmd(nc, in_maps, core_ids=list(range(B)))
    return np.stack([r["y"] for r in res.results], axis=0).astype(np.float32)
```

```python
import numpy as np
import ml_dtypes
from contextlib import ExitStack

import concourse.bass as bass
import concourse.mybir as mybir
from concourse.bass_utils import run_bass_kernel_spmd

F32 = mybir.dt.float32
BF16 = mybir.dt.bfloat16
I32 = mybir.dt.int32
U32 = mybir.dt.uint32
AF = mybir.ActivationFunctionType
ALU = mybir.AluOpType
AX = mybir.AxisListType

D = 1024
SEQ = 8192
NCORES = 8
ALPHA = 2.0 ** 0.25
LN_EPS = 1e-5
N_EXP = 16384

FM_COLS = 18 * 128
TM_A1 = 128 + 512 + 512
NCOL_A1 = FM_COLS + TM_A1
NCOL_G = 3072

SAME_ENGINE_SYNC = True
EPS_T = [None]


class Sync:
    def __init__(self, nc, es):
        self.nc = nc
        self.es = es
        self.sems = {}
        self.cnt = {}

    def sem(self, name):
        if name not in self.sems:
            self.sems[name] = self.es.enter_context(self.nc.semaphore("s_" + name))
            self.cnt[name] = 0
        return self.sems[name]


ENGS = ("pe", "act", "dve", "pool", "sp")


class Prog:
    def __init__(self, nc, sync):
        self.nc = nc
        self.sync = sync
        self.ops = []
        self.lastw = {}
        self.readers = {}

    def op(self, eng, fn, r=(), w=(), dma=None):
        i = len(self.ops)
        deps = set()
        for k in r:
            if k in self.lastw:
                deps.add(self.lastw[k])
        for k in w:
            if k in self.lastw:
                deps.add(self.lastw[k])
            deps.update(self.readers.get(k, ()))
        for k in r:
            self.readers.setdefault(k, []).append(i)
        for k in w:
            self.lastw[k] = i
            self.readers[k] = []
        deps.discard(i)
        self.ops.append(dict(eng=eng, fn=fn, deps=deps, dma=dma))
        return i

    def emit(self):
        nc, sync, ops = self.nc, self.sync, self.ops
        n = len(ops)
        per_eng = {e: [] for e in ENGS}
        for i, o in enumerate(ops):
            per_eng[o["eng"]].append(i)
        need = [False] * n
        for i, o in enumerate(ops):
            for d in o["deps"]:
                od = ops[d]
                if od["dma"] is not None:
                    continue
                if od["eng"] == o["eng"] and o["dma"] is None:
                    if o["eng"] == "pe" or not SAME_ENGINE_SYNC:
                        continue
                need[d] = True
        for e in ENGS:
            for i in reversed(per_eng[e]):
                if ops[i]["dma"] is None:
                    need[i] = True
                    break
        mark = [None] * n
        for i, o in enumerate(ops):
            if o["dma"] is not None:
                nm = "d_" + o["dma"]
                sync.sem(nm)
                sync.cnt[nm] += 16
                mark[i] = (nm, sync.cnt[nm])
            elif need[i]:
                nm = "e_" + o["eng"]
                sync.sem(nm)
                sync.cnt[nm] += 1
                mark[i] = (nm, sync.cnt[nm])
        final = {nm: sync.cnt[nm] for nm in sync.cnt}
        waits = [None] * n
        for i, o in enumerate(ops):
            wl = {}
            for d in o["deps"]:
                od = ops[d]
                if od["dma"] is None and od["eng"] == o["eng"] and o["dma"] is None:
                    if o["eng"] == "pe" or not SAME_ENGINE_SYNC:
                        continue
                nm, v = mark[d]
                wl[nm] = max(wl.get(nm, 0), v)
            waits[i] = wl
        with nc.Block() as block:
            decos = {"pe": block.tensor, "act": block.scalar, "dve": block.vector,
                     "pool": block.gpsimd, "sp": block.sync}
            for eng in ENGS:
                idxs = per_eng[eng]

                def body(e, idxs=idxs):
                    waited = {}
                    for i in idxs:
                        o = ops[i]
                        for nm, v in waits[i].items():
                            if waited.get(nm, 0) >= v:
                                continue
                            e.wait_ge(sync.sems[nm], v)
                            waited[nm] = v
                        ins = o["fn"](e)
                        if mark[i] is not None:
                            nm, v = mark[i]
                            ins.then_inc(sync.sems[nm], 16 if o["dma"] is not None else 1)
                    for nm, v in final.items():
                        if v > 0 and waited.get(nm, 0) < v:
                            e.wait_ge(sync.sems[nm], v)

                decos[eng](body)


class PsumPool:
    def __init__(self, tensors, prefix):
        self.t = tensors
        self.prefix = prefix
        self.i = 0

    def get(self):
        k = self.i % len(self.t)
        self.i += 1
        return self.t[k], "%s%d" % (self.prefix, k)


def bcast(ap, axis, shape):
    return ap.unsqueeze(axis).to_broadcast(list(shape))


def phase_b0(nc, sync, G, weff, wpqT_d, skT_d):
    with ExitStack() as es:
        wq = es.enter_context(nc.sbuf_tensor("b0_wq", [128, 16, 1024], F32))
        sk = es.enter_context(nc.sbuf_tensor("b0_sk", [128, 16, 128], F32))
        P = Prog(nc, sync)
        for q in range(4):
            P.op("sp", lambda e, q=q: e.dma_start(
                out=wq[:, q * 4:(q + 1) * 4, :],
                in_=wpqT_d.rearrange("(g p) m -> p g m", p=128)[:, q * 4:(q + 1) * 4, :]),
                w=["wq"], dma="b0w")
        P.op("sp", lambda e: e.dma_start(out=sk[:], in_=skT_d.rearrange("(g p) n -> p g n", p=128)),
             w=["sk"], dma="b0w")
        for mc in range(8):
            for q in range(4):
                bank, bk = G["psf"].get()
                for j in range(4):
                    hp = q * 4 + j
                    P.op("pe", lambda e, bank=bank, hp=hp, mc=mc, j=j: e.matmul(
                        bank[:, j * 128:(j + 1) * 128], lhsT=wq[:, hp, mc * 128:(mc + 1) * 128],
                        rhs=sk[:, hp, :], start=True, stop=True), r=["wq", "sk"], w=[bk])
                eng = "act" if (mc * 4 + q) % 2 == 0 else "dve"
                if eng == "act":
                    P.op("act", lambda e, bank=bank, mc=mc, q=q: e.copy(
                        out=weff[:, mc, q * 512:(q + 1) * 512], in_=bank[:]), r=[bk], w=["weff"])
                else:
                    P.op("dve", lambda e, bank=bank, mc=mc, q=q: e.tensor_copy(
                        out=weff[:, mc, q * 512:(q + 1) * 512], in_=bank[:]), r=[bk], w=["weff"])
        P.emit()


def phase_bt(nc, sync, G, peer_u, peer_v, uv_scr):
    with ExitStack() as es:
        uv = [es.enter_context(nc.sbuf_tensor("bt_uv%d" % i, [128, 8, 2 * D], BF16)) for i in range(2)]
        P = Prog(nc, sync)
        for ch in range(16):
            bt_chunk(P, ch, uv, peer_u, peer_v, uv_scr)
        P.emit()


def bt_chunk(P, ch, uv, peer_u, peer_v, uv_scr):
    s = ch % 2
    k = "uv%d" % s
    rows = slice(ch * 1024, (ch + 1) * 1024)
    P.op("pool", lambda e: e.dma_start(
        out=uv[s][:, :, 0:D], in_=peer_u[rows, :].rearrange("(p r) d -> p r d", r=8)), w=[k], dma="btl%d" % s)
    P.op("pool", lambda e: e.dma_start(
        out=uv[s][:, :, D:2 * D], in_=peer_v[rows, :].rearrange("(p r) d -> p r d", r=8)), w=[k], dma="btl%d" % s)
    P.op("sp", lambda e: e.dma_start(
        out=uv_scr[rows, :].rearrange("(p r) d -> p r d", r=8), in_=uv[s][:]), r=[k], dma="bts%d" % s)


def phase_b(nc, sync, G, n_tiles, weff, h_scr, uv_scr, ln2g_d, ln2b_d, y_d, NS=16, ND=4, GS=4, NP=2,
            DSPLIT=0):
    with ExitStack() as es:
        def sb(name, shape, dt):
            return es.enter_context(nc.sbuf_tensor("b_" + name, shape, dt))
        ident, identf, iota16 = G["ident"], G["identf"], G["iota16"]
        g2 = sb("g2", [128, D], F32)
        b2 = sb("b2", [128, D], F32)
        h_t = [sb("h%d" % i, [128, D], F32) for i in range(2)]
        hb = [sb("hb%d" % i, [128, D], BF16) for i in range(2)]
        prod = [sb("prod%d" % i, [128, D], BF16) for i in range(NP)]
        junk2 = sb("junk2", [128, D], BF16)
        hT = sb("hT", [128, 8, 128], BF16)
        sc = sb("sc", [128, 16, 128], F32)
        sv = sb("sv", [128, 16, 16], F32)
        si = sb("si", [128, 16, 16], U32)
        sif = sb("sif", [128, 16, 16], F32)
        cand = sb("cand", [128, 8, 256], F32)
        ts = sb("ts", [128, 8, 16], F32)
        pos = sb("pos", [128, 8, 16], U32)
        pij = sb("pij", [128, 2, 128], U32)
        pijf = sb("pijf", [128, 2, 8, 16], F32)
        oh = [sb("oh%d" % i, [128, 8, 16, 16], BF16) for i in range(2)]
        ee = sb("ee", [128, 2, 128], F32)
        ef = sb("ef", [128, 128], F32)
        eidx = [sb("eidx%d" % i, [128, 128], I32) for i in range(2)]
        dsm = sb("dsm", [128, 8, 16], F32)
        ex = sb("ex", [128, 8, 16], F32)
        ssum = sb("ssum", [128, 8], F32)
        gate = [sb("gate%d" % i, [128, 128], F32) for i in range(2)]
        dots = sb("dots", [128, 128], F32)
        actt = sb("actt", [128, 128], F32)
        wt = sb("wt", [128, 128], F32)
        junk = sb("junk", [128, D], BF16)
        uvb = [sb("uvb%d" % i, [128, 2 * D], BF16) for i in range(NS)]
        dg = [sb("dg%d" % i, [128, 128], BF16) for i in range(ND)]
        y_t = [sb("y%d" % i, [128, D], F32) for i in range(2)]
        st6 = sb("st6", [128, 12], F32)
        mv2 = sb("mv2", [128, 2], F32)
        rstd = sb("rstd", [128, 1], F32)
        nmr = sb("nmr", [128, 1], F32)
        pv = G["pv"]
        NG = 128 // GS

        P = Prog(nc, sync)
        P.op("sp", lambda e: e.dma_start(out=g2[:], in_=ln2g_d[:, :]), w=["g2"], dma="bw")
        P.op("sp", lambda e: e.dma_start(out=b2[:], in_=ln2b_d[:, :]), w=["b2"], dma="bw")
        cnt = dict(u=0, d=0, p=0, q=0)
        scpool = PsumPool(G["psf"].t[0:2], "psf")
        dq = [(G["psf"].t[2 + i // 4], (i % 4) * 128) for i in range(8)]

        def load_h(t):
            s = t % 2
            P.op("sp", lambda e: e.dma_start(out=h_t[s][:], in_=h_scr[t * 128:(t + 1) * 128, :]),
                 w=["h%d" % s], dma="hld%d" % s)

        def front(t, OP=None):
            OP = OP or P.op
            s = t % 2
            hk_ = "h%d" % s
            OP("act", lambda e: e.copy(out=hb[s][:], in_=h_t[s][:]), r=[hk_], w=["hb%d" % s])
            pb, pbk = G["psb"].get()
            for c in range(8):
                OP("pe", lambda e, c=c: e.transpose(out=pb[:, c * 128:(c + 1) * 128],
                                                       in_=hb[s][:, c * 128:(c + 1) * 128], identity=ident[:]),
                     r=["hb%d" % s, "ident"], w=[pbk])
            OP("act", lambda e: e.copy(out=hT[:].rearrange("p c t -> p (c t)"), in_=pb[:]), r=[pbk], w=["hT"])
            for nb in range(4):
                bank, bk = scpool.get()
                for c in range(8):
                    OP("pe", lambda e, c=c, nb=nb, bank=bank: e.matmul(
                        bank[:], lhsT=hT[:, c, :], rhs=weff[:, c, nb * 512:(nb + 1) * 512],
                        start=(c == 0), stop=(c == 7)), r=["hT", "weff"], w=[bk])
                for q in range(4):
                    g = nb * 4 + q
                    OP("act", lambda e, q=q, g=g, bank=bank: e.copy(out=sc[:, g, :], in_=bank[:, q * 128:(q + 1) * 128]),
                         r=[bk], w=["sc%d" % g])
            for g in range(16):
                OP("dve", lambda e, g=g: e.max(out=sv[:, g, 0:8], in_=sc[:, g, :]), r=["sc%d" % g], w=["sva%d" % g])
            for g in range(16):
                OP("dve", lambda e, g=g: e.max_index(out=si[:, g, 0:8], in_max=sv[:, g, 0:8], in_values=sc[:, g, :]),
                     r=["sc%d" % g, "sva%d" % g], w=["sia%d" % g])
            for g in range(16):
                OP("dve", lambda e, g=g: e.match_replace(out=sc[:, g, :], in_to_replace=sv[:, g, 0:8],
                                                           in_values=sc[:, g, :], imm_value=-1e30),
                     r=["sva%d" % g], w=["sc%d" % g])
            for g in range(16):
                OP("dve", lambda e, g=g: e.max(out=sv[:, g, 8:16], in_=sc[:, g, :]), r=["sc%d" % g], w=["svb%d" % g])
            for g in range(16):
                OP("dve", lambda e, g=g: e.max_index(out=si[:, g, 8:16], in_max=sv[:, g, 8:16], in_values=sc[:, g, :]),
                     r=["sc%d" % g, "svb%d" % g], w=["sib%d" % g])
            allsv = ["sva%d" % g for g in range(16)] + ["svb%d" % g for g in range(16)]
            allsi = ["sia%d" % g for g in range(16)] + ["sib%d" % g for g in range(16)]
            OP("dve", lambda e: e.tensor_copy(out=sif[:], in_=si[:]), r=allsi, w=["sif"])
            sv4 = sv[:].rearrange("p (h two) k -> p h two k", two=2)
            OP("dve", lambda e: e.tensor_tensor(
                out=cand[:].rearrange("p h (i j) -> p h i j", j=16),
                in0=bcast(sv4[:, :, 0, :], 3, [128, 8, 16, 16]),
                in1=bcast(sv4[:, :, 1, :], 2, [128, 8, 16, 16]), op=ALU.add), r=allsv,
                w=["cand%d" % h for h in range(8)])
            for h in range(8):
                OP("dve", lambda e, h=h: e.max(out=ts[:, h, 0:8], in_=cand[:, h, :]), r=["cand%d" % h], w=["tsa%d" % h])
            for h in range(8):
                OP("dve", lambda e, h=h: e.max_index(out=pos[:, h, 0:8], in_max=ts[:, h, 0:8], in_values=cand[:, h, :]),
                     r=["cand%d" % h, "tsa%d" % h], w=["posa%d" % h])
            for h in range(8):
                OP("dve", lambda e, h=h: e.match_replace(out=cand[:, h, :], in_to_replace=ts[:, h, 0:8],
                                                           in_values=cand[:, h, :], imm_value=-1e30),
                     r=["tsa%d" % h], w=["cand%d" % h])
            for h in range(8):
                OP("dve", lambda e, h=h: e.max(out=ts[:, h, 8:16], in_=cand[:, h, :]), r=["cand%d" % h], w=["tsb%d" % h])
            for h in range(8):
                OP("dve", lambda e, h=h: e.max_index(out=pos[:, h, 8:16], in_max=ts[:, h, 8:16], in_values=cand[:, h, :]),
                     r=["cand%d" % h, "tsb%d" % h], w=["posb%d" % h])
            allts = ["tsa%d" % h for h in range(8)] + ["tsb%d" % h for h in range(8)]
            allpos = ["posa%d" % h for h in range(8)] + ["posb%d" % h for h in range(8)]
            posf = pos[:].rearrange("p h k -> p (h k)")
            OP("dve", lambda e: e.tensor_single_scalar(out=pij[:, 0, :], in_=posf, scalar=4,
                                                         op=ALU.logical_shift_right), r=allpos, w=["pij0"])
            OP("dve", lambda e: e.tensor_single_scalar(out=pij[:, 1, :], in_=posf, scalar=15,
                                                         op=ALU.bitwise_and), r=allpos, w=["pij1"])
            OP("dve", lambda e: e.tensor_copy(out=pijf[:].rearrange("p a h k -> p a (h k)"), in_=pij[:]),
                 r=["pij0", "pij1"], w=["pijf"])
            sif4 = sif[:].rearrange("p (h two) k -> p h two k", two=2)
            for a in range(2):
                OP("dve", lambda e, a=a: e.tensor_tensor(
                    out=oh[a][:], in0=bcast(pijf[:, a, :, :], 3, [128, 8, 16, 16]),
                    in1=iota16[:].unsqueeze(1).unsqueeze(1).to_broadcast([128, 8, 16, 16]),
                    op=ALU.is_equal), r=["pijf"], w=["oh%d" % a])
            for a in range(2):
                OP("dve", lambda e, a=a: e.tensor_tensor(
                    out=oh[a][:], in0=oh[a][:], in1=bcast(sif4[:, :, a, :], 2, [128, 8, 16, 16]),
                    op=ALU.mult), r=["sif"], w=["oh%d" % a])
            for a in range(2):
                OP("dve", lambda e, a=a: e.tensor_reduce(
                    out=ee[:, a, :], in_=oh[a][:].rearrange("p h k i -> p (h k) i"), axis=AX.X, op=ALU.add),
                    r=["oh%d" % a], w=["ee%d" % a])
            OP("dve", lambda e: e.scalar_tensor_tensor(out=ef[:], in0=ee[:, 0, :], scalar=128.0, in1=ee[:, 1, :],
                                                         op0=ALU.mult, op1=ALU.add), r=["ee0", "ee1"], w=["ef"])
            ek = "eidx%d" % s
            OP("dve", lambda e: e.tensor_copy(out=eidx[s][:], in_=ef[:]), r=["ef"], w=[ek])
            OP("dve", lambda e: e.tensor_tensor(out=dsm[:], in0=ts[:], in1=ts[:, :, 0:1].to_broadcast([128, 8, 16]),
                                                  op=ALU.subtract), r=allts, w=["dsm"])
            OP("act", lambda e: e.activation(out=ex[:], in_=dsm[:], func=AF.Exp), r=["dsm"], w=["ex"])
            OP("dve", lambda e: e.tensor_reduce(out=ssum[:], in_=ex[:], axis=AX.X, op=ALU.add), r=["ex"], w=["ssum"])
            OP("dve", lambda e: e.reciprocal(out=ssum[:], in_=ssum[:]), r=["ssum"], w=["ssum"])
            OP("dve", lambda e: e.tensor_tensor(out=gate[s][:].rearrange("p (h k) -> p h k", k=16), in0=ex[:],
                                                  in1=ssum[:].unsqueeze(2).to_broadcast([128, 8, 16]), op=ALU.mult),
                 r=["ex", "ssum"], w=["gate%d" % s])

        def s1(t, gq):
            s = t % 2
            ek = "eidx%d" % s
            cols = slice(gq * GS, (gq + 1) * GS)
            slots = []
            accs = []
            for hk in range(gq * GS, (gq + 1) * GS):
                u = cnt["u"] % NS
                cnt["u"] += 1
                slots.append(u)
                pi = cnt["p"] % NP
                cnt["p"] += 1
                P.op("pool", lambda e, u=u, hk=hk: e.indirect_dma_start(
                    out=uvb[u][:], out_offset=None, in_=uv_scr[:, :],
                    in_offset=bass.IndirectOffsetOnAxis(ap=eidx[s][:, hk:hk + 1], axis=0)),
                    r=[ek], w=["uvb%d" % u], dma="gu%d" % u)
                hk_ = "h%d" % s
                if (hk % GS) < DSPLIT:
                    P.op("dve", lambda e, u=u, pi=pi: e.tensor_tensor(
                        out=prod[pi][:], in0=uvb[u][:, 0:D], in1=hb[s][:], op=ALU.mult),
                        r=["uvb%d" % u, "hb%d" % s], w=["prod%d" % pi])
                    accs.append((pi, hk))
                else:
                    P.op("dve", lambda e, u=u, hk=hk: e.scalar_tensor_tensor(
                        out=junk[:], in0=uvb[u][:, 0:D], scalar=1.0, in1=h_t[s][:],
                        op0=ALU.mult, op1=ALU.mult, accum_out=dots[:, hk:hk + 1]),
                        r=["uvb%d" % u, hk_], w=["dots%d" % gq])
            for (pi, hk) in accs:
                P.op("act", lambda e, pi=pi, hk=hk: e.activation(
                    out=junk2[:], in_=prod[pi][:], func=AF.Copy, accum_out=dots[:, hk:hk + 1]),
                    r=["prod%d" % pi], w=["dots%d" % gq])
            P.op("act", lambda e: e.activation(out=actt[:, cols], in_=dots[:, cols], func=AF.Gelu),
                 r=["dots%d" % gq], w=["actt%d" % gq])
            return slots

        def s2(t, gq, slots):
            s = t % 2
            cols = slice(gq * GS, (gq + 1) * GS)
            P.op("dve", lambda e: e.tensor_tensor(out=wt[:, cols], in0=gate[s][:, cols], in1=actt[:, cols], op=ALU.mult),
                 r=["gate%d" % s, "actt%d" % gq], w=["wt%d" % gq])
            for j, hk in enumerate(range(gq * GS, (gq + 1) * GS)):
                u = slots[j]
                d = cnt["d"] % ND
                cnt["d"] += 1
                P.op("act", lambda e, d=d, hk=hk: e.activation(out=dg[d][:], in_=identf[:], func=AF.Copy,
                                                               scale=wt[:, hk:hk + 1]),
                     r=["wt%d" % gq, "identf"], w=["dg%d" % d])
                for half in range(2):
                    P.op("pe", lambda e, d=d, u=u, half=half, hk=hk: e.matmul(
                        pv[half][:], lhsT=dg[d][:], rhs=uvb[u][:, D + half * 512:D + (half + 1) * 512],
                        start=(hk == 0), stop=(hk == 127)), r=["dg%d" % d, "uvb%d" % u], w=["pv%d" % half])

        def tail(t):
            s = t % 2
            hk_, yk = "h%d" % s, "y%d" % s
            for half in range(2):
                P.op("dve", lambda e, half=half: e.scalar_tensor_tensor(
                    out=y_t[s][:, half * 512:(half + 1) * 512], in0=h_t[s][:, half * 512:(half + 1) * 512],
                    scalar=ALPHA, in1=pv[half][:], op0=ALU.mult, op1=ALU.add),
                    r=[hk_, "pv%d" % half], w=[yk])
            layer_norm_tail(P, y_t[s], yk, g2, "g2", b2, "b2", st6, mv2, rstd, nmr, "b", gb_eng="dve")
            P.op("sp", lambda e: e.dma_start(out=y_d[t * 128:(t + 1) * 128, :], in_=y_t[s][:]),
                 r=[yk], dma="yst%d" % s)

        load_h(0)
        front(0)
        for t in range(n_tiles):
            if t + 1 < n_tiles:
                load_h(t + 1)
            pend = None
            todo = []
            if t + 1 < n_tiles:
                front(t + 1, OP=lambda *a, **k: todo.append((a, k)))
            per = -(-len(todo) // max(1, NG - 4))
            for gq in range(NG):
                slots = s1(t, gq)
                if pend is not None:
                    s2(t, *pend)
                pend = (gq, slots)
                for (a, k) in todo[:per]:
                    P.op(*a, **k)
                del todo[:per]
            s2(t, *pend)
            for (a, k) in todo:
                P.op(*a, **k)
            tail(t)
        P.emit()


def layer_norm_tail(P, z, zk, g, gk, b, bk, st6, mv2, rstd, nmr, pfx, gb_eng="pool"):
    k6, k2, kr, kn = pfx + "st6", pfx + "mv2", pfx + "rstd", pfx + "nmr"
    for half in range(2):
        P.op("dve", lambda e, half=half: e.bn_stats(out=st6[:, half * 6:(half + 1) * 6],
                                                     in_=z[:, half * 512:(half + 1) * 512]), r=[zk], w=[k6])
    P.op("dve", lambda e: e.bn_aggr(out=mv2[:], in_=st6[:]), r=[k6], w=[k2])
    P.op("act", lambda e: e.activation(out=rstd[:], in_=mv2[:, 1:2], func=AF.Sqrt, bias=EPS_T[0][:], scale=1.0),
         r=[k2, "eps"], w=[kr])
    P.op("dve", lambda e: e.reciprocal(out=rstd[:], in_=rstd[:]), r=[kr], w=[kr])
    P.op("dve", lambda e: e.scalar_tensor_tensor(out=nmr[:], in0=mv2[:, 0:1], scalar=-1.0, in1=rstd[:],
                                                 op0=ALU.mult, op1=ALU.mult), r=[k2, kr], w=[kn])
    P.op("act", lambda e: e.activation(out=z[:], in_=z[:], func=AF.Identity, bias=nmr[:], scale=rstd[:]),
         r=[zk, kr, kn], w=[zk])
    P.op(gb_eng, lambda e: e.tensor_tensor(out=z[:], in0=z[:], in1=g[:], op=ALU.mult), r=[zk, gk], w=[zk])
    P.op(gb_eng, lambda e: e.tensor_tensor(out=z[:], in0=z[:], in1=b[:], op=ALU.add), r=[zk, bk], w=[zk])


def alloc_globals(nc, es, sync):
    G = {}
    psf = [es.enter_context(nc.psum_tensor("psf%d" % i, [128, 512], F32)) for i in range(4)]
    pv = [es.enter_context(nc.psum_tensor("pv%d" % i, [128, 512], F32)) for i in range(2)]
    psb = [es.enter_context(nc.psum_tensor("psb%d" % i, [128, 1024], BF16)) for i in range(2)]
    G["psf"] = PsumPool(psf, "psf")
    G["psb"] = PsumPool(psb, "psb")
    G["pv"] = pv
    G["ident"] = es.enter_context(nc.sbuf_tensor("sb_ident", [128, 128], BF16))
    G["identf"] = es.enter_context(nc.sbuf_tensor("sb_identf", [128, 128], F32))
    G["iota16"] = es.enter_context(nc.sbuf_tensor("sb_iota16", [128, 16], F32))
    G["eps"] = es.enter_context(nc.sbuf_tensor("sb_eps", [128, 1], F32))
    EPS_T[0] = G["eps"]
    return G


def phase_const(nc, sync, G, identf_d, iota16_d):
    P = Prog(nc, sync)
    P.op("sp", lambda e: e.dma_start(out=G["identf"][:], in_=identf_d[:, :]), w=["identf"], dma="c0")
    P.op("sp", lambda e: e.dma_start(out=G["iota16"][:], in_=iota16_d[:, :]), w=["iota16"], dma="c0")
    P.op("dve", lambda e: e.tensor_copy(out=G["ident"][:], in_=G["identf"][:]), r=["identf"], w=["ident"])
    P.op("dve", lambda e: e.memset(G["eps"][:], LN_EPS), w=["eps"])
    P.emit()


def build_b_only(n_tiles, tab_dt=F32):
    nc = bass.Bass("TRN2", target_bir_lowering=False)
    T = n_tiles * 128
    dt = lambda name, shape, dtype, kind="ExternalInput": nc.dram_tensor(name, shape, dtype, kind=kind).ap()
    h_d = dt("h_in", [T, D], F32)
    wpqT_d = dt("wpqT", [2048, D], F32)
    skT_d = dt("skT", [2048, 128], F32)
    pu = dt("peer_u", [N_EXP, D], tab_dt)
    pvv = dt("peer_v", [N_EXP, D], tab_dt)
    g2 = dt("ln2g", [128, D], F32)
    b2 = dt("ln2b", [128, D], F32)
    identf_d = dt("identf", [128, 128], F32)
    iota_d = dt("iota16", [128, 16], F32)
    y_d = dt("y", [T, D], F32, kind="ExternalOutput")
    with ExitStack() as es:
        sync = Sync(nc, es)
        G = alloc_globals(nc, es, sync)
        phase_const(nc, sync, G, identf_d, iota_d)
        weff = es.enter_context(nc.sbuf_tensor("sb_weff", [128, 8, 2048], BF16))
        phase_b0(nc, sync, G, weff, wpqT_d, skT_d)
        uv_scr = nc.dram_tensor("uv_scr", [N_EXP, 2 * D], BF16, kind="Internal").ap()
        phase_bt(nc, sync, G, pu, pvv, uv_scr)
        phase_b(nc, sync, G, n_tiles, weff, h_d, uv_scr, g2, b2, y_d)
    return nc


def phase_a0(nc, sync, G, mkT, mv_aug, mem_d, memg_d, memb_d, wkv_d):
    with ExitStack() as es:
        def sb(name, shape, dt):
            return es.enter_context(nc.sbuf_tensor("a0_" + name, shape, dt))
        ident = G["ident"]
        wkv = sb("wkv", [128, 8, 1024], BF16)
        mg = sb("mg", [128, D], F32)
        mb = sb("mb", [128, D], F32)
        mt = [sb("mt%d" % i, [128, D], F32) for i in range(2)]
        mn = sb("mn", [128, D], BF16)
        mnT = sb("mnT", [128, 8, 256], BF16)
        st6 = sb("st6", [128, 12], F32)
        mv2 = sb("mv2", [128, 2], F32)
        rstd = sb("rstd", [128, 1], F32)
        nmr = sb("nmr", [128, 1], F32)
        P = Prog(nc, sync)
        for c in range(8):
            P.op("pool", lambda e, c=c: e.dma_start(out=wkv[:, c, :], in_=wkv_d[c * 128:(c + 1) * 128, :]),
                 w=["wkv"], dma="a0w")
        P.op("sp", lambda e: e.dma_start(out=mg[:], in_=memg_d[:, :]), w=["mg"], dma="a0p")
        P.op("sp", lambda e: e.dma_start(out=mb[:], in_=memb_d[:, :]), w=["mb"], dma="a0p")
        P.op("dve", lambda e: e.memset(mv_aug[:], 1.0), w=["mv_aug"])
        for mc in range(2):
            mk_ = "mt%d" % mc
            P.op("sp", lambda e, mc=mc: e.dma_start(out=mt[mc][:], in_=mem_d[mc * 128:(mc + 1) * 128, :]),
                 w=[mk_], dma="a0m%d" % mc)
            layer_norm_tail(P, mt[mc], mk_, mg, "mg", mb, "mb", st6, mv2, rstd, nmr, "a0", gb_eng="dve")
            P.op("act", lambda e, mc=mc: e.copy(out=mn[:], in_=mt[mc][:]), r=[mk_], w=["mn"])
            pb, pbk = G["psb"].get()
            for c in range(8):
                P.op("pe", lambda e, c=c, pb=pb: e.transpose(out=pb[:, c * 128:(c + 1) * 128],
                                                             in_=mn[:, c * 128:(c + 1) * 128], identity=ident[:]),
                     r=["mn", "ident"], w=[pbk])
            P.op("act", lambda e, mc=mc, pb=pb: e.copy(out=mnT[:, :, mc * 128:(mc + 1) * 128],
                                                       in_=pb[:].rearrange("p (c t) -> p c t", t=128)),
                 r=[pbk], w=["mnT"])
        for h in range(4):
            bank, bk = G["psA"].get()
            for c in range(8):
                P.op("pe", lambda e, c=c, h=h, bank=bank: e.matmul(
                    bank[:, 0:256], lhsT=wkv[:, c, h * 128:(h + 1) * 128], rhs=mnT[:, c, :],
                    start=(c == 0), stop=(c == 7)), r=["wkv", "mnT"], w=[bk])
            P.op("act", lambda e, h=h, bank=bank: e.copy(out=mkT[:, h, :], in_=bank[:, 0:256]), r=[bk], w=["mkT"])
        for mc in range(2):
            bank, bk = G["psA"].get()
            for c in range(8):
                P.op("pe", lambda e, c=c, mc=mc, bank=bank: e.matmul(
                    bank[:], lhsT=mnT[:, c, mc * 128:(mc + 1) * 128], rhs=wkv[:, c, 512:1024],
                    start=(c == 0), stop=(c == 7)), r=["wkv", "mnT"], w=[bk])
            P.op("dve", lambda e, mc=mc, bank=bank: e.tensor_copy(
                out=mv_aug[:, mc, :, 0:128], in_=bank[:].rearrange("p (h d) -> p h d", d=128)),
                r=[bk], w=["mv_aug"])
        P.emit()


def phase_a1(nc, sync, G, n_tiles, mkT, mv_aug, xT_d, w1_d, b1_d, cc_d, ss_d, dt_d, wq_d, wk_d, cd_d,
             mask_d, gng_d, sink_d, brT_scr, bt=None):
    with ExitStack() as es:
        def sb(name, shape, dt):
            return es.enter_context(nc.sbuf_tensor("a1_" + name, shape, dt))
        ident = G["ident"]
        FM = FM_COLS
        w1 = sb("w1", [128, 8, NCOL_A1], BF16)
        b1 = sb("b1", [1, NCOL_A1], BF16)
        ones = sb("ones", [1, 128], BF16)
        dtab = sb("dtab", [128, 4, 128], F32)
        wqt = sb("wqt", [128, 2, 128], F32)
        wkt = sb("wkt", [128, 4], F32)
        cdt = sb("cdt", [128, 2], F32)
        msk = sb("msk", [128, 2, 128], F32)
        gng = sb("gng", [128, 512], F32)
        esink = sb("esink", [128, 8], F32)
        state = sb("state", [128, 2, 128], F32)
        state_bf = sb("state_bf", [128, 2, 128], BF16)
        xT = [sb("xT%d" % i, [128, 8, 128], BF16) for i in range(2)]
        cct = [sb("cc%d" % i, [128, 128], F32) for i in range(2)]
        sst = [sb("ss%d" % i, [128, 128], F32) for i in range(2)]
        qT = sb("qT", [128, 4, 128], BF16)
        pj = sb("pj", [128, FM_COLS], BF16)
        kT = [sb("kT%d" % i, [128, 2, 2, 128], BF16) for i in range(2)]
        kpad = sb("kpad", [128, 2, 2, 128], BF16)
        qspad = sb("qspad", [128, 2, 2, 128], BF16)
        vaug = [sb("vaug%d" % i, [128, 2, 65], BF16) for i in range(2)]
        mqT = sb("mqT", [128, 4, 128], BF16)
        tmp1 = sb("tmp1", [128, 4, 128], F32)
        tmp2 = sb("tmp2", [128, 4, 128], F32)
        rot = sb("rot", [128, 4, 128], BF16)
        ktok = sb("ktok", [128, 256], BF16)
        vret = sb("vret", [128, 4, 128], BF16)
        vw = sb("vw", [128, 4, 128], BF16)
        sg = sb("sg", [128, 512], F32)
        pT = sb("pT", [128, 2, 8, 128], BF16)
        pTc = sb("pTc", [128, 2, 4, 128], BF16)
        innerTm = sb("innerTm", [128, 4, 128], BF16)
        den = sb("den", [128, 8], F32)
        denc = sb("denc", [128, 4], F32)
        br = sb("br", [128, 3, 512], BF16)
        xn = sb("xn", [128, 512], F32)
        gst = sb("gst", [128, 24], F32)
        gmv = sb("gmv", [128, 4, 2], F32)
        grs = sb("grs", [128, 4], F32)
        brT = [sb("brT%d" % i, [128, 12, 128], BF16) for i in range(2)]
        uvt = [sb("uvt%d" % i, [128, 8, 2 * D], BF16) for i in range(2)] if bt is not None else None
        btc = dict(ch=0)

        P = Prog(nc, sync)
        for c in range(8):
            P.op("pool", lambda e, c=c: e.dma_start(out=w1[:, c, :], in_=w1_d[c * 128:(c + 1) * 128, :],
                                                    max_dma_last_dim=4096), w=["w1"], dma="a1w")
        P.op("pool", lambda e: e.dma_start(out=b1[:], in_=b1_d[:, :], max_dma_last_dim=4096), w=["b1"], dma="a1w")
        for (tt, dd, kk) in ((dtab, dt_d, "dtab"), (wqt, wq_d, "wqt")):
            P.op("sp", lambda e, tt=tt, dd=dd: e.dma_start(out=tt[:].rearrange("p a b -> p (a b)"), in_=dd[:, :]),
                 w=[kk], dma="a1p")
        P.op("sp", lambda e: e.dma_start(out=msk[:].rearrange("p a b -> p (a b)"), in_=mask_d[:, :]),
             w=["msk"], dma="a1p")
        for (tt, dd, kk) in ((wkt, wk_d, "wkt"), (cdt, cd_d, "cdt"), (gng, gng_d, "gng"), (esink, sink_d, "esink")):
            P.op("sp", lambda e, tt=tt, dd=dd: e.dma_start(out=tt[:], in_=dd[:, :]), w=[kk], dma="a1p")
        P.op("act", lambda e: e.activation(out=esink[:], in_=esink[:], func=AF.Exp), r=["esink"], w=["esink"])
        P.op("dve", lambda e: e.memset(ones[:], 1.0), w=["ones"])
        P.op("dve", lambda e: e.memset(state[:], 0.0), w=["state"])
        P.op("dve", lambda e: e.memset(state_bf[:], 0.0), w=["state_bf"])
        for i in range(2):
            P.op("dve", lambda e, i=i: e.memset(vaug[i][:], 1.0), w=["vaug%d" % i])
            P.op("dve", lambda e, i=i: e.memset(kT[i][:], 0.0), w=["kT%d" % i])
        P.op("dve", lambda e: e.memset(kpad[:], 0.0), w=["kpad"])
        P.op("dve", lambda e: e.memset(qspad[:], 0.0), w=["qspad"])

        def loads(t):
            s = t % 2
            P.op("pool", lambda e: e.dma_start(
                out=xT[s][:], in_=xT_d.rearrange("(c p) t -> p c t", p=128)[:, :, t * 128:(t + 1) * 128]),
                w=["xT%d" % s], dma="a1x%d" % s)
            P.op("sp", lambda e: e.dma_start(out=cct[s][:], in_=cc_d[:, t * 128:(t + 1) * 128]),
                 w=["cc%d" % s], dma="a1c%d" % s)
            P.op("sp", lambda e: e.dma_start(out=sst[s][:], in_=ss_d[:, t * 128:(t + 1) * 128]),
                 w=["ss%d" % s], dma="a1c%d" % s)

        def fm_group(s, fbs, bank, bk):
            xk = "xT%d" % s
            for j, fb in enumerate(fbs):
                for c in range(8):
                    P.op("pe", lambda e, c=c, fb=fb, j=j: e.matmul(
                        bank[:, j * 128:(j + 1) * 128], lhsT=w1[:, c, fb * 128:(fb + 1) * 128], rhs=xT[s][:, c, :],
                        start=(c == 0), stop=False), r=["w1", xk], w=[bk])
                P.op("pe", lambda e, fb=fb, j=j: e.matmul(
                    bank[:, j * 128:(j + 1) * 128], lhsT=b1[0:1, fb * 128:(fb + 1) * 128], rhs=ones[0:1, :],
                    start=False, stop=True), r=["b1", "ones"], w=[bk])

        def tm_group(s, col0, ncols, bank, bk):
            xk = "xT%d" % s
            for c in range(8):
                P.op("pe", lambda e, c=c: e.matmul(
                    bank[:, 0:ncols], lhsT=xT[s][:, c, :], rhs=w1[:, c, col0:col0 + ncols],
                    start=(c == 0), stop=False), r=["w1", xk], w=[bk])
            P.op("pe", lambda e: e.matmul(bank[:, 0:ncols], lhsT=ones[0:1, :], rhs=b1[0:1, col0:col0 + ncols],
                                          start=False, stop=True), r=["b1", "ones"], w=[bk])

        def do_tile(t):
            s = t % 2
            sp_ = 1 - s
            if t + 1 < n_tiles:
                loads(t + 1)
            for nb in range(5):
                c0 = nb * 512
                ncol = min(512, FM - c0)
                bpj, bkpj = G["psA"].get()
                tm_group(s, c0, ncol, bpj, bkpj)
                if nb % 2 == 0:
                    P.op("act", lambda e, c0=c0, ncol=ncol, bpj=bpj: e.copy(out=pj[:, c0:c0 + ncol], in_=bpj[:, 0:ncol]),
                         r=[bkpj], w=["pj%d" % nb])
                else:
                    P.op("dve", lambda e, c0=c0, ncol=ncol, bpj=bpj: e.tensor_copy(out=pj[:, c0:c0 + ncol],
                                                                                   in_=bpj[:, 0:ncol]),
                         r=[bkpj], w=["pj%d" % nb])

            def fm_T(fbs):
                pbx, pbxk = G["psb"].get()
                for j, fb in enumerate(fbs):
                    P.op("pe", lambda e, j=j, fb=fb, pbx=pbx: e.transpose(
                        out=pbx[:, j * 128:(j + 1) * 128], in_=pj[:, fb * 128:(fb + 1) * 128], identity=ident[:]),
                        r=["pj%d" % (fb // 4), "ident"], w=[pbxk])
                return pbx, pbxk

            bank, bk = fm_T([0, 1, 2, 3])
            P.op("act", lambda e: e.copy(out=qT[:].rearrange("p a b -> p (a b)"), in_=bank[:, 0:512]), r=[bk], w=["qT"])
            bankk, bkk = fm_T([4, 5])
            kTk = "kT%d" % s
            for hf in range(2):
                P.op("act", lambda e, hf=hf: e.copy(
                    out=kT[s][hf * 64:(hf + 1) * 64, :, hf, :],
                    in_=bankk[hf * 64:(hf + 1) * 64, 0:256].rearrange("p (g t) -> p g t", t=128)), r=[bkk], w=[kTk])
            bra, bkra = fm_T([6, 7, 8, 9])
            P.op("dve", lambda e: e.tensor_tensor(out=tmp1[:], in0=bra[:, 0:512].rearrange("p (a b) -> p a b", b=128),
                                                  in1=bcast(cct[s][:], 1, [128, 4, 128]), op=ALU.mult),
                 r=[bkra, "cc%d" % s], w=["tmp1"])
            brb, bkrb = fm_T([10, 11, 12, 13])
            P.op("dve", lambda e: e.tensor_tensor(out=tmp2[:], in0=brb[:, 0:512].rearrange("p (a b) -> p a b", b=128),
                                                  in1=bcast(sst[s][:], 1, [128, 4, 128]), op=ALU.mult),
                 r=[bkrb, "ss%d" % s], w=["tmp2"])
            P.op("pool", lambda e: e.tensor_tensor(out=rot[:], in0=tmp1[:], in1=tmp2[:], op=ALU.add),
                 r=["tmp1", "tmp2"], w=["rot"])
            for hf in range(2):
                rows = slice(hf * 64, (hf + 1) * 64)
                P.op("pool", lambda e, hf=hf, rows=rows: e.tensor_tensor(
                    out=qspad[rows, :, hf, :], in0=rot[rows, 0:2, :], in1=wqt[rows, :, :], op=ALU.mult),
                    r=["rot", "wqt"], w=["qspad"])
                P.op("pool", lambda e, hf=hf, rows=rows: e.tensor_copy(out=kpad[rows, :, hf, :], in_=rot[rows, 2:4, :]),
                     r=["rot"], w=["kpad"])
            bankm, bkm = fm_T([14, 15, 16, 17])
            P.op("act", lambda e: e.copy(out=mqT[:].rearrange("p a b -> p (a b)"), in_=bankm[:, 0:512]), r=[bkm], w=["mqT"])
            bav, bkav = G["psA"].get()
            tm_group(s, FM, 128, bav, bkav)
            vk = "vaug%d" % s
            P.op("act", lambda e: e.copy(out=vaug[s][:, :, 0:64], in_=bav[:, 0:128].rearrange("p (g d) -> p g d", d=64)),
                 r=[bkav], w=[vk])
            brv, bkrv = G["psA"].get()
            tm_group(s, FM + 128, 512, brv, bkrv)
            P.op("act", lambda e: e.copy(out=vret[:].rearrange("p a b -> p (a b)"), in_=brv[:]), r=[bkrv], w=["vret"])
            P.op("dve", lambda e: e.tensor_tensor(out=vw[:], in0=vret[:],
                                                  in1=bcast(wkt[:], 2, [128, 4, 128]), op=ALU.mult),
                 r=["vret", "wkt"], w=["vw"])
            brg, bkrg = G["psA"].get()
            tm_group(s, FM + 640, 512, brg, bkrg)
            P.op("act", lambda e: e.activation(out=sg[:], in_=brg[:], func=AF.Silu), r=[bkrg], w=["sg"])
            P.op("pool", lambda e: e.tensor_tensor(out=sg[:], in0=sg[:], in1=gng[:], op=ALU.mult),
                 r=["sg", "gng"], w=["sg"])
            whichs = [(0, s)] + ([(1, sp_)] if t > 0 else [])
            for hb2 in range(2):
                for (wi, slot) in whichs:
                    bl, bkl = G["psA"].get()
                    for j in range(4):
                        h = 4 * hb2 + j
                        i, half = h // 2, h % 2
                        P.op("pe", lambda e, j=j, i=i, half=half, slot=slot, bl=bl, hb2=hb2: e.matmul(
                            bl[:, j * 128:(j + 1) * 128], lhsT=kT[slot][:, hb2, half, :],
                            rhs=qT[:, i, :], start=True, stop=True),
                            r=["kT%d" % slot, "qT"], w=[bkl])
                    pk = "pT%d%d" % (wi, hb2)
                    P.op("act", lambda e, wi=wi, bl=bl, hb2=hb2: e.activation(
                        out=pT[:, wi, hb2 * 4:(hb2 + 1) * 4, :].rearrange("p a b -> p (a b)"), in_=bl[:],
                        func=AF.Exp, scale=0.125), r=[bkl], w=[pk])
                    P.op("pool", lambda e, wi=wi, hb2=hb2: e.tensor_tensor(
                        out=pT[:, wi, hb2 * 4:(hb2 + 1) * 4, :], in0=pT[:, wi, hb2 * 4:(hb2 + 1) * 4, :],
                        in1=bcast(msk[:, wi, :], 1, [128, 4, 128]), op=ALU.mult), r=[pk, "msk"], w=[pk])
            for hb2 in range(2):
                bo, bko = G["psA"].get()
                for j in range(4):
                    h = 4 * hb2 + j
                    if t > 0:
                        P.op("pe", lambda e, j=j, h=h, bo=bo, hb2=hb2: e.matmul(
                            bo[:, j * 65:(j + 1) * 65], lhsT=pT[:, 1, h, :], rhs=vaug[sp_][:, hb2, :],
                            start=True, stop=False), r=["pT1%d" % hb2, "vaug%d" % sp_], w=[bko])
                    P.op("pe", lambda e, j=j, h=h, bo=bo, hb2=hb2: e.matmul(
                        bo[:, j * 65:(j + 1) * 65], lhsT=pT[:, 0, h, :], rhs=vaug[s][:, hb2, :],
                        start=(t == 0), stop=True), r=["pT0%d" % hb2, vk], w=[bko])
                bo3 = bo[:, 0:260].rearrange("p (j d) -> p j d", d=65)
                dk = "den%d" % hb2
                P.op("dve", lambda e, bo3=bo3, hb2=hb2: e.tensor_tensor(
                    out=den[:, hb2 * 4:(hb2 + 1) * 4], in0=bo3[:, :, 64], in1=esink[:, hb2 * 4:(hb2 + 1) * 4],
                    op=ALU.add), r=[bko, "esink"], w=[dk])
                P.op("dve", lambda e, hb2=hb2: e.reciprocal(out=den[:, hb2 * 4:(hb2 + 1) * 4], in_=den[:, hb2 * 4:(hb2 + 1) * 4]),
                     r=[dk], w=[dk])
                P.op("dve", lambda e, bo3=bo3, hb2=hb2: e.tensor_tensor(
                    out=br[:, 0, hb2 * 256:(hb2 + 1) * 256].rearrange("p (j d) -> p j d", d=64),
                    in0=bo3[:, :, 0:64], in1=bcast(den[:, hb2 * 4:(hb2 + 1) * 4], 2, [128, 4, 64]), op=ALU.mult),
                    r=[bko, dk], w=["br0"])
            for mc in range(2):
                bl, bkl = G["psA"].get()
                for h in range(4):
                    P.op("pe", lambda e, h=h, mc=mc, bl=bl: e.matmul(
                        bl[:, h * 128:(h + 1) * 128], lhsT=mkT[:, h, mc * 128:(mc + 1) * 128], rhs=mqT[:, h, :],
                        start=True, stop=True), r=["mkT", "mqT"], w=[bkl])
                P.op("act", lambda e, mc=mc, bl=bl: e.activation(
                    out=pTc[:, mc, :, :].rearrange("p a b -> p (a b)"), in_=bl[:], func=AF.Exp,
                    scale=float(128 ** -0.5)), r=[bkl], w=["pTc%d" % mc])
            for hp2 in range(2):
                bo, bko = G["psA"].get()
                for j in range(2):
                    h = 2 * hp2 + j
                    for mc in range(2):
                        P.op("pe", lambda e, j=j, h=h, mc=mc, bo=bo: e.matmul(
                            bo[:, j * 129:(j + 1) * 129], lhsT=pTc[:, mc, h, :], rhs=mv_aug[:, mc, h, :],
                            start=(mc == 0), stop=(mc == 1)), r=["pTc%d" % mc, "mv_aug"], w=[bko])
                bo3 = bo[:, 0:258].rearrange("p (j d) -> p j d", d=129)
                dk = "denc%d" % hp2
                P.op("dve", lambda e, bo3=bo3, hp2=hp2: e.reciprocal(out=denc[:, hp2 * 2:(hp2 + 1) * 2], in_=bo3[:, :, 128]),
                     r=[bko], w=[dk])
                P.op("dve", lambda e, bo3=bo3, hp2=hp2: e.tensor_tensor(
                    out=br[:, 2, hp2 * 256:(hp2 + 1) * 256].rearrange("p (j d) -> p j d", d=128),
                    in0=bo3[:, :, 0:128], in1=bcast(denc[:, hp2 * 2:(hp2 + 1) * 2], 2, [128, 2, 128]), op=ALU.mult),
                    r=[bko, dk], w=["br2"])
            pb, pbk = G["psb"].get()
            for blk in range(2):
                P.op("pe", lambda e, blk=blk: e.transpose(out=pb[:, blk * 128:(blk + 1) * 128], in_=rot[:, 2 + blk, :],
                                                          identity=ident[:]), r=["rot", "ident"], w=[pbk])
            P.op("act", lambda e: e.copy(out=ktok[:], in_=pb[:, 0:256]), r=[pbk], w=["ktok"])
            bi, bki = G["psA"].get()
            for h in range(4):
                blk, half = h // 2, h % 2
                P.op("pe", lambda e, h=h, blk=blk, half=half: e.matmul(
                    bi[:, h * 128:(h + 1) * 128], lhsT=kpad[:, blk, half, :],
                    rhs=rot[:, blk, :], start=True, stop=True), r=["rot", "kpad"], w=[bki])
            P.op("dve", lambda e: e.tensor_tensor(out=innerTm[:], in0=bi[:].rearrange("p (a b) -> p a b", b=128),
                                                  in1=dtab[:], op=ALU.mult), r=[bki, "dtab"], w=["innerTm"])
            bo, bko = G["psA"].get()
            for h in range(4):
                blk, half = h // 2, h % 2
                P.op("pe", lambda e, h=h: e.matmul(
                    bo[:, h * 128:(h + 1) * 128], lhsT=innerTm[:, h, :], rhs=vret[:, h, :],
                    start=True, stop=(t == 0)), r=["innerTm", "vret"], w=[bko])
                if t > 0:
                    P.op("pe", lambda e, h=h, blk=blk, half=half: e.matmul(
                        bo[:, h * 128:(h + 1) * 128], lhsT=qspad[:, blk, half, :],
                        rhs=state_bf[:, blk, :], start=False, stop=True),
                        r=["qspad", "state_bf"], w=[bko])
            bkv, bkkv = G["psA"].get()
            for h in range(4):
                blk, half = h // 2, h % 2
                P.op("pe", lambda e, h=h, blk=blk, half=half: e.matmul(
                    bkv[:, h * 128:(h + 1) * 128], lhsT=ktok[:, blk * 128:(blk + 1) * 128],
                    rhs=vw[:, h, :], start=True, stop=True), r=["ktok", "vw"], w=[bkkv])
            for h in range(4):
                blk, half = h // 2, h % 2
                rows = slice(half * 64, (half + 1) * 64)
                P.op("dve", lambda e, h=h, blk=blk, rows=rows: e.scalar_tensor_tensor(
                    out=state[rows, blk, :], in0=state[rows, blk, :], scalar=cdt[rows, blk:blk + 1],
                    in1=bkv[rows, h * 128:(h + 1) * 128], op0=ALU.mult, op1=ALU.add),
                    r=["state", "cdt", bkkv], w=["state"])
            P.op("act", lambda e: e.copy(out=state_bf[:], in_=state[:]), r=["state"], w=["state_bf"])
            for h in range(4):
                P.op("dve", lambda e, h=h: e.bn_stats(out=gst[:, h * 6:(h + 1) * 6], in_=bo[:, h * 128:(h + 1) * 128]),
                     r=[bko], w=["gst%d" % h])
                P.op("dve", lambda e, h=h: e.bn_aggr(out=gmv[:, h, :], in_=gst[:, h * 6:(h + 1) * 6]),
                     r=["gst%d" % h], w=["gmv"])
            P.op("act", lambda e: e.activation(out=grs[:], in_=gmv[:, :, 1], func=AF.Sqrt, bias=EPS_T[0][:], scale=1.0),
                 r=["gmv", "eps"], w=["grs"])
            P.op("dve", lambda e: e.reciprocal(out=grs[:], in_=grs[:]), r=["grs"], w=["grs"])
            for h in range(4):
                P.op("dve", lambda e, h=h: e.tensor_scalar(
                    out=xn[:, h * 128:(h + 1) * 128], in0=bo[:, h * 128:(h + 1) * 128], scalar1=gmv[:, h, 0:1],
                    scalar2=grs[:, h:h + 1], op0=ALU.subtract, op1=ALU.mult), r=[bko, "gmv", "grs"], w=["xn"])
            P.op("pool", lambda e: e.tensor_tensor(out=br[:, 1, :], in0=xn[:], in1=sg[:], op=ALU.mult),
                 r=["xn", "sg"], w=["br1"])
            pba, pbka = G["psb"].get()
            for b in range(2):
                for kc in range(4):
                    P.op("pe", lambda e, b=b, kc=kc: e.transpose(
                        out=pba[:, (b * 4 + kc) * 128:(b * 4 + kc + 1) * 128], in_=br[:, b, kc * 128:(kc + 1) * 128],
                        identity=ident[:]), r=["br%d" % b, "ident"], w=[pbka])
            bt = "brT%d" % s
            P.op("act", lambda e: e.copy(out=brT[s][:, 0:8, :].rearrange("p a b -> p (a b)"), in_=pba[:]),
                 r=[pbka], w=[bt])
            pbc, pbkc = G["psb"].get()
            for kc in range(4):
                P.op("pe", lambda e, kc=kc: e.transpose(out=pbc[:, kc * 128:(kc + 1) * 128],
                                                        in_=br[:, 2, kc * 128:(kc + 1) * 128], identity=ident[:]),
                     r=["br2", "ident"], w=[pbkc])
            P.op("dve", lambda e: e.tensor_copy(out=brT[s][:, 8:12, :].rearrange("p a b -> p (a b)"), in_=pbc[:, 0:512]),
                 r=[pbkc], w=[bt])
            P.op("sp", lambda e: e.dma_start(out=brT_scr[t, :, :], in_=brT[s][:].rearrange("p a b -> p (a b)")),
                 r=[bt], dma="a1s%d" % s)

        loads(0)
        every = max(1, n_tiles // 16)
        for t in range(n_tiles):
            if bt is not None and t % every == 0 and btc["ch"] < 16:
                bt_chunk(P, btc["ch"], uvt, *bt)
                btc["ch"] += 1
            do_tile(t)
        while bt is not None and btc["ch"] < 16:
            bt_chunk(P, btc["ch"], uvt, *bt)
            btc["ch"] += 1
        P.emit()


def phase_a2(nc, sync, G, n_tiles, x_d, xT_d, wg_d, bg_d, wbr_d, wout_d, ln1g_d, ln1b_d, brT_scr, h_scr):
    with ExitStack() as es:
        def sb(name, shape, dt):
            return es.enter_context(nc.sbuf_tensor("a2_" + name, shape, dt))
        ident = G["ident"]
        wg = sb("wg", [128, 8, NCOL_G], BF16)
        bg = sb("bg", [1, NCOL_G], BF16)
        ones = sb("ones", [1, 128], BF16)
        wbr = sb("wbr", [128, 12, D], BF16)
        wout = sb("wout", [128, 8, D], BF16)
        g1 = sb("g1", [128, D], F32)
        b1 = sb("b1", [128, D], F32)
        xT = [sb("xT%d" % i, [128, 8, 128], BF16) for i in range(2)]
        xt = [sb("x%d" % i, [128, D], F32) for i in range(2)]
        brT = [sb("brT%d" % i, [128, 12, 128], BF16) for i in range(2)]
        gsb = [sb("gsb%d" % i, [128, 512], F32) for i in range(2)]
        acc = sb("acc", [128, D], F32)
        tmp = [sb("tmp%d" % i, [128, 512], F32) for i in range(2)]
        mbf = sb("mbf", [128, D], BF16)
        mT = sb("mT", [128, 8, 128], BF16)
        z = [sb("z%d" % i, [128, D], F32) for i in range(2)]
        st6 = sb("st6", [128, 12], F32)
        mv2 = sb("mv2", [128, 2], F32)
        rstd = sb("rstd", [128, 1], F32)
        nmr = sb("nmr", [128, 1], F32)

        P = Prog(nc, sync)
        for c in range(8):
            P.op("pool", lambda e, c=c: e.dma_start(out=wg[:, c, :], in_=wg_d[c * 128:(c + 1) * 128, :],
                                                    max_dma_last_dim=4096), w=["wg"], dma="a2w")
        P.op("pool", lambda e: e.dma_start(out=bg[:], in_=bg_d[:, :], max_dma_last_dim=4096), w=["bg"], dma="a2w")
        for c in range(12):
            P.op("pool", lambda e, c=c: e.dma_start(out=wbr[:, c, :], in_=wbr_d[c * 128:(c + 1) * 128, :]),
                 w=["wbr"], dma="a2w")
        for c in range(8):
            P.op("pool", lambda e, c=c: e.dma_start(out=wout[:, c, :], in_=wout_d[c * 128:(c + 1) * 128, :]),
                 w=["wout"], dma="a2w")
        P.op("sp", lambda e: e.dma_start(out=g1[:], in_=ln1g_d[:, :]), w=["g1"], dma="a2p")
        P.op("sp", lambda e: e.dma_start(out=b1[:], in_=ln1b_d[:, :]), w=["b1"], dma="a2p")
        P.op("dve", lambda e: e.memset(ones[:], 1.0), w=["ones"])
        cnt = dict(g=0)

        def loads(t):
            s = t % 2
            P.op("pool", lambda e: e.dma_start(
                out=xT[s][:], in_=xT_d.rearrange("(c p) t -> p c t", p=128)[:, :, t * 128:(t + 1) * 128]),
                w=["xT%d" % s], dma="a2x%d" % s)
            P.op("sp", lambda e: e.dma_start(out=xt[s][:], in_=x_d[t * 128:(t + 1) * 128, :]),
                 w=["x%d" % s], dma="a2l%d" % s)
            P.op("sp", lambda e: e.dma_start(out=brT[s][:].rearrange("p a b -> p (a b)"), in_=brT_scr[t, :, :]),
                 w=["brT%d" % s], dma="a2l%d" % s)

        def do_tile(t):
            s = t % 2
            if t + 1 < n_tiles:
                loads(t + 1)
            xk, xtk, btk = "xT%d" % s, "x%d" % s, "brT%d" % s
            for b in range(3):
                for half in range(2):
                    col0 = b * 1024 + half * 512
                    gi = cnt["g"] % 2
                    cnt["g"] += 1
                    bgk, bkg = G["psA"].get()
                    for c in range(8):
                        P.op("pe", lambda e, c=c, col0=col0, bgk=bgk: e.matmul(
                            bgk[:], lhsT=xT[s][:, c, :], rhs=wg[:, c, col0:col0 + 512], start=(c == 0), stop=False),
                            r=["wg", xk], w=[bkg])
                    P.op("pe", lambda e, col0=col0, bgk=bgk: e.matmul(
                        bgk[:], lhsT=ones[0:1, :], rhs=bg[0:1, col0:col0 + 512], start=False, stop=True),
                        r=["bg", "ones"], w=[bkg])
                    P.op("act", lambda e, gi=gi, bgk=bgk: e.activation(out=gsb[gi][:], in_=bgk[:], func=AF.Sigmoid),
                         r=[bkg], w=["gsb%d" % gi])
                    by, bky = G["psA"].get()
                    for kc in range(4):
                        P.op("pe", lambda e, kc=kc, b=b, half=half, by=by: e.matmul(
                            by[:], lhsT=brT[s][:, b * 4 + kc, :], rhs=wbr[:, b * 4 + kc, half * 512:(half + 1) * 512],
                            start=(kc == 0), stop=(kc == 3)), r=["wbr", btk], w=[bky])
                    ak = "acc%d" % half
                    if b == 0:
                        P.op("dve", lambda e, gi=gi, half=half, by=by: e.tensor_tensor(
                            out=acc[:, half * 512:(half + 1) * 512], in0=by[:], in1=gsb[gi][:], op=ALU.mult),
                            r=[bky, "gsb%d" % gi], w=[ak])
                    else:
                        P.op("dve", lambda e, gi=gi, by=by: e.tensor_tensor(
                            out=tmp[gi][:], in0=by[:], in1=gsb[gi][:], op=ALU.mult),
                            r=[bky, "gsb%d" % gi], w=["tmp%d" % gi])
                        if b == 1:
                            P.op("pool", lambda e, gi=gi, half=half: e.tensor_tensor(
                                out=acc[:, half * 512:(half + 1) * 512], in0=acc[:, half * 512:(half + 1) * 512],
                                in1=tmp[gi][:], op=ALU.add), r=[ak, "tmp%d" % gi], w=[ak])
                        else:
                            P.op("pool", lambda e, gi=gi, half=half: e.tensor_tensor(
                                out=mbf[:, half * 512:(half + 1) * 512], in0=acc[:, half * 512:(half + 1) * 512],
                                in1=tmp[gi][:], op=ALU.add), r=[ak, "tmp%d" % gi], w=["mbf"])
            pb, pbk = G["psb"].get()
            for c in range(8):
                P.op("pe", lambda e, c=c: e.transpose(out=pb[:, c * 128:(c + 1) * 128], in_=mbf[:, c * 128:(c + 1) * 128],
                                                      identity=ident[:]), r=["mbf", "ident"], w=[pbk])
            P.op("act", lambda e: e.copy(out=mT[:].rearrange("p a b -> p (a b)"), in_=pb[:]), r=[pbk], w=["mT"])
            zk = "z%d" % s
            for half in range(2):
                bz, bkz = G["psA"].get()
                for c in range(8):
                    P.op("pe", lambda e, c=c, half=half, bz=bz: e.matmul(
                        bz[:], lhsT=mT[:, c, :], rhs=wout[:, c, half * 512:(half + 1) * 512],
                        start=(c == 0), stop=(c == 7)), r=["mT", "wout"], w=[bkz])
                P.op("dve", lambda e, half=half, bz=bz: e.scalar_tensor_tensor(
                    out=z[s][:, half * 512:(half + 1) * 512], in0=xt[s][:, half * 512:(half + 1) * 512],
                    scalar=ALPHA, in1=bz[:], op0=ALU.mult, op1=ALU.add), r=[xtk, bkz], w=[zk])
            layer_norm_tail(P, z[s], zk, g1, "g1", b1, "b1", st6, mv2, rstd, nmr, "a2", gb_eng="pool")
            P.op("sp", lambda e: e.dma_start(out=h_scr[t * 128:(t + 1) * 128, :], in_=z[s][:]), r=[zk],
                 dma="a2s%d" % s)

        loads(0)
        for t in range(n_tiles):
            do_tile(t)
        P.emit()


IN_SPECS = [
    ("x", lambda T: [T, D], F32), ("xT", lambda T: [D, T], F32), ("mem", lambda T: [256, D], F32),
    ("memg", lambda T: [128, D], F32), ("memb", lambda T: [128, D], F32), ("wkv", lambda T: [D, D], F32),
    ("w1", lambda T: [D, NCOL_A1], F32), ("b1", lambda T: [1, NCOL_A1], F32),
    ("cc", lambda T: [128, T], F32), ("ss", lambda T: [128, T], F32),
    ("dtab", lambda T: [128, 512], F32), ("wqt", lambda T: [128, 256], F32), ("wkt", lambda T: [128, 4], F32),
    ("cdt", lambda T: [128, 2], F32), ("mask", lambda T: [128, 256], F32), ("gng", lambda T: [128, 512], F32),
    ("sink", lambda T: [128, 8], F32),
    ("wg", lambda T: [D, NCOL_G], F32), ("bg", lambda T: [1, NCOL_G], F32), ("wbr", lambda T: [1536, D], F32),
    ("wout", lambda T: [D, D], F32), ("ln1g", lambda T: [128, D], F32), ("ln1b", lambda T: [128, D], F32),
    ("wpqT", lambda T: [2048, D], F32), ("skT", lambda T: [2048, 128], F32),
    ("peer_u", lambda T: [N_EXP, D], F32), ("peer_v", lambda T: [N_EXP, D], F32),
    ("ln2g", lambda T: [128, D], F32), ("ln2b", lambda T: [128, D], F32),
    ("identf", lambda T: [128, 128], F32), ("iota16", lambda T: [128, 16], F32),
]


def build_full(n_tiles, debug_h=False, stop_after=None):
    nc = bass.Bass("TRN2", target_bir_lowering=False)
    T = n_tiles * 128
    d = {}
    for name, shp, dtp in IN_SPECS:
        d[name] = nc.dram_tensor(name, shp(T), dtp, kind="ExternalInput").ap()
    y_d = nc.dram_tensor("y", [T, D], F32, kind="ExternalOutput").ap()
    h_scr = nc.dram_tensor("h_scr", [T, D], F32, kind="ExternalOutput" if debug_h else "Internal").ap()
    brT_scr = nc.dram_tensor("brT_scr", [n_tiles, 128, 1536], BF16,
                             kind="ExternalOutput" if debug_h else "Internal").ap()
    with ExitStack() as es:
        sync = Sync(nc, es)
        G = alloc_globals(nc, es, sync)
        G["psA"] = PsumPool(G["psf"].t + G["pv"], "psA")
        phase_const(nc, sync, G, d["identf"], d["iota16"])
        uv_scr = nc.dram_tensor("uv_scr", [N_EXP, 2 * D], BF16, kind="Internal").ap()
        with ExitStack() as es1:
            mkT = es1.enter_context(nc.sbuf_tensor("sb_mkT", [128, 4, 256], BF16))
            mv_aug = es1.enter_context(nc.sbuf_tensor("sb_mvaug", [128, 2, 4, 129], BF16))
            phase_a0(nc, sync, G, mkT, mv_aug, d["mem"], d["memg"], d["memb"], d["wkv"])
            if stop_after == "a0":
                return nc
            phase_a1(nc, sync, G, n_tiles, mkT, mv_aug, d["xT"], d["w1"], d["b1"], d["cc"], d["ss"], d["dtab"],
                     d["wqt"], d["wkt"], d["cdt"], d["mask"], d["gng"], d["sink"], brT_scr,
                     bt=(d["peer_u"], d["peer_v"], uv_scr))
        if stop_after == "a1":
            return nc
        phase_a2(nc, sync, G, n_tiles, d["x"], d["xT"], d["wg"], d["bg"], d["wbr"], d["wout"], d["ln1g"],
                 d["ln1b"], brT_scr, h_scr)
        if stop_after == "a2":
            return nc
        with ExitStack() as es2:
            weff = es2.enter_context(nc.sbuf_tensor("sb_weff", [128, 8, 2048], BF16))
            phase_b0(nc, sync, G, weff, d["wpqT"], d["skT"])
            phase_b(nc, sync, G, n_tiles, weff, h_scr, uv_scr, d["ln2g"], d["ln2b"], y_d)
    return nc


def _w_in_cols():
    fm = list(range(0, 512))
    fm += list(range(512, 576)) * 2 + list(range(576, 640)) * 2
    rq0, rk0 = 768, 1024
    fm += list(range(rq0, rq0 + 256)) + list(range(rk0, rk0 + 256))
    sw = []
    for base in (rq0, rk0):
        for h in range(4):
            hb = base + 64 * h
            sw += list(range(hb + 32, hb + 64)) + list(range(hb, hb + 32))
    fm += sw
    fm += list(range(2304, 2816))
    tm = list(range(640, 768)) + list(range(1280, 1792)) + list(range(1792, 2304))
    return np.array(fm + tm), np.arange(2816, 5888)


def _const_tables(T):
    half = 32
    theta = (1.0 / np.power(np.float32(10000.0), np.linspace(0.0, 1.0, half, dtype=np.float32))).astype(np.float32)
    pos = np.arange(T, dtype=np.float32)
    ang = (pos[:, None] * theta[None, :]).astype(np.float32)
    cos, sin = np.cos(ang).astype(np.float32), np.sin(ang).astype(np.float32)
    p = np.arange(128)
    cc = np.ascontiguousarray(cos[:, p % 32].T)
    sgn = np.where((p % 64) < 32, -1.0, 1.0).astype(np.float32)
    ss = np.ascontiguousarray((sin[:, p % 32] * sgn[None, :]).T)
    lg = np.log(1.0 - 2.0 ** (-5.0 - np.arange(4, dtype=np.float64)))
    i = np.arange(128, dtype=np.float64)
    diff = i[None, :] - i[:, None]
    dt = np.zeros((128, 4, 128), np.float64)
    for h in range(4):
        dt[:, h, :] = np.where(diff >= 0, np.exp(lg[h] * np.maximum(diff, 0.0)), 0.0) * 0.125
    wq = np.zeros((128, 2, 128), np.float64)
    cd = np.zeros((128, 2), np.float64)
    for blk in range(2):
        for hf in range(2):
            h = blk * 2 + hf
            wq[hf * 64:(hf + 1) * 64, blk, :] = np.exp(lg[h] * (i + 1.0))[None, :]
            cd[hf * 64:(hf + 1) * 64, blk] = np.exp(lg[h] * 128.0)
    wk = np.zeros((128, 4), np.float64)
    for h in range(4):
        wk[:, h] = np.exp(lg[h] * (127.0 - i)) * 0.125
    k = np.arange(128)[:, None]
    q = np.arange(128)[None, :]
    mask = np.concatenate([(k <= q), (k > q)], axis=1).astype(np.float32)
    f = lambda a: np.ascontiguousarray(a.astype(np.float32))
    return dict(cc=cc, ss=ss, dtab=f(dt.reshape(128, 512)), wqt=f(wq.reshape(128, 256)), wkt=f(wk), cdt=f(cd),
                mask=mask, identf=np.eye(128, dtype=np.float32),
                iota16=f(np.broadcast_to(np.arange(16.0), (128, 16))))


def _rep(v, n=128):
    return np.ascontiguousarray(np.broadcast_to(np.asarray(v, np.float32)[None, :], (n, v.shape[-1])))


def host_prep(inputs, T):
    g = lambda n: np.asarray(inputs[n], np.float32)[0]
    c1, cg = _w_in_cols()
    w_in, b_in = g("w_in"), g("b_in")
    sh = dict(
        memg=_rep(g("mem_ln_g")), memb=_rep(g("mem_ln_b")), wkv=g("w_mem_kv"),
        w1=np.ascontiguousarray(w_in[:, c1]), b1=np.ascontiguousarray(b_in[c1][None, :]),
        gng=_rep(g("ret_gn_g")), sink=_rep(g("attn_sinks")),
        wg=np.ascontiguousarray(w_in[:, cg]), bg=np.ascontiguousarray(b_in[cg][None, :]),
        wbr=np.ascontiguousarray(np.concatenate([g("w_branch_attn"), g("w_branch_ret"), g("w_branch_mem")], axis=0)),
        wout=g("w_out"), ln1g=_rep(g("ln1_g")), ln1b=_rep(g("ln1_b")),
        wpqT=np.ascontiguousarray(g("w_peer_q").T),
        skT=np.ascontiguousarray(g("peer_sub_keys").transpose(0, 1, 3, 2).reshape(2048, 128)),
        peer_u=g("peer_u"), peer_v=g("peer_v"), ln2g=_rep(g("ln2_g")), ln2b=_rep(g("ln2_b")),
    )
    sh.update(_const_tables(T))
    return sh


def kernel(**inputs):
    x = np.asarray(inputs["x"], np.float32)
    mem = np.asarray(inputs["mem"], np.float32)
    B, S, _ = x.shape
    n_tiles = S // 128
    sh = host_prep(inputs, S)
    in_maps = []
    for b in range(B):
        m = dict(sh)
        m["x"] = np.ascontiguousarray(x[b])
        m["xT"] = np.ascontiguousarray(x[b].T)
        m["mem"] = np.ascontiguousarray(mem[b])
        in_maps.append(m)
    nc = build_full(n_tiles)
    res = run_bass_kernel_spmd(nc, in_maps, core_ids=list(range(B)))
    return np.stack([r["y"] for r in res.results], axis=0).astype(np.float32)
```

```python
import bisect
import numpy as np
import ml_dtypes
from contextlib import ExitStack

import concourse.bass as bass
import concourse.mybir as mybir
from concourse.bass_utils import run_bass_kernel_spmd

F32 = mybir.dt.float32
BF16 = mybir.dt.bfloat16
I32 = mybir.dt.int32
U32 = mybir.dt.uint32
AF = mybir.ActivationFunctionType
ALU = mybir.AluOpType
AX = mybir.AxisListType

D = 1024
SEQ = 8192
NCORES = 8
ALPHA = 2.0 ** 0.25
LN_EPS = 1e-5
N_EXP = 16384

FM_COLS = 18 * 128
TM_A1 = 128 + 512 + 512
NCOL_A1 = FM_COLS + TM_A1
NCOL_G = 3072

SAME_ENGINE_SYNC = True
EPS_T = [None]


class Sync:
    def __init__(self, nc, es):
        self.nc = nc
        self.es = es
        self.sems = {}
        self.cnt = {}

    def sem(self, name):
        if name not in self.sems:
            self.sems[name] = self.es.enter_context(self.nc.semaphore("s_" + name))
            self.cnt[name] = 0
        return self.sems[name]


ENGS = ("pe", "act", "dve", "pool", "sp")


class Prog:
    def __init__(self, nc, sync):
        self.nc = nc
        self.sync = sync
        self.ops = []
        self.lastw = {}
        self.readers = {}

    def op(self, eng, fn, r=(), w=(), dma=None):
        i = len(self.ops)
        deps = set()
        for k in r:
            if k in self.lastw:
                deps.add(self.lastw[k])
        for k in w:
            if k in self.lastw:
                deps.add(self.lastw[k])
            deps.update(self.readers.get(k, ()))
        for k in r:
            self.readers.setdefault(k, []).append(i)
        for k in w:
            self.lastw[k] = i
            self.readers[k] = []
        deps.discard(i)
        self.ops.append(dict(eng=eng, fn=fn, deps=deps, dma=dma))
        return i

    def emit(self):
        nc, sync, ops = self.nc, self.sync, self.ops
        n = len(ops)
        per_eng = {e: [] for e in ENGS}
        for i, o in enumerate(ops):
            per_eng[o["eng"]].append(i)
        need = [False] * n
        for i, o in enumerate(ops):
            for d in o["deps"]:
                od = ops[d]
                if od["dma"] is not None:
                    continue
                if od["eng"] == o["eng"] and o["dma"] is None:
                    if o["eng"] == "pe" or not SAME_ENGINE_SYNC:
                        continue
                need[d] = True
        for e in ENGS:
            for i in reversed(per_eng[e]):
                if ops[i]["dma"] is None:
                    need[i] = True
                    break
        mark = [None] * n
        for i, o in enumerate(ops):
            if o["dma"] is not None:
                nm = "d_" + o["dma"]
                sync.sem(nm)
                sync.cnt[nm] += 16
                mark[i] = (nm, sync.cnt[nm])
            elif need[i]:
                nm = "e_" + o["eng"]
                sync.sem(nm)
                sync.cnt[nm] += 1
                mark[i] = (nm, sync.cnt[nm])
        final = {nm: sync.cnt[nm] for nm in sync.cnt}
        dma_hist = {}
        for i, o in enumerate(ops):
            if o["dma"] is not None:
                nm, v = mark[i]
                dma_hist.setdefault(nm, []).append((i, v))
        waits = [None] * n
        for i, o in enumerate(ops):
            wl = {}
            for d in o["deps"]:
                od = ops[d]
                if od["dma"] is None and od["eng"] == o["eng"] and o["dma"] is None:
                    if o["eng"] == "pe" or not SAME_ENGINE_SYNC:
                        continue
                nm, v = mark[d]
                if od["dma"] is not None:
                    hist = dma_hist[nm]
                    k = bisect.bisect_left(hist, (i, 0)) - 1
                    v = max(v, hist[k][1])
                wl[nm] = max(wl.get(nm, 0), v)
            waits[i] = wl
        with nc.Block() as block:
            decos = {"pe": block.tensor, "act": block.scalar, "dve": block.vector,
                     "pool": block.gpsimd, "sp": block.sync}
            for eng in ENGS:
                idxs = per_eng[eng]

                def body(e, idxs=idxs):
                    waited = {}
                    for i in idxs:
                        o = ops[i]
                        for nm, v in waits[i].items():
                            if waited.get(nm, 0) >= v:
                                continue
                            e.wait_ge(sync.sems[nm], v)
                            waited[nm] = v
                        ins = o["fn"](e)
                        if mark[i] is not None:
                            nm, v = mark[i]
                            ins.then_inc(sync.sems[nm], 16 if o["dma"] is not None else 1)
                    for nm, v in final.items():
                        if v > 0 and waited.get(nm, 0) < v:
                            e.wait_ge(sync.sems[nm], v)

                decos[eng](body)


class PsumPool:
    def __init__(self, tensors, prefix):
        self.t = tensors
        self.prefix = prefix
        self.i = 0

    def get(self):
        k = self.i % len(self.t)
        self.i += 1
        return self.t[k], "%s%d" % (self.prefix, k)


def bcast(ap, axis, shape):
    return ap.unsqueeze(axis).to_broadcast(list(shape))


def phase_b0(nc, sync, G, weff, wpqT_d, skT_d):
    with ExitStack() as es:
        wq = es.enter_context(nc.sbuf_tensor("b0_wq", [128, 16, 1024], F32))
        sk = es.enter_context(nc.sbuf_tensor("b0_sk", [128, 16, 128], F32))
        P = Prog(nc, sync)
        for q in range(4):
            P.op("sp", lambda e, q=q: e.dma_start(
                out=wq[:, q * 4:(q + 1) * 4, :],
                in_=wpqT_d.rearrange("(g p) m -> p g m", p=128)[:, q * 4:(q + 1) * 4, :]),
                w=["wq"], dma="b0w")
        P.op("sp", lambda e: e.dma_start(out=sk[:], in_=skT_d.rearrange("(g p) n -> p g n", p=128)),
             w=["sk"], dma="b0w")
        for mc in range(8):
            for q in range(4):
                bank, bk = G["psf"].get()
                for j in range(4):
                    hp = q * 4 + j
                    P.op("pe", lambda e, bank=bank, hp=hp, mc=mc, j=j: e.matmul(
                        bank[:, j * 128:(j + 1) * 128], lhsT=wq[:, hp, mc * 128:(mc + 1) * 128],
                        rhs=sk[:, hp, :], start=True, stop=True), r=["wq", "sk"], w=[bk])
                eng = "act" if (mc * 4 + q) % 2 == 0 else "dve"
                if eng == "act":
                    P.op("act", lambda e, bank=bank, mc=mc, q=q: e.copy(
                        out=weff[:, mc, q * 512:(q + 1) * 512], in_=bank[:]), r=[bk], w=["weff"])
                else:
                    P.op("dve", lambda e, bank=bank, mc=mc, q=q: e.tensor_copy(
                        out=weff[:, mc, q * 512:(q + 1) * 512], in_=bank[:]), r=[bk], w=["weff"])
        P.emit()


def phase_bt(nc, sync, G, peer_u, peer_v, uv_scr):
    with ExitStack() as es:
        uv = [es.enter_context(nc.sbuf_tensor("bt_uv%d" % i, [128, 8, 2 * D], BF16)) for i in range(2)]
        P = Prog(nc, sync)
        for ch in range(16):
            bt_chunk(P, ch, uv, peer_u, peer_v, uv_scr)
        P.emit()


def bt_chunk(P, ch, uv, peer_u, peer_v, uv_scr):
    s = ch % 2
    k = "uv%d" % s
    rows = slice(ch * 1024, (ch + 1) * 1024)
    P.op("pool", lambda e: e.dma_start(
        out=uv[s][:, :, 0:D], in_=peer_u[rows, :].rearrange("(p r) d -> p r d", r=8)), w=[k], dma="btl%d" % s)
    P.op("pool", lambda e: e.dma_start(
        out=uv[s][:, :, D:2 * D], in_=peer_v[rows, :].rearrange("(p r) d -> p r d", r=8)), w=[k], dma="btl%d" % s)
    P.op("sp", lambda e: e.dma_start(
        out=uv_scr[rows, :].rearrange("(p r) d -> p r d", r=8), in_=uv[s][:]), r=[k], dma="bts%d" % s)


def phase_b(nc, sync, G, n_tiles, weff, h_scr, uv_scr, ln2g_d, ln2b_d, y_d, NS=16, ND=4, GS=4, NP=2,
            DSPLIT=0):
    with ExitStack() as es:
        def sb(name, shape, dt):
            return es.enter_context(nc.sbuf_tensor("b_" + name, shape, dt))
        ident, identf, iota16 = G["ident"], G["identf"], G["iota16"]
        g2 = sb("g2", [128, D], F32)
        b2 = sb("b2", [128, D], F32)
        h_t = [sb("h%d" % i, [128, D], F32) for i in range(2)]
        hb = [sb("hb%d" % i, [128, D], BF16) for i in range(2)]
        prod = [sb("prod%d" % i, [128, D], BF16) for i in range(NP)]
        junk2 = sb("junk2", [128, D], BF16)
        hT = sb("hT", [128, 8, 128], BF16)
        sc = sb("sc", [128, 16, 128], F32)
        sv = sb("sv", [128, 16, 16], F32)
        si = sb("si", [128, 16, 16], U32)
        sif = sb("sif", [128, 16, 16], F32)
        cand = sb("cand", [128, 8, 256], F32)
        ts = sb("ts", [128, 8, 16], F32)
        pos = sb("pos", [128, 8, 16], U32)
        pij = sb("pij", [128, 2, 128], U32)
        pijf = sb("pijf", [128, 2, 8, 16], F32)
        oh = [sb("oh%d" % i, [128, 8, 16, 16], BF16) for i in range(2)]
        ee = sb("ee", [128, 2, 128], F32)
        ef = sb("ef", [128, 128], F32)
        eidx = [sb("eidx%d" % i, [128, 128], I32) for i in range(2)]
        dsm = sb("dsm", [128, 8, 16], F32)
        ex = sb("ex", [128, 8, 16], F32)
        ssum = sb("ssum", [128, 8], F32)
        gate = [sb("gate%d" % i, [128, 128], F32) for i in range(2)]
        dots = sb("dots", [128, 128], F32)
        actt = sb("actt", [128, 128], F32)
        wt = sb("wt", [128, 128], F32)
        junk = sb("junk", [128, D], BF16)
        uvb = [sb("uvb%d" % i, [128, 2 * D], BF16) for i in range(NS)]
        dg = [sb("dg%d" % i, [128, 128], BF16) for i in range(ND)]
        y_t = [sb("y%d" % i, [128, D], F32) for i in range(2)]
        st6 = sb("st6", [128, 12], F32)
        mv2 = sb("mv2", [128, 2], F32)
        rstd = sb("rstd", [128, 1], F32)
        nmr = sb("nmr", [128, 1], F32)
        pv = G["pv"]
        NG = 128 // GS

        P = Prog(nc, sync)
        P.op("sp", lambda e: e.dma_start(out=g2[:], in_=ln2g_d[:, :]), w=["g2"], dma="bw")
        P.op("sp", lambda e: e.dma_start(out=b2[:], in_=ln2b_d[:, :]), w=["b2"], dma="bw")
        cnt = dict(u=0, d=0, p=0, q=0)
        scpool = PsumPool(G["psf"].t[0:2], "psf")
        dq = [(G["psf"].t[2 + i // 4], (i % 4) * 128) for i in range(8)]

        def load_h(t):
            s = t % 2
            P.op("sp", lambda e: e.dma_start(out=h_t[s][:], in_=h_scr[t * 128:(t + 1) * 128, :]),
                 w=["h%d" % s], dma="hld%d" % s)

        def front(t, OP=None):
            OP = OP or P.op
            s = t % 2
            hk_ = "h%d" % s
            OP("act", lambda e: e.copy(out=hb[s][:], in_=h_t[s][:]), r=[hk_], w=["hb%d" % s])
            pb, pbk = G["psb"].get()
            for c in range(8):
                OP("pe", lambda e, c=c: e.transpose(out=pb[:, c * 128:(c + 1) * 128],
                                                       in_=hb[s][:, c * 128:(c + 1) * 128], identity=ident[:]),
                     r=["hb%d" % s, "ident"], w=[pbk])
            OP("act", lambda e: e.copy(out=hT[:].rearrange("p c t -> p (c t)"), in_=pb[:]), r=[pbk], w=["hT"])
            for nb in range(4):
                bank, bk = scpool.get()
                for c in range(8):
                    OP("pe", lambda e, c=c, nb=nb, bank=bank: e.matmul(
                        bank[:], lhsT=hT[:, c, :], rhs=weff[:, c, nb * 512:(nb + 1) * 512],
                        start=(c == 0), stop=(c == 7)), r=["hT", "weff"], w=[bk])
                for q in range(4):
                    g = nb * 4 + q
                    OP("act", lambda e, q=q, g=g, bank=bank: e.copy(out=sc[:, g, :], in_=bank[:, q * 128:(q + 1) * 128]),
                         r=[bk], w=["sc%d" % g])
            for g in range(16):
                OP("dve", lambda e, g=g: e.max(out=sv[:, g, 0:8], in_=sc[:, g, :]), r=["sc%d" % g], w=["sva%d" % g])
            for g in range(16):
                OP("dve", lambda e, g=g: e.max_index(out=si[:, g, 0:8], in_max=sv[:, g, 0:8], in_values=sc[:, g, :]),
                     r=["sc%d" % g, "sva%d" % g], w=["sia%d" % g])
            for g in range(16):
                OP("dve", lambda e, g=g: e.match_replace(out=sc[:, g, :], in_to_replace=sv[:, g, 0:8],
                                                           in_values=sc[:, g, :], imm_value=-1e30),
                     r=["sva%d" % g], w=["sc%d" % g])
            for g in range(16):
                OP("dve", lambda e, g=g: e.max(out=sv[:, g, 8:16], in_=sc[:, g, :]), r=["sc%d" % g], w=["svb%d" % g])
            for g in range(16):
                OP("dve", lambda e, g=g: e.max_index(out=si[:, g, 8:16], in_max=sv[:, g, 8:16], in_values=sc[:, g, :]),
                     r=["sc%d" % g, "svb%d" % g], w=["sib%d" % g])
            allsv = ["sva%d" % g for g in range(16)] + ["svb%d" % g for g in range(16)]
            allsi = ["sia%d" % g for g in range(16)] + ["sib%d" % g for g in range(16)]
            OP("dve", lambda e: e.tensor_copy(out=sif[:], in_=si[:]), r=allsi, w=["sif"])
            sv4 = sv[:].rearrange("p (h two) k -> p h two k", two=2)
            OP("dve", lambda e: e.tensor_tensor(
                out=cand[:].rearrange("p h (i j) -> p h i j", j=16),
                in0=bcast(sv4[:, :, 0, :], 3, [128, 8, 16, 16]),
                in1=bcast(sv4[:, :, 1, :], 2, [128, 8, 16, 16]), op=ALU.add), r=allsv,
                w=["cand%d" % h for h in range(8)])
            for h in range(8):
                OP("dve", lambda e, h=h: e.max(out=ts[:, h, 0:8], in_=cand[:, h, :]), r=["cand%d" % h], w=["tsa%d" % h])
            for h in range(8):
                OP("dve", lambda e, h=h: e.max_index(out=pos[:, h, 0:8], in_max=ts[:, h, 0:8], in_values=cand[:, h, :]),
                     r=["cand%d" % h, "tsa%d" % h], w=["posa%d" % h])
            for h in range(8):
                OP("dve", lambda e, h=h: e.match_replace(out=cand[:, h, :], in_to_replace=ts[:, h, 0:8],
                                                           in_values=cand[:, h, :], imm_value=-1e30),
                     r=["tsa%d" % h], w=["cand%d" % h])
            for h in range(8):
                OP("dve", lambda e, h=h: e.max(out=ts[:, h, 8:16], in_=cand[:, h, :]), r=["cand%d" % h], w=["tsb%d" % h])
            for h in range(8):
                OP("dve", lambda e, h=h: e.max_index(out=pos[:, h, 8:16], in_max=ts[:, h, 8:16], in_values=cand[:, h, :]),
                     r=["cand%d" % h, "tsb%d" % h], w=["posb%d" % h])
            allts = ["tsa%d" % h for h in range(8)] + ["tsb%d" % h for h in range(8)]
            allpos = ["posa%d" % h for h in range(8)] + ["posb%d" % h for h in range(8)]
            posf = pos[:].rearrange("p h k -> p (h k)")
            OP("dve", lambda e: e.tensor_single_scalar(out=pij[:, 0, :], in_=posf, scalar=4,
                                                         op=ALU.logical_shift_right), r=allpos, w=["pij0"])
            OP("dve", lambda e: e.tensor_single_scalar(out=pij[:, 1, :], in_=posf, scalar=15,
                                                         op=ALU.bitwise_and), r=allpos, w=["pij1"])
            OP("dve", lambda e: e.tensor_copy(out=pijf[:].rearrange("p a h k -> p a (h k)"), in_=pij[:]),
                 r=["pij0", "pij1"], w=["pijf"])
            sif4 = sif[:].rearrange("p (h two) k -> p h two k", two=2)
            for a in range(2):
                OP("dve", lambda e, a=a: e.tensor_tensor(
                    out=oh[a][:], in0=bcast(pijf[:, a, :, :], 3, [128, 8, 16, 16]),
                    in1=iota16[:].unsqueeze(1).unsqueeze(1).to_broadcast([128, 8, 16, 16]),
                    op=ALU.is_equal), r=["pijf"], w=["oh%d" % a])
            for a in range(2):
                OP("dve", lambda e, a=a: e.tensor_tensor(
                    out=oh[a][:], in0=oh[a][:], in1=bcast(sif4[:, :, a, :], 2, [128, 8, 16, 16]),
                    op=ALU.mult), r=["sif"], w=["oh%d" % a])
            for a in range(2):
                OP("dve", lambda e, a=a: e.tensor_reduce(
                    out=ee[:, a, :], in_=oh[a][:].rearrange("p h k i -> p (h k) i"), axis=AX.X, op=ALU.add),
                    r=["oh%d" % a], w=["ee%d" % a])
            OP("dve", lambda e: e.scalar_tensor_tensor(out=ef[:], in0=ee[:, 0, :], scalar=128.0, in1=ee[:, 1, :],
                                                         op0=ALU.mult, op1=ALU.add), r=["ee0", "ee1"], w=["ef"])
            ek = "eidx%d" % s
            OP("dve", lambda e: e.tensor_copy(out=eidx[s][:], in_=ef[:]), r=["ef"], w=[ek])
            OP("dve", lambda e: e.tensor_tensor(out=dsm[:], in0=ts[:], in1=ts[:, :, 0:1].to_broadcast([128, 8, 16]),
                                                  op=ALU.subtract), r=allts, w=["dsm"])
            OP("act", lambda e: e.activation(out=ex[:], in_=dsm[:], func=AF.Exp), r=["dsm"], w=["ex"])
            OP("dve", lambda e: e.tensor_reduce(out=ssum[:], in_=ex[:], axis=AX.X, op=ALU.add), r=["ex"], w=["ssum"])
            OP("dve", lambda e: e.reciprocal(out=ssum[:], in_=ssum[:]), r=["ssum"], w=["ssum"])
            OP("dve", lambda e: e.tensor_tensor(out=gate[s][:].rearrange("p (h k) -> p h k", k=16), in0=ex[:],
                                                  in1=ssum[:].unsqueeze(2).to_broadcast([128, 8, 16]), op=ALU.mult),
                 r=["ex", "ssum"], w=["gate%d" % s])

        def s1(t, gq):
            s = t % 2
            ek = "eidx%d" % s
            cols = slice(gq * GS, (gq + 1) * GS)
            slots = []
            accs = []
            for hk in range(gq * GS, (gq + 1) * GS):
                u = cnt["u"] % NS
                cnt["u"] += 1
                slots.append(u)
                pi = cnt["p"] % NP
                cnt["p"] += 1
                P.op("pool", lambda e, u=u, hk=hk: e.indirect_dma_start(
                    out=uvb[u][:], out_offset=None, in_=uv_scr[:, :],
                    in_offset=bass.IndirectOffsetOnAxis(ap=eidx[s][:, hk:hk + 1], axis=0)),
                    r=[ek], w=["uvb%d" % u], dma="gu%d" % u)
                hk_ = "h%d" % s
                if (hk % GS) < DSPLIT:
                    P.op("dve", lambda e, u=u, pi=pi: e.tensor_tensor(
                        out=prod[pi][:], in0=uvb[u][:, 0:D], in1=hb[s][:], op=ALU.mult),
                        r=["uvb%d" % u, "hb%d" % s], w=["prod%d" % pi])
                    accs.append((pi, hk))
                else:
                    P.op("dve", lambda e, u=u, hk=hk: e.scalar_tensor_tensor(
                        out=junk[:], in0=uvb[u][:, 0:D], scalar=1.0, in1=h_t[s][:],
                        op0=ALU.mult, op1=ALU.mult, accum_out=dots[:, hk:hk + 1]),
                        r=["uvb%d" % u, hk_], w=["dots%d" % gq])
            for (pi, hk) in accs:
                P.op("act", lambda e, pi=pi, hk=hk: e.activation(
                    out=junk2[:], in_=prod[pi][:], func=AF.Copy, accum_out=dots[:, hk:hk + 1]),
                    r=["prod%d" % pi], w=["dots%d" % gq])
            P.op("act", lambda e: e.activation(out=actt[:, cols], in_=dots[:, cols], func=AF.Gelu),
                 r=["dots%d" % gq], w=["actt%d" % gq])
            return slots

        def s2(t, gq, slots):
            s = t % 2
            cols = slice(gq * GS, (gq + 1) * GS)
            P.op("dve", lambda e: e.tensor_tensor(out=wt[:, cols], in0=gate[s][:, cols], in1=actt[:, cols], op=ALU.mult),
                 r=["gate%d" % s, "actt%d" % gq], w=["wt%d" % gq])
            for j, hk in enumerate(range(gq * GS, (gq + 1) * GS)):
                u = slots[j]
                d = cnt["d"] % ND
                cnt["d"] += 1
                P.op("act", lambda e, d=d, hk=hk: e.activation(out=dg[d][:], in_=identf[:], func=AF.Copy,
                                                               scale=wt[:, hk:hk + 1]),
                     r=["wt%d" % gq, "identf"], w=["dg%d" % d])
                for half in range(2):
                    P.op("pe", lambda e, d=d, u=u, half=half, hk=hk: e.matmul(
                        pv[half][:], lhsT=dg[d][:], rhs=uvb[u][:, D + half * 512:D + (half + 1) * 512],
                        start=(hk == 0), stop=(hk == 127)), r=["dg%d" % d, "uvb%d" % u], w=["pv%d" % half])

        def tail(t):
            s = t % 2
            hk_, yk = "h%d" % s, "y%d" % s
            for half in range(2):
                P.op("dve", lambda e, half=half: e.scalar_tensor_tensor(
                    out=y_t[s][:, half * 512:(half + 1) * 512], in0=h_t[s][:, half * 512:(half + 1) * 512],
                    scalar=ALPHA, in1=pv[half][:], op0=ALU.mult, op1=ALU.add),
                    r=[hk_, "pv%d" % half], w=[yk])
            layer_norm_tail(P, y_t[s], yk, g2, "g2", b2, "b2", st6, mv2, rstd, nmr, "b", gb_eng="dve")
            P.op("sp", lambda e: e.dma_start(out=y_d[t * 128:(t + 1) * 128, :], in_=y_t[s][:]),
                 r=[yk], dma="yst%d" % s)

        load_h(0)
        front(0)
        for t in range(n_tiles):
            if t + 1 < n_tiles:
                load_h(t + 1)
            pend = None
            todo = []
            if t + 1 < n_tiles:
                front(t + 1, OP=lambda *a, **k: todo.append((a, k)))
            per = -(-len(todo) // max(1, NG - 4))
            for gq in range(NG):
                slots = s1(t, gq)
                if pend is not None:
                    s2(t, *pend)
                pend = (gq, slots)
                for (a, k) in todo[:per]:
                    P.op(*a, **k)
                del todo[:per]
            s2(t, *pend)
            for (a, k) in todo:
                P.op(*a, **k)
            tail(t)
        P.emit()


def layer_norm_tail(P, z, zk, g, gk, b, bk, st6, mv2, rstd, nmr, pfx, gb_eng="pool"):
    k6, k2, kr, kn = pfx + "st6", pfx + "mv2", pfx + "rstd", pfx + "nmr"
    for half in range(2):
        P.op("dve", lambda e, half=half: e.bn_stats(out=st6[:, half * 6:(half + 1) * 6],
                                                     in_=z[:, half * 512:(half + 1) * 512]), r=[zk], w=[k6])
    P.op("dve", lambda e: e.bn_aggr(out=mv2[:], in_=st6[:]), r=[k6], w=[k2])
    P.op("act", lambda e: e.activation(out=rstd[:], in_=mv2[:, 1:2], func=AF.Sqrt, bias=EPS_T[0][:], scale=1.0),
         r=[k2, "eps"], w=[kr])
    P.op("dve", lambda e: e.reciprocal(out=rstd[:], in_=rstd[:]), r=[kr], w=[kr])
    P.op("dve", lambda e: e.scalar_tensor_tensor(out=nmr[:], in0=mv2[:, 0:1], scalar=-1.0, in1=rstd[:],
                                                 op0=ALU.mult, op1=ALU.mult), r=[k2, kr], w=[kn])
    P.op("act", lambda e: e.activation(out=z[:], in_=z[:], func=AF.Identity, bias=nmr[:], scale=rstd[:]),
         r=[zk, kr, kn], w=[zk])
    P.op(gb_eng, lambda e: e.tensor_tensor(out=z[:], in0=z[:], in1=g[:], op=ALU.mult), r=[zk, gk], w=[zk])
    P.op(gb_eng, lambda e: e.tensor_tensor(out=z[:], in0=z[:], in1=b[:], op=ALU.add), r=[zk, bk], w=[zk])


def alloc_globals(nc, es, sync):
    G = {}
    psf = [es.enter_context(nc.psum_tensor("psf%d" % i, [128, 512], F32)) for i in range(4)]
    pv = [es.enter_context(nc.psum_tensor("pv%d" % i, [128, 512], F32)) for i in range(2)]
    psb = [es.enter_context(nc.psum_tensor("psb%d" % i, [128, 1024], BF16)) for i in range(2)]
    G["psf"] = PsumPool(psf, "psf")
    G["psb"] = PsumPool(psb, "psb")
    G["pv"] = pv
    G["ident"] = es.enter_context(nc.sbuf_tensor("sb_ident", [128, 128], BF16))
    G["identf"] = es.enter_context(nc.sbuf_tensor("sb_identf", [128, 128], F32))
    G["iota16"] = es.enter_context(nc.sbuf_tensor("sb_iota16", [128, 16], F32))
    G["eps"] = es.enter_context(nc.sbuf_tensor("sb_eps", [128, 1], F32))
    EPS_T[0] = G["eps"]
    return G


def phase_const(nc, sync, G, identf_d, iota16_d):
    P = Prog(nc, sync)
    P.op("sp", lambda e: e.dma_start(out=G["identf"][:], in_=identf_d[:, :]), w=["identf"], dma="c0")
    P.op("sp", lambda e: e.dma_start(out=G["iota16"][:], in_=iota16_d[:, :]), w=["iota16"], dma="c0")
    P.op("dve", lambda e: e.tensor_copy(out=G["ident"][:], in_=G["identf"][:]), r=["identf"], w=["ident"])
    P.op("dve", lambda e: e.memset(G["eps"][:], LN_EPS), w=["eps"])
    P.emit()


def build_b_only(n_tiles, tab_dt=F32):
    nc = bass.Bass("TRN2", target_bir_lowering=False)
    T = n_tiles * 128
    dt = lambda name, shape, dtype, kind="ExternalInput": nc.dram_tensor(name, shape, dtype, kind=kind).ap()
    h_d = dt("h_in", [T, D], F32)
    wpqT_d = dt("wpqT", [2048, D], F32)
    skT_d = dt("skT", [2048, 128], F32)
    pu = dt("peer_u", [N_EXP, D], tab_dt)
    pvv = dt("peer_v", [N_EXP, D], tab_dt)
    g2 = dt("ln2g", [128, D], F32)
    b2 = dt("ln2b", [128, D], F32)
    identf_d = dt("identf", [128, 128], F32)
    iota_d = dt("iota16", [128, 16], F32)
    y_d = dt("y", [T, D], F32, kind="ExternalOutput")
    with ExitStack() as es:
        sync = Sync(nc, es)
        G = alloc_globals(nc, es, sync)
        phase_const(nc, sync, G, identf_d, iota_d)
        weff = es.enter_context(nc.sbuf_tensor("sb_weff", [128, 8, 2048], BF16))
        phase_b0(nc, sync, G, weff, wpqT_d, skT_d)
        uv_scr = nc.dram_tensor("uv_scr", [N_EXP, 2 * D], BF16, kind="Internal").ap()
        phase_bt(nc, sync, G, pu, pvv, uv_scr)
        phase_b(nc, sync, G, n_tiles, weff, h_d, uv_scr, g2, b2, y_d)
    return nc


def phase_a0(nc, sync, G, mkT, mv_aug, mem_d, memg_d, memb_d, wkv_d):
    with ExitStack() as es:
        def sb(name, shape, dt):
            return es.enter_context(nc.sbuf_tensor("a0_" + name, shape, dt))
        ident = G["ident"]
        wkv = sb("wkv", [128, 8, 1024], BF16)
        mg = sb("mg", [128, D], F32)
        mb = sb("mb", [128, D], F32)
        mt = [sb("mt%d" % i, [128, D], F32) for i in range(2)]
        mn = sb("mn", [128, D], BF16)
        mnT = sb("mnT", [128, 8, 256], BF16)
        st6 = sb("st6", [128, 12], F32)
        mv2 = sb("mv2", [128, 2], F32)
        rstd = sb("rstd", [128, 1], F32)
        nmr = sb("nmr", [128, 1], F32)
        P = Prog(nc, sync)
        for c in range(8):
            P.op("pool", lambda e, c=c: e.dma_start(out=wkv[:, c, :], in_=wkv_d[c * 128:(c + 1) * 128, :]),
                 w=["wkv"], dma="a0w")
        P.op("sp", lambda e: e.dma_start(out=mg[:], in_=memg_d[:, :]), w=["mg"], dma="a0p")
        P.op("sp", lambda e: e.dma_start(out=mb[:], in_=memb_d[:, :]), w=["mb"], dma="a0p")
        P.op("dve", lambda e: e.memset(mv_aug[:], 1.0), w=["mv_aug"])
        for mc in range(2):
            mk_ = "mt%d" % mc
            P.op("sp", lambda e, mc=mc: e.dma_start(out=mt[mc][:], in_=mem_d[mc * 128:(mc + 1) * 128, :]),
                 w=[mk_], dma="a0m%d" % mc)
            layer_norm_tail(P, mt[mc], mk_, mg, "mg", mb, "mb", st6, mv2, rstd, nmr, "a0", gb_eng="dve")
            P.op("act", lambda e, mc=mc: e.copy(out=mn[:], in_=mt[mc][:]), r=[mk_], w=["mn"])
            pb, pbk = G["psb"].get()
            for c in range(8):
                P.op("pe", lambda e, c=c, pb=pb: e.transpose(out=pb[:, c * 128:(c + 1) * 128],
                                                             in_=mn[:, c * 128:(c + 1) * 128], identity=ident[:]),
                     r=["mn", "ident"], w=[pbk])
            P.op("act", lambda e, mc=mc, pb=pb: e.copy(out=mnT[:, :, mc * 128:(mc + 1) * 128],
                                                       in_=pb[:].rearrange("p (c t) -> p c t", t=128)),
                 r=[pbk], w=["mnT"])
        for h in range(4):
            bank, bk = G["psA"].get()
            for c in range(8):
                P.op("pe", lambda e, c=c, h=h, bank=bank: e.matmul(
                    bank[:, 0:256], lhsT=wkv[:, c, h * 128:(h + 1) * 128], rhs=mnT[:, c, :],
                    start=(c == 0), stop=(c == 7)), r=["wkv", "mnT"], w=[bk])
            P.op("act", lambda e, h=h, bank=bank: e.copy(out=mkT[:, h, :], in_=bank[:, 0:256]), r=[bk], w=["mkT"])
        for mc in range(2):
            bank, bk = G["psA"].get()
            for c in range(8):
                P.op("pe", lambda e, c=c, mc=mc, bank=bank: e.matmul(
                    bank[:], lhsT=mnT[:, c, mc * 128:(mc + 1) * 128], rhs=wkv[:, c, 512:1024],
                    start=(c == 0), stop=(c == 7)), r=["wkv", "mnT"], w=[bk])
            P.op("dve", lambda e, mc=mc, bank=bank: e.tensor_copy(
                out=mv_aug[:, mc, :, 0:128], in_=bank[:].rearrange("p (h d) -> p h d", d=128)),
                r=[bk], w=["mv_aug"])
        P.emit()


def phase_a1(nc, sync, G, n_tiles, mkT, mv_aug, xT_d, w1_d, b1_d, cc_d, ss_d, dt_d, wq_d, wk_d, cd_d,
             mask_d, gng_d, sink_d, brT_scr, bt=None):
    with ExitStack() as es:
        def sb(name, shape, dt):
            return es.enter_context(nc.sbuf_tensor("a1_" + name, shape, dt))
        ident = G["ident"]
        FM = FM_COLS
        w1 = sb("w1", [128, 8, NCOL_A1], BF16)
        b1 = sb("b1", [1, NCOL_A1], BF16)
        ones = sb("ones", [1, 128], BF16)
        dtab = sb("dtab", [128, 4, 128], F32)
        wqt = sb("wqt", [128, 2, 128], F32)
        wkt = sb("wkt", [128, 4], F32)
        cdt = sb("cdt", [128, 2], F32)
        msk = sb("msk", [128, 2, 128], F32)
        gng = sb("gng", [128, 512], F32)
        esink = sb("esink", [128, 8], F32)
        state = sb("state", [128, 2, 128], F32)
        state_bf = sb("state_bf", [128, 2, 128], BF16)
        xT = [sb("xT%d" % i, [128, 8, 128], BF16) for i in range(2)]
        cct = [sb("cc%d" % i, [128, 128], F32) for i in range(2)]
        sst = [sb("ss%d" % i, [128, 128], F32) for i in range(2)]
        qT_l = [sb("qT%d" % i, [128, 4, 128], BF16) for i in range(2)]
        pj = sb("pj", [128, FM_COLS], BF16)
        kT = [sb("kT%d" % i, [128, 2, 2, 128], BF16) for i in range(3)]
        kpad_l = [sb("kpad%d" % i, [128, 2, 2, 128], BF16) for i in range(2)]
        qspad_l = [sb("qspad%d" % i, [128, 2, 2, 128], BF16) for i in range(2)]
        vaug = [sb("vaug%d" % i, [128, 2, 65], BF16) for i in range(3)]
        mqT_l = [sb("mqT%d" % i, [128, 4, 128], BF16) for i in range(2)]
        tmp1 = sb("tmp1", [128, 4, 128], F32)
        tmp2 = sb("tmp2", [128, 4, 128], F32)
        rot_l = [sb("rot%d" % i, [128, 4, 128], BF16) for i in range(2)]
        ktok = sb("ktok", [128, 256], BF16)
        vret_l = [sb("vret%d" % i, [128, 4, 128], BF16) for i in range(2)]
        vw_l = [sb("vw%d" % i, [128, 4, 128], BF16) for i in range(2)]
        sg_l = [sb("sg%d" % i, [128, 512], F32) for i in range(2)]
        pT = sb("pT", [128, 2, 8, 128], BF16)
        pTc = sb("pTc", [128, 2, 4, 128], BF16)
        innerTm = sb("innerTm", [128, 4, 128], BF16)
        den = sb("den", [128, 8], F32)
        denc = sb("denc", [128, 4], F32)
        br = sb("br", [128, 3, 512], BF16)
        xn = sb("xn", [128, 512], F32)
        gst = sb("gst", [128, 24], F32)
        gmv = sb("gmv", [128, 4, 2], F32)
        grs = sb("grs", [128, 4], F32)
        brT = [sb("brT%d" % i, [128, 12, 128], BF16) for i in range(2)]
        uvt = [sb("uvt%d" % i, [128, 8, 2 * D], BF16) for i in range(2)] if bt is not None else None
        btc = dict(ch=0)
        poolP = PsumPool(G["psA"].t[0:3], "psAp")
        poolQ = PsumPool(G["psA"].t[3:6], "psAq")
        psbP = PsumPool(G["psb"].t[0:1], "psbp")
        psbQ = PsumPool(G["psb"].t[1:2], "psbq")

        P = Prog(nc, sync)
        for c in range(8):
            P.op("pool", lambda e, c=c: e.dma_start(out=w1[:, c, :], in_=w1_d[c * 128:(c + 1) * 128, :],
                                                    max_dma_last_dim=4096), w=["w1"], dma="a1w")
        P.op("pool", lambda e: e.dma_start(out=b1[:], in_=b1_d[:, :], max_dma_last_dim=4096), w=["b1"], dma="a1w")
        for (tt, dd, kk) in ((dtab, dt_d, "dtab"), (wqt, wq_d, "wqt")):
            P.op("sp", lambda e, tt=tt, dd=dd: e.dma_start(out=tt[:].rearrange("p a b -> p (a b)"), in_=dd[:, :]),
                 w=[kk], dma="a1p")
        P.op("sp", lambda e: e.dma_start(out=msk[:].rearrange("p a b -> p (a b)"), in_=mask_d[:, :]),
             w=["msk"], dma="a1p")
        for (tt, dd, kk) in ((wkt, wk_d, "wkt"), (cdt, cd_d, "cdt"), (gng, gng_d, "gng"), (esink, sink_d, "esink")):
            P.op("sp", lambda e, tt=tt, dd=dd: e.dma_start(out=tt[:], in_=dd[:, :]), w=[kk], dma="a1p")
        P.op("act", lambda e: e.activation(out=esink[:], in_=esink[:], func=AF.Exp), r=["esink"], w=["esink"])
        P.op("dve", lambda e: e.memset(ones[:], 1.0), w=["ones"])
        P.op("dve", lambda e: e.memset(state[:], 0.0), w=["state"])
        P.op("dve", lambda e: e.memset(state_bf[:], 0.0), w=["state_bf"])
        for i in range(3):
            P.op("dve", lambda e, i=i: e.memset(vaug[i][:], 1.0), w=["vaug%d" % i])
            P.op("dve", lambda e, i=i: e.memset(kT[i][:], 0.0), w=["kT%d" % i])
        for i in range(2):
            P.op("dve", lambda e, i=i: e.memset(kpad_l[i][:], 0.0), w=["kpad%d" % i])
            P.op("dve", lambda e, i=i: e.memset(qspad_l[i][:], 0.0), w=["qspad%d" % i])

        def loads(t):
            s = t % 2
            P.op("pool", lambda e: e.dma_start(
                out=xT[s][:], in_=xT_d.rearrange("(c p) t -> p c t", p=128)[:, :, t * 128:(t + 1) * 128]),
                w=["xT%d" % s], dma="a1x%d" % s)
            P.op("sp", lambda e: e.dma_start(out=cct[s][:], in_=cc_d[:, t * 128:(t + 1) * 128]),
                 w=["cc%d" % s], dma="a1c%d" % s)
            P.op("sp", lambda e: e.dma_start(out=sst[s][:], in_=ss_d[:, t * 128:(t + 1) * 128]),
                 w=["ss%d" % s], dma="a1c%d" % s)

        def fm_group(s, fbs, bank, bk):
            xk = "xT%d" % s
            for j, fb in enumerate(fbs):
                for c in range(8):
                    P.op("pe", lambda e, c=c, fb=fb, j=j: e.matmul(
                        bank[:, j * 128:(j + 1) * 128], lhsT=w1[:, c, fb * 128:(fb + 1) * 128], rhs=xT[s][:, c, :],
                        start=(c == 0), stop=False), r=["w1", xk], w=[bk])
                P.op("pe", lambda e, fb=fb, j=j: e.matmul(
                    bank[:, j * 128:(j + 1) * 128], lhsT=b1[0:1, fb * 128:(fb + 1) * 128], rhs=ones[0:1, :],
                    start=False, stop=True), r=["b1", "ones"], w=[bk])

        def tm_group(s, col0, ncols, bank, bk):
            xk = "xT%d" % s
            for c in range(8):
                P.op("pe", lambda e, c=c: e.matmul(
                    bank[:, 0:ncols], lhsT=xT[s][:, c, :], rhs=w1[:, c, col0:col0 + ncols],
                    start=(c == 0), stop=False), r=["w1", xk], w=[bk])
            P.op("pe", lambda e: e.matmul(bank[:, 0:ncols], lhsT=ones[0:1, :], rhs=b1[0:1, col0:col0 + ncols],
                                          start=False, stop=True), r=["b1", "ones"], w=[bk])

        def stage_p(t):
            s = t % 2
            s3, sp3 = t % 3, (t - 1) % 3
            qT, mqT, rot, kpad, qspad = qT_l[s], mqT_l[s], rot_l[s], kpad_l[s], qspad_l[s]
            vret, vw, sg = vret_l[s], vw_l[s], sg_l[s]
            K = lambda nm: nm + str(s)
            vk = "vaug%d" % s3
            if t + 1 < n_tiles:
                loads(t + 1)
            for nb in range(5):
                c0 = nb * 512
                ncol = min(512, FM - c0)
                bpj, bkpj = poolP.get()
                tm_group(s, c0, ncol, bpj, bkpj)
                if nb % 2 == 0:
                    P.op("act", lambda e, c0=c0, ncol=ncol, bpj=bpj: e.copy(out=pj[:, c0:c0 + ncol], in_=bpj[:, 0:ncol]),
                         r=[bkpj], w=["pj%d" % nb])
                else:
                    P.op("dve", lambda e, c0=c0, ncol=ncol, bpj=bpj: e.tensor_copy(out=pj[:, c0:c0 + ncol],
                                                                                   in_=bpj[:, 0:ncol]),
                         r=[bkpj], w=["pj%d" % nb])

            def fm_T(fbs):
                pbx, pbxk = psbP.get()
                for j, fb in enumerate(fbs):
                    P.op("pe", lambda e, j=j, fb=fb, pbx=pbx: e.transpose(
                        out=pbx[:, j * 128:(j + 1) * 128], in_=pj[:, fb * 128:(fb + 1) * 128], identity=ident[:]),
                        r=["pj%d" % (fb // 4), "ident"], w=[pbxk])
                return pbx, pbxk

            bank, bk = fm_T([0, 1, 2, 3])
            P.op("act", lambda e: e.copy(out=qT[:].rearrange("p a b -> p (a b)"), in_=bank[:, 0:512]), r=[bk], w=[K("qT")])
            bankk, bkk = fm_T([4, 5])
            kTk = "kT%d" % s3
            for hf in range(2):
                P.op("act", lambda e, hf=hf: e.copy(
                    out=kT[s3][hf * 64:(hf + 1) * 64, :, hf, :],
                    in_=bankk[hf * 64:(hf + 1) * 64, 0:256].rearrange("p (g t) -> p g t", t=128)), r=[bkk], w=[kTk])
            bra, bkra = fm_T([6, 7, 8, 9])
            P.op("dve", lambda e: e.tensor_tensor(out=tmp1[:], in0=bra[:, 0:512].rearrange("p (a b) -> p a b", b=128),
                                                  in1=bcast(cct[s][:], 1, [128, 4, 128]), op=ALU.mult),
                 r=[bkra, "cc%d" % s], w=["tmp1"])
            brb, bkrb = fm_T([10, 11, 12, 13])
            P.op("dve", lambda e: e.tensor_tensor(out=tmp2[:], in0=brb[:, 0:512].rearrange("p (a b) -> p a b", b=128),
                                                  in1=bcast(sst[s][:], 1, [128, 4, 128]), op=ALU.mult),
                 r=[bkrb, "ss%d" % s], w=["tmp2"])
            P.op("pool", lambda e: e.tensor_tensor(out=rot[:], in0=tmp1[:], in1=tmp2[:], op=ALU.add),
                 r=["tmp1", "tmp2"], w=[K("rot")])
            for hf in range(2):
                rows = slice(hf * 64, (hf + 1) * 64)
                P.op("pool", lambda e, hf=hf, rows=rows: e.tensor_tensor(
                    out=qspad[rows, :, hf, :], in0=rot[rows, 0:2, :], in1=wqt[rows, :, :], op=ALU.mult),
                    r=[K("rot"), "wqt"], w=[K("qspad")])
                P.op("pool", lambda e, hf=hf, rows=rows: e.tensor_copy(out=kpad[rows, :, hf, :], in_=rot[rows, 2:4, :]),
                     r=[K("rot")], w=[K("kpad")])
            bankm, bkm = fm_T([14, 15, 16, 17])
            P.op("act", lambda e: e.copy(out=mqT[:].rearrange("p a b -> p (a b)"), in_=bankm[:, 0:512]), r=[bkm], w=[K("mqT")])
            bav, bkav = poolP.get()
            tm_group(s, FM, 128, bav, bkav)
            P.op("act", lambda e: e.copy(out=vaug[s3][:, :, 0:64], in_=bav[:, 0:128].rearrange("p (g d) -> p g d", d=64)),
                 r=[bkav], w=[vk])
            brv, bkrv = poolP.get()
            tm_group(s, FM + 128, 512, brv, bkrv)
            P.op("act", lambda e: e.copy(out=vret[:].rearrange("p a b -> p (a b)"), in_=brv[:]), r=[bkrv], w=[K("vret")])
            P.op("dve", lambda e: e.tensor_tensor(out=vw[:], in0=vret[:],
                                                  in1=bcast(wkt[:], 2, [128, 4, 128]), op=ALU.mult),
                 r=[K("vret"), "wkt"], w=[K("vw")])
            brg, bkrg = poolP.get()
            tm_group(s, FM + 640, 512, brg, bkrg)
            P.op("act", lambda e: e.activation(out=sg[:], in_=brg[:], func=AF.Silu), r=[bkrg], w=[K("sg")])
            P.op("pool", lambda e: e.tensor_tensor(out=sg[:], in0=sg[:], in1=gng[:], op=ALU.mult),
                 r=[K("sg"), "gng"], w=[K("sg")])
        def stage_q(t, drip=lambda: None):
            s = t % 2
            s3, sp3 = t % 3, (t - 1) % 3
            qT, mqT, rot, kpad, qspad = qT_l[s], mqT_l[s], rot_l[s], kpad_l[s], qspad_l[s]
            vret, vw, sg = vret_l[s], vw_l[s], sg_l[s]
            K = lambda nm: nm + str(s)
            vk = "vaug%d" % s3
            whichs = [(0, s3)] + ([(1, sp3)] if t > 0 else [])
            for hb2 in range(2):
                for (wi, slot) in whichs:
                    bl, bkl = poolQ.get()
                    for j in range(4):
                        h = 4 * hb2 + j
                        i, half = h // 2, h % 2
                        P.op("pe", lambda e, j=j, i=i, half=half, slot=slot, bl=bl, hb2=hb2: e.matmul(
                            bl[:, j * 128:(j + 1) * 128], lhsT=kT[slot][:, hb2, half, :],
                            rhs=qT[:, i, :], start=True, stop=True),
                            r=["kT%d" % slot, K("qT")], w=[bkl])
                    pk = "pT%d%d" % (wi, hb2)
                    P.op("act", lambda e, wi=wi, bl=bl, hb2=hb2: e.activation(
                        out=pT[:, wi, hb2 * 4:(hb2 + 1) * 4, :].rearrange("p a b -> p (a b)"), in_=bl[:],
                        func=AF.Exp, scale=0.125), r=[bkl], w=[pk])
                    P.op("pool", lambda e, wi=wi, hb2=hb2: e.tensor_tensor(
                        out=pT[:, wi, hb2 * 4:(hb2 + 1) * 4, :], in0=pT[:, wi, hb2 * 4:(hb2 + 1) * 4, :],
                        in1=bcast(msk[:, wi, :], 1, [128, 4, 128]), op=ALU.mult), r=[pk, "msk"], w=[pk])
            drip()
            for hb2 in range(2):
                bo, bko = poolQ.get()
                for j in range(4):
                    h = 4 * hb2 + j
                    if t > 0:
                        P.op("pe", lambda e, j=j, h=h, bo=bo, hb2=hb2: e.matmul(
                            bo[:, j * 65:(j + 1) * 65], lhsT=pT[:, 1, h, :], rhs=vaug[sp3][:, hb2, :],
                            start=True, stop=False), r=["pT1%d" % hb2, "vaug%d" % sp3], w=[bko])
                    P.op("pe", lambda e, j=j, h=h, bo=bo, hb2=hb2: e.matmul(
                        bo[:, j * 65:(j + 1) * 65], lhsT=pT[:, 0, h, :], rhs=vaug[s3][:, hb2, :],
                        start=(t == 0), stop=True), r=["pT0%d" % hb2, vk], w=[bko])
                bo3 = bo[:, 0:260].rearrange("p (j d) -> p j d", d=65)
                dk = "den%d" % hb2
                P.op("dve", lambda e, bo3=bo3, hb2=hb2: e.tensor_tensor(
                    out=den[:, hb2 * 4:(hb2 + 1) * 4], in0=bo3[:, :, 64], in1=esink[:, hb2 * 4:(hb2 + 1) * 4],
                    op=ALU.add), r=[bko, "esink"], w=[dk])
                P.op("dve", lambda e, hb2=hb2: e.reciprocal(out=den[:, hb2 * 4:(hb2 + 1) * 4], in_=den[:, hb2 * 4:(hb2 + 1) * 4]),
                     r=[dk], w=[dk])
                P.op("dve", lambda e, bo3=bo3, hb2=hb2: e.tensor_tensor(
                    out=br[:, 0, hb2 * 256:(hb2 + 1) * 256].rearrange("p (j d) -> p j d", d=64),
                    in0=bo3[:, :, 0:64], in1=bcast(den[:, hb2 * 4:(hb2 + 1) * 4], 2, [128, 4, 64]), op=ALU.mult),
                    r=[bko, dk], w=["br0"])
            drip()
            for mc in range(2):
                bl, bkl = poolQ.get()
                for h in range(4):
                    P.op("pe", lambda e, h=h, mc=mc, bl=bl: e.matmul(
                        bl[:, h * 128:(h + 1) * 128], lhsT=mkT[:, h, mc * 128:(mc + 1) * 128], rhs=mqT[:, h, :],
                        start=True, stop=True), r=["mkT", K("mqT")], w=[bkl])
                P.op("act", lambda e, mc=mc, bl=bl: e.activation(
                    out=pTc[:, mc, :, :].rearrange("p a b -> p (a b)"), in_=bl[:], func=AF.Exp,
                    scale=float(128 ** -0.5)), r=[bkl], w=["pTc%d" % mc])
            drip()
            for hp2 in range(2):
                bo, bko = poolQ.get()
                for j in range(2):
                    h = 2 * hp2 + j
                    for mc in range(2):
                        P.op("pe", lambda e, j=j, h=h, mc=mc, bo=bo: e.matmul(
                            bo[:, j * 129:(j + 1) * 129], lhsT=pTc[:, mc, h, :], rhs=mv_aug[:, mc, h, :],
                            start=(mc == 0), stop=(mc == 1)), r=["pTc%d" % mc, "mv_aug"], w=[bko])
                bo3 = bo[:, 0:258].rearrange("p (j d) -> p j d", d=129)
                dk = "denc%d" % hp2
                P.op("dve", lambda e, bo3=bo3, hp2=hp2: e.reciprocal(out=denc[:, hp2 * 2:(hp2 + 1) * 2], in_=bo3[:, :, 128]),
                     r=[bko], w=[dk])
                P.op("dve", lambda e, bo3=bo3, hp2=hp2: e.tensor_tensor(
                    out=br[:, 2, hp2 * 256:(hp2 + 1) * 256].rearrange("p (j d) -> p j d", d=128),
                    in0=bo3[:, :, 0:128], in1=bcast(denc[:, hp2 * 2:(hp2 + 1) * 2], 2, [128, 2, 128]), op=ALU.mult),
                    r=[bko, dk], w=["br2"])
            drip()
            pb, pbk = psbQ.get()
            for blk in range(2):
                P.op("pe", lambda e, blk=blk: e.transpose(out=pb[:, blk * 128:(blk + 1) * 128], in_=rot[:, 2 + blk, :],
                                                          identity=ident[:]), r=[K("rot"), "ident"], w=[pbk])
            P.op("act", lambda e: e.copy(out=ktok[:], in_=pb[:, 0:256]), r=[pbk], w=["ktok"])
            bi, bki = poolQ.get()
            for h in range(4):
                blk, half = h // 2, h % 2
                P.op("pe", lambda e, h=h, blk=blk, half=half: e.matmul(
                    bi[:, h * 128:(h + 1) * 128], lhsT=kpad[:, blk, half, :],
                    rhs=rot[:, blk, :], start=True, stop=True), r=[K("rot"), K("kpad")], w=[bki])
            P.op("dve", lambda e: e.tensor_tensor(out=innerTm[:], in0=bi[:].rearrange("p (a b) -> p a b", b=128),
                                                  in1=dtab[:], op=ALU.mult), r=[bki, "dtab"], w=["innerTm"])
            drip()
            bo, bko = poolQ.get()
            for h in range(4):
                blk, half = h // 2, h % 2
                P.op("pe", lambda e, h=h: e.matmul(
                    bo[:, h * 128:(h + 1) * 128], lhsT=innerTm[:, h, :], rhs=vret[:, h, :],
                    start=True, stop=(t == 0)), r=["innerTm", K("vret")], w=[bko])
                if t > 0:
                    P.op("pe", lambda e, h=h, blk=blk, half=half: e.matmul(
                        bo[:, h * 128:(h + 1) * 128], lhsT=qspad[:, blk, half, :],
                        rhs=state_bf[:, blk, :], start=False, stop=True),
                        r=[K("qspad"), "state_bf"], w=[bko])
            bkv, bkkv = poolQ.get()
            for h in range(4):
                blk, half = h // 2, h % 2
                P.op("pe", lambda e, h=h, blk=blk, half=half: e.matmul(
                    bkv[:, h * 128:(h + 1) * 128], lhsT=ktok[:, blk * 128:(blk + 1) * 128],
                    rhs=vw[:, h, :], start=True, stop=True), r=["ktok", K("vw")], w=[bkkv])
            for h in range(4):
                blk, half = h // 2, h % 2
                rows = slice(half * 64, (half + 1) * 64)
                P.op("dve", lambda e, h=h, blk=blk, rows=rows: e.scalar_tensor_tensor(
                    out=state[rows, blk, :], in0=state[rows, blk, :], scalar=cdt[rows, blk:blk + 1],
                    in1=bkv[rows, h * 128:(h + 1) * 128], op0=ALU.mult, op1=ALU.add),
                    r=["state", "cdt", bkkv], w=["state"])
            P.op("act", lambda e: e.copy(out=state_bf[:], in_=state[:]), r=["state"], w=["state_bf"])
            drip()
            for h in range(4):
                P.op("dve", lambda e, h=h: e.bn_stats(out=gst[:, h * 6:(h + 1) * 6], in_=bo[:, h * 128:(h + 1) * 128]),
                     r=[bko], w=["gst%d" % h])
                P.op("dve", lambda e, h=h: e.bn_aggr(out=gmv[:, h, :], in_=gst[:, h * 6:(h + 1) * 6]),
                     r=["gst%d" % h], w=["gmv"])
            P.op("act", lambda e: e.activation(out=grs[:], in_=gmv[:, :, 1], func=AF.Sqrt, bias=EPS_T[0][:], scale=1.0),
                 r=["gmv", "eps"], w=["grs"])
            P.op("dve", lambda e: e.reciprocal(out=grs[:], in_=grs[:]), r=["grs"], w=["grs"])
            for h in range(4):
                P.op("dve", lambda e, h=h: e.tensor_scalar(
                    out=xn[:, h * 128:(h + 1) * 128], in0=bo[:, h * 128:(h + 1) * 128], scalar1=gmv[:, h, 0:1],
                    scalar2=grs[:, h:h + 1], op0=ALU.subtract, op1=ALU.mult), r=[bko, "gmv", "grs"], w=["xn"])
            P.op("pool", lambda e: e.tensor_tensor(out=br[:, 1, :], in0=xn[:], in1=sg[:], op=ALU.mult),
                 r=["xn", K("sg")], w=["br1"])
            drip()
            pba, pbka = psbQ.get()
            for b in range(2):
                for kc in range(4):
                    P.op("pe", lambda e, b=b, kc=kc: e.transpose(
                        out=pba[:, (b * 4 + kc) * 128:(b * 4 + kc + 1) * 128], in_=br[:, b, kc * 128:(kc + 1) * 128],
                        identity=ident[:]), r=["br%d" % b, "ident"], w=[pbka])
            bt = "brT%d" % s
            P.op("act", lambda e: e.copy(out=brT[s][:, 0:8, :].rearrange("p a b -> p (a b)"), in_=pba[:]),
                 r=[pbka], w=[bt])
            pbc, pbkc = psbQ.get()
            for kc in range(4):
                P.op("pe", lambda e, kc=kc: e.transpose(out=pbc[:, kc * 128:(kc + 1) * 128],
                                                        in_=br[:, 2, kc * 128:(kc + 1) * 128], identity=ident[:]),
                     r=["br2", "ident"], w=[pbkc])
            P.op("dve", lambda e: e.tensor_copy(out=brT[s][:, 8:12, :].rearrange("p a b -> p (a b)"), in_=pbc[:, 0:512]),
                 r=[pbkc], w=[bt])
            P.op("sp", lambda e: e.dma_start(out=brT_scr[t, :, :], in_=brT[s][:].rearrange("p a b -> p (a b)")),
                 r=[bt], dma="a1s%d" % s)

        loads(0)
        every = max(1, n_tiles // 16)
        for t in range(n_tiles):
            if bt is not None and t % every == 0 and btc["ch"] < 16:
                bt_chunk(P, btc["ch"], uvt, *bt)
                btc["ch"] += 1
            if t == 0:
                stage_p(0)
            todo = []
            if t + 1 < n_tiles:
                real = P.op
                P.op = lambda *a, **k: todo.append((a, k))
                stage_p(t + 1)
                del P.op
            per = -(-len(todo) // 7) if todo else 0

            def drip():
                for (a, k) in todo[:per]:
                    P.op(*a, **k)
                del todo[:per]

            stage_q(t, drip)
            for (a, k) in todo:
                P.op(*a, **k)
        while bt is not None and btc["ch"] < 16:
            bt_chunk(P, btc["ch"], uvt, *bt)
            btc["ch"] += 1
        P.emit()


def phase_a2(nc, sync, G, n_tiles, x_d, xT_d, wg_d, bg_d, wbr_d, wout_d, ln1g_d, ln1b_d, brT_scr, h_scr):
    with ExitStack() as es:
        def sb(name, shape, dt):
            return es.enter_context(nc.sbuf_tensor("a2_" + name, shape, dt))
        ident = G["ident"]
        wg = sb("wg", [128, 8, NCOL_G], BF16)
        bg = sb("bg", [1, NCOL_G], BF16)
        ones = sb("ones", [1, 128], BF16)
        wbr = sb("wbr", [128, 12, D], BF16)
        wout = sb("wout", [128, 8, D], BF16)
        g1 = sb("g1", [128, D], F32)
        b1 = sb("b1", [128, D], F32)
        xT = [sb("xT%d" % i, [128, 8, 128], BF16) for i in range(2)]
        xt = [sb("x%d" % i, [128, D], F32) for i in range(2)]
        brT = [sb("brT%d" % i, [128, 12, 128], BF16) for i in range(2)]
        gsb = [sb("gsb%d" % i, [128, 512], F32) for i in range(2)]
        acc = sb("acc", [128, D], F32)
        tmp = [sb("tmp%d" % i, [128, 512], F32) for i in range(2)]
        mbf = sb("mbf", [128, D], BF16)
        mT = sb("mT", [128, 8, 128], BF16)
        z = [sb("z%d" % i, [128, D], F32) for i in range(2)]
        st6 = sb("st6", [128, 12], F32)
        mv2 = sb("mv2", [128, 2], F32)
        rstd = sb("rstd", [128, 1], F32)
        nmr = sb("nmr", [128, 1], F32)

        P = Prog(nc, sync)
        for c in range(8):
            P.op("pool", lambda e, c=c: e.dma_start(out=wg[:, c, :], in_=wg_d[c * 128:(c + 1) * 128, :],
                                                    max_dma_last_dim=4096), w=["wg"], dma="a2w")
        P.op("pool", lambda e: e.dma_start(out=bg[:], in_=bg_d[:, :], max_dma_last_dim=4096), w=["bg"], dma="a2w")
        for c in range(12):
            P.op("pool", lambda e, c=c: e.dma_start(out=wbr[:, c, :], in_=wbr_d[c * 128:(c + 1) * 128, :]),
                 w=["wbr"], dma="a2w")
        for c in range(8):
            P.op("pool", lambda e, c=c: e.dma_start(out=wout[:, c, :], in_=wout_d[c * 128:(c + 1) * 128, :]),
                 w=["wout"], dma="a2w")
        P.op("sp", lambda e: e.dma_start(out=g1[:], in_=ln1g_d[:, :]), w=["g1"], dma="a2p")
        P.op("sp", lambda e: e.dma_start(out=b1[:], in_=ln1b_d[:, :]), w=["b1"], dma="a2p")
        P.op("dve", lambda e: e.memset(ones[:], 1.0), w=["ones"])
        cnt = dict(g=0)

        def loads(t):
            s = t % 2
            P.op("pool", lambda e: e.dma_start(
                out=xT[s][:], in_=xT_d.rearrange("(c p) t -> p c t", p=128)[:, :, t * 128:(t + 1) * 128]),
                w=["xT%d" % s], dma="a2x%d" % s)
            P.op("sp", lambda e: e.dma_start(out=xt[s][:], in_=x_d[t * 128:(t + 1) * 128, :]),
                 w=["x%d" % s], dma="a2l%d" % s)
            P.op("sp", lambda e: e.dma_start(out=brT[s][:].rearrange("p a b -> p (a b)"), in_=brT_scr[t, :, :]),
                 w=["brT%d" % s], dma="a2l%d" % s)

        def do_tile(t):
            s = t % 2
            if t + 1 < n_tiles:
                loads(t + 1)
            xk, xtk, btk = "xT%d" % s, "x%d" % s, "brT%d" % s
            for b in range(3):
                for half in range(2):
                    col0 = b * 1024 + half * 512
                    gi = cnt["g"] % 2
                    cnt["g"] += 1
                    bgk, bkg = G["psA"].get()
                    for c in range(8):
                        P.op("pe", lambda e, c=c, col0=col0, bgk=bgk: e.matmul(
                            bgk[:], lhsT=xT[s][:, c, :], rhs=wg[:, c, col0:col0 + 512], start=(c == 0), stop=False),
                            r=["wg", xk], w=[bkg])
                    P.op("pe", lambda e, col0=col0, bgk=bgk: e.matmul(
                        bgk[:], lhsT=ones[0:1, :], rhs=bg[0:1, col0:col0 + 512], start=False, stop=True),
                        r=["bg", "ones"], w=[bkg])
                    P.op("act", lambda e, gi=gi, bgk=bgk: e.activation(out=gsb[gi][:], in_=bgk[:], func=AF.Sigmoid),
                         r=[bkg], w=["gsb%d" % gi])
                    by, bky = G["psA"].get()
                    for kc in range(4):
                        P.op("pe", lambda e, kc=kc, b=b, half=half, by=by: e.matmul(
                            by[:], lhsT=brT[s][:, b * 4 + kc, :], rhs=wbr[:, b * 4 + kc, half * 512:(half + 1) * 512],
                            start=(kc == 0), stop=(kc == 3)), r=["wbr", btk], w=[bky])
                    ak = "acc%d" % half
                    if b == 0:
                        P.op("dve", lambda e, gi=gi, half=half, by=by: e.tensor_tensor(
                            out=acc[:, half * 512:(half + 1) * 512], in0=by[:], in1=gsb[gi][:], op=ALU.mult),
                            r=[bky, "gsb%d" % gi], w=[ak])
                    else:
                        P.op("dve", lambda e, gi=gi, by=by: e.tensor_tensor(
                            out=tmp[gi][:], in0=by[:], in1=gsb[gi][:], op=ALU.mult),
                            r=[bky, "gsb%d" % gi], w=["tmp%d" % gi])
                        if b == 1:
                            P.op("pool", lambda e, gi=gi, half=half: e.tensor_tensor(
                                out=acc[:, half * 512:(half + 1) * 512], in0=acc[:, half * 512:(half + 1) * 512],
                                in1=tmp[gi][:], op=ALU.add), r=[ak, "tmp%d" % gi], w=[ak])
                        else:
                            P.op("pool", lambda e, gi=gi, half=half: e.tensor_tensor(
                                out=mbf[:, half * 512:(half + 1) * 512], in0=acc[:, half * 512:(half + 1) * 512],
                                in1=tmp[gi][:], op=ALU.add), r=[ak, "tmp%d" % gi], w=["mbf"])
            pb, pbk = G["psb"].get()
            for c in range(8):
                P.op("pe", lambda e, c=c: e.transpose(out=pb[:, c * 128:(c + 1) * 128], in_=mbf[:, c * 128:(c + 1) * 128],
                                                      identity=ident[:]), r=["mbf", "ident"], w=[pbk])
            P.op("act", lambda e: e.copy(out=mT[:].rearrange("p a b -> p (a b)"), in_=pb[:]), r=[pbk], w=["mT"])
            zk = "z%d" % s
            for half in range(2):
                bz, bkz = G["psA"].get()
                for c in range(8):
                    P.op("pe", lambda e, c=c, half=half, bz=bz: e.matmul(
                        bz[:], lhsT=mT[:, c, :], rhs=wout[:, c, half * 512:(half + 1) * 512],
                        start=(c == 0), stop=(c == 7)), r=["mT", "wout"], w=[bkz])
                P.op("dve", lambda e, half=half, bz=bz: e.scalar_tensor_tensor(
                    out=z[s][:, half * 512:(half + 1) * 512], in0=xt[s][:, half * 512:(half + 1) * 512],
                    scalar=ALPHA, in1=bz[:], op0=ALU.mult, op1=ALU.add), r=[xtk, bkz], w=[zk])
            layer_norm_tail(P, z[s], zk, g1, "g1", b1, "b1", st6, mv2, rstd, nmr, "a2", gb_eng="pool")
            P.op("sp", lambda e: e.dma_start(out=h_scr[t * 128:(t + 1) * 128, :], in_=z[s][:]), r=[zk],
                 dma="a2s%d" % s)

        loads(0)
        for t in range(n_tiles):
            do_tile(t)
        P.emit()


IN_SPECS = [
    ("x", lambda T: [T, D], F32), ("xT", lambda T: [D, T], F32), ("mem", lambda T: [256, D], F32),
    ("memg", lambda T: [128, D], F32), ("memb", lambda T: [128, D], F32), ("wkv", lambda T: [D, D], F32),
    ("w1", lambda T: [D, NCOL_A1], F32), ("b1", lambda T: [1, NCOL_A1], F32),
    ("cc", lambda T: [128, T], F32), ("ss", lambda T: [128, T], F32),
    ("dtab", lambda T: [128, 512], F32), ("wqt", lambda T: [128, 256], F32), ("wkt", lambda T: [128, 4], F32),
    ("cdt", lambda T: [128, 2], F32), ("mask", lambda T: [128, 256], F32), ("gng", lambda T: [128, 512], F32),
    ("sink", lambda T: [128, 8], F32),
    ("wg", lambda T: [D, NCOL_G], F32), ("bg", lambda T: [1, NCOL_G], F32), ("wbr", lambda T: [1536, D], F32),
    ("wout", lambda T: [D, D], F32), ("ln1g", lambda T: [128, D], F32), ("ln1b", lambda T: [128, D], F32),
    ("wpqT", lambda T: [2048, D], F32), ("skT", lambda T: [2048, 128], F32),
    ("peer_u", lambda T: [N_EXP, D], F32), ("peer_v", lambda T: [N_EXP, D], F32),
    ("ln2g", lambda T: [128, D], F32), ("ln2b", lambda T: [128, D], F32),
    ("identf", lambda T: [128, 128], F32), ("iota16", lambda T: [128, 16], F32),
]


def build_full(n_tiles, debug_h=False, stop_after=None):
    nc = bass.Bass("TRN2", target_bir_lowering=False)
    T = n_tiles * 128
    d = {}
    for name, shp, dtp in IN_SPECS:
        d[name] = nc.dram_tensor(name, shp(T), dtp, kind="ExternalInput").ap()
    y_d = nc.dram_tensor("y", [T, D], F32, kind="ExternalOutput").ap()
    h_scr = nc.dram_tensor("h_scr", [T, D], F32, kind="ExternalOutput" if debug_h else "Internal").ap()
    brT_scr = nc.dram_tensor("brT_scr", [n_tiles, 128, 1536], BF16,
                             kind="ExternalOutput" if debug_h else "Internal").ap()
    with ExitStack() as es:
        sync = Sync(nc, es)
        G = alloc_globals(nc, es, sync)
        G["psA"] = PsumPool(G["psf"].t + G["pv"], "psA")
        phase_const(nc, sync, G, d["identf"], d["iota16"])
        uv_scr = nc.dram_tensor("uv_scr", [N_EXP, 2 * D], BF16, kind="Internal").ap()
        with ExitStack() as es1:
            mkT = es1.enter_context(nc.sbuf_tensor("sb_mkT", [128, 4, 256], BF16))
            mv_aug = es1.enter_context(nc.sbuf_tensor("sb_mvaug", [128, 2, 4, 129], BF16))
            phase_a0(nc, sync, G, mkT, mv_aug, d["mem"], d["memg"], d["memb"], d["wkv"])
            if stop_after == "a0":
                return nc
            phase_a1(nc, sync, G, n_tiles, mkT, mv_aug, d["xT"], d["w1"], d["b1"], d["cc"], d["ss"], d["dtab"],
                     d["wqt"], d["wkt"], d["cdt"], d["mask"], d["gng"], d["sink"], brT_scr,
                     bt=(d["peer_u"], d["peer_v"], uv_scr))
        if stop_after == "a1":
            return nc
        phase_a2(nc, sync, G, n_tiles, d["x"], d["xT"], d["wg"], d["bg"], d["wbr"], d["wout"], d["ln1g"],
                 d["ln1b"], brT_scr, h_scr)
        if stop_after == "a2":
            return nc
        with ExitStack() as es2:
            weff = es2.enter_context(nc.sbuf_tensor("sb_weff", [128, 8, 2048], BF16))
            phase_b0(nc, sync, G, weff, d["wpqT"], d["skT"])
            phase_b(nc, sync, G, n_tiles, weff, h_scr, uv_scr, d["ln2g"], d["ln2b"], y_d)
    return nc


def _w_in_cols():
    fm = list(range(0, 512))
    fm += list(range(512, 576)) * 2 + list(range(576, 640)) * 2
    rq0, rk0 = 768, 1024
    fm += list(range(rq0, rq0 + 256)) + list(range(rk0, rk0 + 256))
    sw = []
    for base in (rq0, rk0):
        for h in range(4):
            hb = base + 64 * h
            sw += list(range(hb + 32, hb + 64)) + list(range(hb, hb + 32))
    fm += sw
    fm += list(range(2304, 2816))
    tm = list(range(640, 768)) + list(range(1280, 1792)) + list(range(1792, 2304))
    return np.array(fm + tm), np.arange(2816, 5888)


def _const_tables(T):
    half = 32
    theta = (1.0 / np.power(np.float32(10000.0), np.linspace(0.0, 1.0, half, dtype=np.float32))).astype(np.float32)
    pos = np.arange(T, dtype=np.float32)
    ang = (pos[:, None] * theta[None, :]).astype(np.float32)
    cos, sin = np.cos(ang).astype(np.float32), np.sin(ang).astype(np.float32)
    p = np.arange(128)
    cc = np.ascontiguousarray(cos[:, p % 32].T)
    sgn = np.where((p % 64) < 32, -1.0, 1.0).astype(np.float32)
    ss = np.ascontiguousarray((sin[:, p % 32] * sgn[None, :]).T)
    lg = np.log(1.0 - 2.0 ** (-5.0 - np.arange(4, dtype=np.float64)))
    i = np.arange(128, dtype=np.float64)
    diff = i[None, :] - i[:, None]
    dt = np.zeros((128, 4, 128), np.float64)
    for h in range(4):
        dt[:, h, :] = np.where(diff >= 0, np.exp(lg[h] * np.maximum(diff, 0.0)), 0.0) * 0.125
    wq = np.zeros((128, 2, 128), np.float64)
    cd = np.zeros((128, 2), np.float64)
    for blk in range(2):
        for hf in range(2):
            h = blk * 2 + hf
            wq[hf * 64:(hf + 1) * 64, blk, :] = np.exp(lg[h] * (i + 1.0))[None, :]
            cd[hf * 64:(hf + 1) * 64, blk] = np.exp(lg[h] * 128.0)
    wk = np.zeros((128, 4), np.float64)
    for h in range(4):
        wk[:, h] = np.exp(lg[h] * (127.0 - i)) * 0.125
    k = np.arange(128)[:, None]
    q = np.arange(128)[None, :]
    mask = np.concatenate([(k <= q), (k > q)], axis=1).astype(np.float32)
    f = lambda a: np.ascontiguousarray(a.astype(np.float32))
    return dict(cc=cc, ss=ss, dtab=f(dt.reshape(128, 512)), wqt=f(wq.reshape(128, 256)), wkt=f(wk), cdt=f(cd),
                mask=mask, identf=np.eye(128, dtype=np.float32),
                iota16=f(np.broadcast_to(np.arange(16.0), (128, 16))))


def _rep(v, n=128):
    return np.ascontiguousarray(np.broadcast_to(np.asarray(v, np.float32)[None, :], (n, v.shape[-1])))


def host_prep(inputs, T):
    g = lambda n: np.asarray(inputs[n], np.float32)[0]
    c1, cg = _w_in_cols()
    w_in, b_in = g("w_in"), g("b_in")
    sh = dict(
        memg=_rep(g("mem_ln_g")), memb=_rep(g("mem_ln_b")), wkv=g("w_mem_kv"),
        w1=np.ascontiguousarray(w_in[:, c1]), b1=np.ascontiguousarray(b_in[c1][None, :]),
        gng=_rep(g("ret_gn_g")), sink=_rep(g("attn_sinks")),
        wg=np.ascontiguousarray(w_in[:, cg]), bg=np.ascontiguousarray(b_in[cg][None, :]),
        wbr=np.ascontiguousarray(np.concatenate([g("w_branch_attn"), g("w_branch_ret"), g("w_branch_mem")], axis=0)),
        wout=g("w_out"), ln1g=_rep(g("ln1_g")), ln1b=_rep(g("ln1_b")),
        wpqT=np.ascontiguousarray(g("w_peer_q").T),
        skT=np.ascontiguousarray(g("peer_sub_keys").transpose(0, 1, 3, 2).reshape(2048, 128)),
        peer_u=g("peer_u"), peer_v=g("peer_v"), ln2g=_rep(g("ln2_g")), ln2b=_rep(g("ln2_b")),
    )
    sh.update(_const_tables(T))
    return sh


def kernel(**inputs):
    x = np.asarray(inputs["x"], np.float32)
    mem = np.asarray(inputs["mem"], np.float32)
    B, S, _ = x.shape
    n_tiles = S // 128
    sh = host_prep(inputs, S)
    in_maps = []
    for b in range(B):
        m = dict(sh)
        m["x"] = np.ascontiguousarray(x[b])
        m["xT"] = np.ascontiguousarray(x[b].T)
        m["mem"] = np.ascontiguousarray(mem[b])
        in_maps.append(m)
    nc = build_full(n_tiles)
    res = run_bass_kernel_spmd(nc, in_maps, core_ids=list(range(B)))
    return np.stack([r["y"] for r in res.results], axis=0).astype(np.float32)
```
